# Optimizing a Trainium2 kernel written in Bass

```python
import math, functools
import jax, jax.numpy as jnp
from jax import lax
import numpy as np

D_MODEL = 2048
BATCH = 4
SEQ = 8192
DEPTH = 2
DEC_BATCH = 32
DEC_SEQ = 16
PAST_LEN = 2048

CHUNK = 64
Q_BLOCK = 128
EPS = 1e-6
ROPE_BASE = 10000.0
D_FF = 5632
SSD_INNER = D_MODEL
SSD_HEAD_DIM = 64
SSD_HEADS = SSD_INNER // SSD_HEAD_DIM
SSD_GROUPS = 4
SSD_HPG = SSD_HEADS // SSD_GROUPS
SSD_STATE = 128
CONV_K = 4
CONV_DIM = SSD_INNER + 2 * SSD_GROUPS * SSD_STATE
RET_HEAD_DIM = 256
RET_INNER = D_MODEL
RET_HEADS = RET_INNER // RET_HEAD_DIM
AB_IN = SSD_INNER + CONV_DIM + SSD_HEADS + 4 * RET_INNER
AB_SPLITS = (SSD_INNER, SSD_INNER + CONV_DIM, SSD_INNER + CONV_DIM + SSD_HEADS,
             SSD_INNER + CONV_DIM + SSD_HEADS + RET_INNER,
             SSD_INNER + CONV_DIM + SSD_HEADS + 2 * RET_INNER,
             SSD_INNER + CONV_DIM + SSD_HEADS + 3 * RET_INNER)
AB_OUT = SSD_INNER + RET_INNER
MLA_HEADS = D_MODEL // 128
Q_LORA = 512
KV_LORA = 512
QK_NOPE = 128
QK_ROPE = 64
V_HEAD = 128
C_IN = Q_LORA + KV_LORA + QK_ROPE
MLA_SCALE = (QK_NOPE + QK_ROPE) ** -0.5
MEM_TOKENS = 256
MEM_HEADS = 4
MEM_HEAD_DIM = 128
MEM_INNER = MEM_HEADS * MEM_HEAD_DIM
N_EVEN = (DEPTH + 1) // 2
N_ODD = DEPTH // 2

kernel_name = 'hybrid_ssd_retention_mla_streaming_step'


def _rms(x):
    xf = x.astype(jnp.float32)
    return xf * lax.rsqrt(jnp.mean(xf * xf, axis=-1, keepdims=True) + EPS)


def rmsnorm(x, g):
    return (_rms(x) * g).astype(x.dtype)


def rope(x, pos):
    half = x.shape[-1] // 2
    inv = ROPE_BASE ** (-jnp.arange(half, dtype=jnp.float32) / half)
    ang = pos.astype(jnp.float32)[:, None] * inv[None, :]
    cos, sin = jnp.cos(ang)[:, None, :], jnp.sin(ang)[:, None, :]
    x1, x2 = x[..., :half], x[..., half:]
    return jnp.concatenate([x1 * cos - x2 * sin, x2 * cos + x1 * sin], axis=-1).astype(x.dtype)


def swiglu(xn, w1, w2):
    a, b = jnp.split(xn @ w1, 2, axis=-1)
    return (jax.nn.silu(a) * b) @ w2


def causal_conv(u, state, w, b):
    T = u.shape[1]
    up = jnp.concatenate([state.astype(u.dtype), u], axis=1)
    y = b + sum(up[:, j:j + T] * w[j] for j in range(CONV_K))
    return jax.nn.silu(y), up[:, T:]


def chunked_decay_scan(q, k, v, log_a, h0):
    f32 = jnp.float32
    Bsz, T, G, N = q.shape
    Hg, P = v.shape[3], v.shape[4]
    L = CHUNK if T % CHUNK == 0 else T
    nC = T // L
    q = q.astype(f32).reshape(Bsz, nC, L, G, N)
    k = k.astype(f32).reshape(Bsz, nC, L, G, N)
    v = v.astype(f32).reshape(Bsz, nC, L, G, Hg, P)
    cum = jnp.cumsum(log_a.astype(f32).reshape(Bsz, nC, L, G, Hg), axis=2)
    cum_t = jnp.moveaxis(cum, 2, -1)
    causal = jnp.tril(jnp.ones((L, L), dtype=bool))
    seg = cum_t[..., :, None] - cum_t[..., None, :]
    decay = jnp.where(causal, jnp.exp(jnp.where(causal, seg, 0.0)), 0.0)
    qk = jnp.einsum('bclgn,bcsgn->bcgls', q, k)
    y_intra = jnp.einsum('bcghls,bcsghp->bclghp', qk[:, :, :, None] * decay, v)

    def step(h, blk):
        q_c, k_c, v_c, cum_c = blk
        y_c = jnp.einsum('blgn,bghnp->blghp', q_c, h) * jnp.exp(cum_c)[..., None]
        last = cum_c[:, -1]
        w = jnp.exp(last[:, None] - cum_c)
        h = jnp.exp(last)[..., None, None] * h + jnp.einsum('blgn,blgh,blghp->bghnp', k_c, w, v_c)
        return h, y_c

    xs = tuple(jnp.moveaxis(a, 1, 0) for a in (q, k, v, cum))
    h_T, y_inter = lax.scan(step, h0.astype(f32), xs)
    y = y_intra + jnp.moveaxis(y_inter, 0, 1)
    return y.reshape(Bsz, T, G, Hg, P), h_T


def ssd_ret_mixer(xn, pos, conv_state, ssd_state, ret_state, w_in, conv_w, conv_b,
                  dt_bias, a_log, d_skip, ssd_norm, w_out):
    f32 = jnp.float32
    Bsz, T, _ = xn.shape
    z, xbc, dt_raw, q, k, v, gate = jnp.split(xn @ w_in, AB_SPLITS, axis=-1)
    xbc, new_conv = causal_conv(xbc, conv_state, conv_w, conv_b)
    xs, b_ssm, c_ssm = jnp.split(xbc, [SSD_INNER, SSD_INNER + SSD_GROUPS * SSD_STATE], axis=-1)
    xs = xs.reshape(Bsz, T, SSD_GROUPS, SSD_HPG, SSD_HEAD_DIM)
    b_ssm = b_ssm.reshape(Bsz, T, SSD_GROUPS, SSD_STATE)
    c_ssm = c_ssm.reshape(Bsz, T, SSD_GROUPS, SSD_STATE)
    dt = jax.nn.softplus((dt_raw + dt_bias).astype(f32)).reshape(Bsz, T, SSD_GROUPS, SSD_HPG)
    a = -jnp.exp(a_log.astype(f32)).reshape(SSD_GROUPS, SSD_HPG)
    h0 = ssd_state.reshape(Bsz, SSD_GROUPS, SSD_HPG, SSD_STATE, SSD_HEAD_DIM)
    y, h_T = chunked_decay_scan(c_ssm, b_ssm, xs * dt[..., None], dt * a, h0)
    y = y + d_skip.reshape(SSD_GROUPS, SSD_HPG, 1) * xs
    y = y.reshape(Bsz, T, SSD_INNER) * jax.nn.silu(z)
    y = rmsnorm(y.reshape(Bsz, T, SSD_GROUPS, SSD_INNER // SSD_GROUPS),
                ssd_norm.reshape(SSD_GROUPS, SSD_INNER // SSD_GROUPS)).reshape(Bsz, T, SSD_INNER)
    qr = rope(q.reshape(Bsz, T, RET_HEADS, RET_HEAD_DIM), pos)
    kr = rope(k.reshape(Bsz, T, RET_HEADS, RET_HEAD_DIM), pos) * RET_HEAD_DIM ** -0.5
    log_gamma = jnp.log1p(-jnp.exp2(-5.0 - jnp.arange(RET_HEADS, dtype=f32)))
    log_a = jnp.broadcast_to(log_gamma[:, None], (Bsz, T, RET_HEADS, 1))
    o, r_T = chunked_decay_scan(qr, kr, v.reshape(Bsz, T, RET_HEADS, 1, RET_HEAD_DIM), log_a,
                                ret_state[:, :, None])
    o = _rms(o[:, :, :, 0]).reshape(Bsz, T, RET_INNER) * jax.nn.silu(gate)
    out = jnp.concatenate([y.astype(f32), o.astype(f32)], axis=-1) @ w_out
    new_ssd = h_T.reshape(Bsz, SSD_HEADS, SSD_STATE, SSD_HEAD_DIM)
    return out.astype(xn.dtype), (new_conv, new_ssd, r_T[:, :, 0])


def mla_attend(q_nope, q_rope, q_pos, ckv, kpe, k_pos, w_uk, w_uv):
    q_lat = jnp.einsum('bqhd,chd->bqhc', q_nope, w_uk)
    s = (jnp.einsum('bqhc,bkc->bhqk', q_lat, ckv)
         + jnp.einsum('bqhr,bkr->bhqk', q_rope, kpe)).astype(jnp.float32) * MLA_SCALE
    visible = (k_pos[None, :] // CHUNK) <= (q_pos[:, None] // CHUNK)
    p = jax.nn.softmax(jnp.where(visible, s, -jnp.inf), axis=-1).astype(ckv.dtype)
    o_lat = jnp.einsum('bhqk,bkc->bqhc', p, ckv)
    return jnp.einsum('bqhc,chd->bqhd', o_lat, w_uv)


def mla_mixer(xn, pos, past_ckv, past_kpe, w_in, q_norm, kv_norm, w_uq, w_uk, w_uv, w_out):
    Bsz, T, _ = xn.shape
    cq, ckv, kpe = jnp.split(xn @ w_in, [Q_LORA, Q_LORA + KV_LORA], axis=-1)
    q = (rmsnorm(cq, q_norm) @ w_uq).reshape(Bsz, T, MLA_HEADS, QK_NOPE + QK_ROPE)
    q_nope, q_rope = q[..., :QK_NOPE], rope(q[..., QK_NOPE:], pos)
    ckv = rmsnorm(ckv, kv_norm)
    kpe = rope(kpe[:, :, None, :], pos)[:, :, 0]
    if past_ckv is None:
        keys_ckv, keys_kpe, k_pos = ckv, kpe, pos
    else:
        keys_ckv = jnp.concatenate([past_ckv.astype(ckv.dtype), ckv], axis=1)
        keys_kpe = jnp.concatenate([past_kpe.astype(kpe.dtype), kpe], axis=1)
        k_pos = jnp.concatenate([jnp.arange(past_ckv.shape[1]), pos])
    attend = functools.partial(mla_attend, ckv=keys_ckv, kpe=keys_kpe, k_pos=k_pos, w_uk=w_uk, w_uv=w_uv)
    if T % Q_BLOCK == 0:
        nb = T // Q_BLOCK
        def blocks(a):
            return jnp.moveaxis(a.reshape(Bsz, nb, Q_BLOCK, *a.shape[2:]), 1, 0)
        o = lax.map(lambda blk: attend(blk[0], blk[1], blk[2]),
                    (blocks(q_nope), blocks(q_rope), pos.reshape(nb, Q_BLOCK)))
        o = jnp.moveaxis(o, 0, 1).reshape(Bsz, T, MLA_HEADS, V_HEAD)
    else:
        o = attend(q_nope, q_rope, pos)
    out = o.reshape(Bsz, T, MLA_HEADS * V_HEAD) @ w_out
    return out.astype(xn.dtype), (ckv, kpe)


def mem_kv(mem, g, w_mkv):
    Bsz, M, _ = mem.shape
    k, v = jnp.split(rmsnorm(mem, g) @ w_mkv, 2, axis=-1)
    return (k.reshape(Bsz, M, MEM_HEADS, MEM_HEAD_DIM), v.reshape(Bsz, M, MEM_HEADS, MEM_HEAD_DIM))


def mem_attend(xn, mem_k, mem_v, w_mq, w_mo):
    Bsz, T, _ = xn.shape
    q = (xn @ w_mq).reshape(Bsz, T, MEM_HEADS, MEM_HEAD_DIM)
    s = jnp.einsum('bthd,bmhd->bhtm', q, mem_k.astype(q.dtype)).astype(jnp.float32) * MEM_HEAD_DIM ** -0.5
    p = jax.nn.softmax(s, axis=-1).astype(q.dtype)
    o = jnp.einsum('bhtm,bmhd->bthd', p, mem_v.astype(q.dtype)).reshape(Bsz, T, MEM_INNER)
    return (o @ w_mo).astype(xn.dtype)


def trunk_layer(x, mix_fn, norms, ffn_w1, ffn_w2, mem_k, mem_v, w_mq, w_mo):
    x = x + 0.5 * swiglu(rmsnorm(x, norms[0]), ffn_w1[0], ffn_w2[0])
    mixed, new_state = mix_fn(rmsnorm(x, norms[1]))
    x = x + mixed
    x = x + mem_attend(rmsnorm(x, norms[2]), mem_k, mem_v, w_mq, w_mo)
    x = x + 0.5 * swiglu(rmsnorm(x, norms[3]), ffn_w1[1], ffn_w2[1])
    return x, new_state


def setup_inputs(seed: int = 0) -> dict:
    key = jax.random.key(seed)
    ks = iter(jax.random.split(key, 48))

    def nrm(shape, scale):
        return jax.random.normal(next(ks), shape, jnp.float32) * scale

    def gain(shape):
        return 1.0 + nrm(shape, 0.01)

    dt_init = jnp.exp(jax.random.uniform(next(ks), (N_EVEN, SSD_HEADS), jnp.float32,
                                         math.log(1e-3), math.log(1e-1)))
    return {
        'x_prompt': nrm((BATCH, SEQ, D_MODEL), 1.0),
        'x_sample': nrm((DEC_BATCH, DEC_SEQ, D_MODEL), 1.0),
        'mem_prompt': nrm((BATCH, MEM_TOKENS, D_MODEL), 1.0),
        'state_conv': nrm((N_EVEN, DEC_BATCH, CONV_K - 1, CONV_DIM), 1.0),
        'state_ssd': nrm((N_EVEN, DEC_BATCH, SSD_HEADS, SSD_STATE, SSD_HEAD_DIM), 0.5),
        'state_ret': nrm((N_EVEN, DEC_BATCH, RET_HEADS, RET_HEAD_DIM, RET_HEAD_DIM), 0.3),
        'cache_ckv': nrm((N_ODD, DEC_BATCH, PAST_LEN, KV_LORA), 1.0),
        'cache_kpe': nrm((N_ODD, DEC_BATCH, PAST_LEN, QK_ROPE), 1.0),
        'cache_mem_k': nrm((DEPTH, DEC_BATCH, MEM_TOKENS, MEM_HEADS, MEM_HEAD_DIM), 1.0),
        'cache_mem_v': nrm((DEPTH, DEC_BATCH, MEM_TOKENS, MEM_HEADS, MEM_HEAD_DIM), 1.0),
        'norms': gain((DEPTH, 4, D_MODEL)),
        'ffn_w1': nrm((DEPTH, 2, D_MODEL, 2 * D_FF), D_MODEL ** -0.5),
        'ffn_w2': nrm((DEPTH, 2, D_FF, D_MODEL), D_FF ** -0.5),
        'mem_norm': gain((DEPTH, D_MODEL)),
        'w_mq': nrm((DEPTH, D_MODEL, MEM_INNER), D_MODEL ** -0.5),
        'w_mkv': nrm((DEPTH, D_MODEL, 2 * MEM_INNER), D_MODEL ** -0.5),
        'w_mo': nrm((DEPTH, MEM_INNER, D_MODEL), MEM_INNER ** -0.5),
        'ab_w_in': nrm((N_EVEN, D_MODEL, AB_IN), D_MODEL ** -0.5),
        'ab_conv_w': nrm((N_EVEN, CONV_K, CONV_DIM), CONV_K ** -0.5),
        'ab_conv_b': nrm((N_EVEN, CONV_DIM), 0.02),
        'ab_dt_bias': dt_init + jnp.log(-jnp.expm1(-dt_init)),
        'ab_a_log': jnp.log(jax.random.uniform(next(ks), (N_EVEN, SSD_HEADS), jnp.float32, 1.0, 16.0)),
        'ab_d_skip': gain((N_EVEN, SSD_HEADS)),
        'ab_ssd_norm': gain((N_EVEN, SSD_INNER)),
        'ab_w_out': nrm((N_EVEN, AB_OUT, D_MODEL), AB_OUT ** -0.5),
        'c_w_in': nrm((N_ODD, D_MODEL, C_IN), D_MODEL ** -0.5),
        'c_q_norm': gain((N_ODD, Q_LORA)),
        'c_kv_norm': gain((N_ODD, KV_LORA)),
        'c_w_uq': nrm((N_ODD, Q_LORA, MLA_HEADS * (QK_NOPE + QK_ROPE)), Q_LORA ** -0.5),
        'c_w_uk': nrm((N_ODD, KV_LORA, MLA_HEADS, QK_NOPE), KV_LORA ** -0.5),
        'c_w_uv': nrm((N_ODD, KV_LORA, MLA_HEADS, V_HEAD), KV_LORA ** -0.5),
        'c_w_out': nrm((N_ODD, MLA_HEADS * V_HEAD, D_MODEL), (MLA_HEADS * V_HEAD) ** -0.5),
        'final_norm': gain((D_MODEL,)),
    }


def reference(x_prompt, x_sample, mem_prompt, state_conv, state_ssd, state_ret, cache_ckv, cache_kpe,
              cache_mem_k, cache_mem_v, norms, ffn_w1, ffn_w2, mem_norm, w_mq, w_mkv, w_mo,
              ab_w_in, ab_conv_w, ab_conv_b, ab_dt_bias, ab_a_log, ab_d_skip, ab_ssd_norm, ab_w_out,
              c_w_in, c_q_norm, c_kv_norm, c_w_uq, c_w_uk, c_w_uv, c_w_out, final_norm):
    f32 = jnp.float32
    bp, tp = x_prompt.shape[0], x_prompt.shape[1]
    ts = x_sample.shape[1]
    past_len = cache_ckv.shape[2]
    pos_p = jnp.arange(tp)
    pos_s = past_len + jnp.arange(ts)
    xp, xs = x_prompt, x_sample
    conv_p, ssd_p, ret_p, ckv_p, kpe_p, memk_p, memv_p = [], [], [], [], [], [], []
    conv_s, ssd_s, ret_s, ckv_s, kpe_s = [], [], [], [], []
    for i in range(DEPTH):
        j = i // 2
        mk_p, mv_p = mem_kv(mem_prompt, mem_norm[i], w_mkv[i])
        memk_p.append(mk_p)
        memv_p.append(mv_p)
        if i % 2 == 0:
            ab = dict(w_in=ab_w_in[j], conv_w=ab_conv_w[j], conv_b=ab_conv_b[j], dt_bias=ab_dt_bias[j],
                      a_log=ab_a_log[j], d_skip=ab_d_skip[j], ssd_norm=ab_ssd_norm[j], w_out=ab_w_out[j])
            mix_p = functools.partial(
                ssd_ret_mixer, pos=pos_p,
                conv_state=jnp.zeros((bp, CONV_K - 1, CONV_DIM), x_prompt.dtype),
                ssd_state=jnp.zeros((bp, SSD_HEADS, SSD_STATE, SSD_HEAD_DIM), f32),
                ret_state=jnp.zeros((bp, RET_HEADS, RET_HEAD_DIM, RET_HEAD_DIM), f32), **ab)
            mix_s = functools.partial(ssd_ret_mixer, pos=pos_s, conv_state=state_conv[j],
                                      ssd_state=state_ssd[j], ret_state=state_ret[j], **ab)
        else:
            cp = dict(w_in=c_w_in[j], q_norm=c_q_norm[j], kv_norm=c_kv_norm[j], w_uq=c_w_uq[j],
                      w_uk=c_w_uk[j], w_uv=c_w_uv[j], w_out=c_w_out[j])
            mix_p = functools.partial(mla_mixer, pos=pos_p, past_ckv=None, past_kpe=None, **cp)
            mix_s = functools.partial(mla_mixer, pos=pos_s, past_ckv=cache_ckv[j], past_kpe=cache_kpe[j], **cp)
        layer = functools.partial(trunk_layer, norms=norms[i], ffn_w1=ffn_w1[i], ffn_w2=ffn_w2[i],
                                  w_mq=w_mq[i], w_mo=w_mo[i])
        xp, st_p = layer(xp, mix_p, mem_k=mk_p, mem_v=mv_p)
        xs, st_s = layer(xs, mix_s, mem_k=cache_mem_k[i], mem_v=cache_mem_v[i])
        if i % 2 == 0:
            conv_p.append(st_p[0]); ssd_p.append(st_p[1]); ret_p.append(st_p[2])
            conv_s.append(st_s[0]); ssd_s.append(st_s[1]); ret_s.append(st_s[2])
        else:
            ckv_p.append(st_p[0]); kpe_p.append(st_p[1])
            ckv_s.append(st_s[0]); kpe_s.append(st_s[1])
    y_prompt = rmsnorm(xp, final_norm)
    y_sample = rmsnorm(xs, final_norm)
    return (y_prompt, y_sample,
            jnp.stack(conv_p), jnp.stack(ssd_p), jnp.stack(ret_p), jnp.stack(ckv_p), jnp.stack(kpe_p),
            jnp.stack(memk_p), jnp.stack(memv_p),
            jnp.stack(conv_s), jnp.stack(ssd_s), jnp.stack(ret_s), jnp.stack(ckv_s), jnp.stack(kpe_s))
```

```python
import math
import numpy as np
import concourse.bass as bass
import concourse.mybir as mybir
from concourse.bass_utils import run_bass_kernel_spmd

F32 = mybir.dt.float32
BF16 = mybir.dt.bfloat16
AF = mybir.ActivationFunctionType
ALU = mybir.AluOpType
AX = mybir.AxisListType
T = 512
EPS = 1e-6


class Cfg:
    def __init__(s, D=2048, DFF=5632, SEQ=8192, NSAMP=32, TS=16, PAST=2048, MEM=256,
                 SSD_G=4, QL=512, KVL=512, DEPTH=2):
        s.D, s.DFF, s.SEQ, s.NSAMP, s.TS, s.PAST, s.MEM = D, DFF, SEQ, NSAMP, TS, PAST, MEM
        s.DEPTH = DEPTH
        s.KC = D // 128
        s.FC = DFF // 128
        s.NT = SEQ // T
        assert NSAMP * TS == T
        s.SH = D // 64; s.SG = SSD_G; s.HPG = s.SH // SSD_G; s.SN = 128; s.SP = 64
        s.CONV = D + 2 * SSD_G * 128
        s.RH = D // 256
        s.AB_IN = D + s.CONV + s.SH + 4 * D
        s.MH = D // 128; s.QL = QL; s.KVL = KVL
        s.C_IN = QL + KVL + 64
        s.MHEADS = 4; s.MINNER = 512


class Res:
    __slots__ = ("w", "r")

    def __init__(s):
        s.w = None
        s.r = {}


class Buf:
    def __init__(s, t, nslots=1):
        s.t = t
        s.res = [Res() for _ in range(nslots)]

    def __getitem__(s, key):
        return s.t[key]

    def R(s, i=None):
        if i is None:
            return list(s.res)
        if isinstance(i, (list, tuple, range)):
            return [s.res[j] for j in i]
        return [s.res[i]]


class FW:
    ENG = ("pe", "act", "dve", "pool", "sp")
    NDMA = 12

    def __init__(s, nc):
        s.nc = nc
        s.h = {"pe": nc.tensor, "act": nc.scalar, "dve": nc.vector, "pool": nc.gpsimd, "sp": nc.sync}
        s.sem = {e: nc.alloc_semaphore("S_" + e) for e in s.ENG}
        s.cnt = {e: 0 for e in s.ENG}
        s.known = {e: {} for e in s.ENG}
        s.dsem = {q: [nc.alloc_semaphore(f"D_{q}{i}") for i in range(s.NDMA)] for q in ("sp", "pool")}
        s.dn = {"sp": 0, "pool": 0}
        s.semobj = {}
        for e in s.ENG:
            s.semobj[id(s.sem[e])] = s.sem[e]
        for q in s.dsem:
            for x in s.dsem[q]:
                s.semobj[id(x)] = x
        s.dry = False
        s.ninst = 0
        s.out_tokens = []

    def new_epoch(s):
        if s.dry:
            return
        for e in s.ENG:
            sem = s.nc.alloc_semaphore(f"S_{e}_{len(s.semobj)}")
            s.sem[e] = sem
            s.cnt[e] = 0
            s.semobj[id(sem)] = sem

    def _wait(s, e, deps):
        kn = s.known[e]
        own = id(s.sem[e])
        best = {}
        for tok in deps:
            if tok is None:
                continue
            sid, val = tok
            if e == "pe" and sid == own:
                continue
            if kn.get(sid, 0) >= val:
                continue
            if best.get(sid, 0) < val:
                best[sid] = val
        for sid, val in best.items():
            s.h[e].wait_ge(s.semobj[sid], val)
            s.ninst += 1
            kn[sid] = val

    def _deps(s, reads, writes):
        deps = []
        for r in reads:
            deps.append(r.w)
        for w in writes:
            deps.append(w.w)
            for sid, val in w.r.items():
                deps.append((sid, val))
        return deps

    def _commit(s, tok, reads, writes):
        sid, val = tok
        for r in reads:
            if r.r.get(sid, 0) < val:
                r.r[sid] = val
        for w in writes:
            w.w = tok
            w.r = {}

    def op(s, e, fn, reads=(), writes=()):
        if s.dry:
            return
        reads = [r for r in reads if r is not None]
        writes = [w for w in writes if w is not None]
        s._wait(e, s._deps(reads, writes))
        ins = fn(s.h[e])
        s.cnt[e] += 1
        ins.then_inc(s.sem[e], 1)
        s.ninst += 1
        tok = (id(s.sem[e]), s.cnt[e])
        s._commit(tok, reads, writes)
        return tok

    def dma(s, q, out, in_, reads=(), writes=(), is_output=False, **kw):
        if s.dry:
            return
        reads = [r for r in reads if r is not None]
        writes = [w for w in writes if w is not None]
        n = s.dn[q]
        slot = n % s.NDMA
        use = n // s.NDMA
        sem = s.dsem[q][slot]
        deps = s._deps(reads, writes)
        if use > 0:
            deps.append((id(sem), 16 * use))
        s._wait(q, deps)
        s.h[q].dma_start(out=out, in_=in_, **kw).then_inc(sem, 16)
        s.ninst += 1
        s.dn[q] = n + 1
        tok = (id(sem), 16 * (use + 1))
        s._commit(tok, reads, writes)
        if is_output:
            s.out_tokens.append(tok)
        return tok

    def finish(s):
        if s.dry:
            return
        for q in ("sp", "pool"):
            deps = []
            for i, sem in enumerate(s.dsem[q]):
                n = s.dn[q]
                uses = (n - i + s.NDMA - 1) // s.NDMA if n > i else 0
                if uses > 0:
                    deps.append((id(sem), 16 * uses))
            s._wait("sp", deps)


class Stream:
    def __init__(s, fw, ring, nslots):
        s.fw, s.ring, s.nslots = fw, ring, nslots
        s.plan = []
        s.i = 0
        s.issued = 0

    def reset(s):
        s.i = 0
        s.issued = 0

    def barrier(s):
        if s.fw.dry:
            s.plan.append(None)
            return
        assert s.plan[s.i] is None and s.issued <= s.i, (s.i, s.issued)
        s.i += 1
        s.issued = s.i

    def get(s, src, parts, n, reads=()):
        if s.fw.dry:
            s.plan.append((src, parts, n, list(reads)))
            return s.ring[0:parts, 0, 0:n], s.ring.R(0)
        assert s.plan[s.i] is not None and s.plan[s.i][2] == n
        j = s.issued
        while j < len(s.plan) and j <= s.i + s.nslots - 1 and s.plan[j] is not None:
            psrc, pparts, pn, preads = s.plan[j]
            slot = j % s.nslots
            s.fw.dma("sp", s.ring[0:pparts, slot, 0:pn], psrc, reads=preads, writes=s.ring.R(slot))
            j += 1
        s.issued = max(s.issued, j)
        assert s.issued > s.i
        slot = s.i % s.nslots
        s.i += 1
        return s.ring[0:parts, slot, 0:n], s.ring.R(slot)


class View:
    def __init__(s, arena, off, shape, dtype, chunk_bytes=None):
        esz = 2 if dtype == BF16 else 4
        n = 1
        for d in shape[1:]:
            n *= d
        s.nbytes = n * esz
        assert off % 4 == 0 and s.nbytes % 4 == 0
        s.arena, s.off = arena, off
        ap = arena[0:shape[0], off // 4:(off + s.nbytes) // 4]
        if dtype == BF16:
            ap = ap.bitcast(BF16)
        if len(shape) == 3:
            ap = ap.rearrange("p (k n) -> p k n", k=shape[1])
        elif len(shape) == 4:
            ap = ap.rearrange("p (a b n) -> p a b n", a=shape[1], b=shape[2])
        s.ap = ap
        s.chunk_bytes = chunk_bytes if chunk_bytes else s.nbytes

    def __getitem__(s, key):
        return s.ap[key]

    def R(s, k=None, k2=None):
        if k is None:
            lo, hi = s.off, s.off + s.nbytes
        else:
            lo = s.off + k * s.chunk_bytes
            hi = s.off + ((k2 if k2 is not None else k) + 1) * s.chunk_bytes
        return s.arena.res[lo // 1024:(hi + 1023) // 1024]


class WDesc:
    pass


class Model:
    def __init__(m, cfg, stages=("ffn", "mix", "mem")):
        m.c = c = cfg
        m.stages = stages
        m.nc = nc = bass.Bass("TRN2", target_bir_lowering=False)
        m.fw = fw = FW(nc)
        m.ins = {}
        m.outs = {}
        D, KC = c.D, c.KC
        def din(name, shape):
            b = Buf(nc.dram_tensor(name, list(shape), F32, kind="ExternalInput").ap())
            m.ins[name] = b
            return b

        def dout(name, shape):
            b = Buf(nc.dram_tensor(name, list(shape), F32, kind="ExternalOutput").ap())
            m.outs[name] = b
            return b

        m.x_p = din("x_p", [c.SEQ, D]); m.x_s = din("x_s", [T, D]); m.mem_p = din("mem_p", [c.MEM, D])
        m.st_conv = din("st_conv", [c.NSAMP * 3, c.CONV])
        m.st_ssd = din("st_ssd", [c.NSAMP, c.SH, 128, 64])
        m.st_ret = din("st_ret", [c.NSAMP, c.RH, 256, 256])
        m.c_ckv = din("c_ckv", [c.NSAMP, c.PAST, c.KVL]); m.c_kpe = din("c_kpe", [c.NSAMP, c.PAST, 64])
        m.c_mk = din("c_mk", [2, c.NSAMP, c.MEM, 512]); m.c_mv = din("c_mv", [2, c.NSAMP, c.MEM, 512])
        m.norms = din("norms", [2, 4, D]); m.mem_norm = din("mem_norm", [2, D]); m.final_norm = din("final_norm", [D])
        m.ffn_w1 = din("ffn_w1", [2, 2, D, 2 * c.DFF]); m.ffn_w2 = din("ffn_w2", [2, 2, c.DFF, D])
        m.w_mq = din("w_mq", [2, D, 512]); m.w_mkv = din("w_mkv", [2, D, 1024]); m.w_mo = din("w_mo", [2, 512, D])
        m.ab_w_in = din("ab_w_in", [D, c.AB_IN]); m.ab_conv_w = din("ab_conv_w", [4, c.CONV]); m.ab_conv_b = din("ab_conv_b", [c.CONV])
        m.ab_dt_bias = din("ab_dt_bias", [c.SH]); m.ab_a_log = din("ab_a_log", [c.SH]); m.ab_d_skip = din("ab_d_skip", [c.SH])
        m.ab_ssd_norm = din("ab_ssd_norm", [D]); m.ab_w_out = din("ab_w_out", [2 * D, D])
        m.c_w_in = din("c_w_in", [D, c.C_IN]); m.c_q_norm = din("c_q_norm", [c.QL]); m.c_kv_norm = din("c_kv_norm", [c.KVL])
        m.c_w_uq = din("c_w_uq", [c.QL, c.MH * 192]); m.c_w_uk = din("c_w_uk", [c.KVL, c.MH * 128])
        m.c_w_uv = din("c_w_uv", [c.KVL, c.MH * 128]); m.c_w_out = din("c_w_out", [D, D])
        m.rope_ret = din("rope_ret", [2, 128, c.SEQ + T])
        m.rope_mla = din("rope_mla", [2, 64, c.SEQ + T])
        m.ret_dec = din("ret_dec", [2, 64, c.RH * 64 + c.RH * 3])

        m.y_p = dout("y_p", [c.SEQ, D]); m.y_s = dout("y_s", [T, D])
        m.conv_p = dout("conv_p", [3, c.CONV]); m.ssd_p = dout("ssd_p", [c.SH, 128, 64]); m.ret_p = dout("ret_p", [c.RH, 256, 256])
        m.ckv_p = dout("ckv_p", [c.SEQ, c.KVL]); m.kpe_p = dout("kpe_p", [c.SEQ, 64])
        m.memk_p = dout("memk_p", [2, c.MEM, 512]); m.memv_p = dout("memv_p", [2, c.MEM, 512])
        m.conv_s = dout("conv_s", [c.NSAMP * 3, c.CONV]); m.ssd_s = dout("ssd_s", [c.NSAMP, c.SH, 128, 64])
        m.ret_s = dout("ret_s", [c.NSAMP, c.RH, 256, 256])
        m.ckv_s = dout("ckv_s", [T, c.KVL]); m.kpe_s = dout("kpe_s", [T, 64])

        m.RING_N = 6144
        m.NSLOT = 3
        m.x = Buf(nc.alloc_sbuf_tensor("x", [128, KC, T], F32), KC)
        m.xn = Buf(nc.alloc_sbuf_tensor("xn", [128, KC, T], BF16), KC)
        m.ring = Buf(nc.alloc_sbuf_tensor("ring", [128, m.NSLOT, m.RING_N], BF16), m.NSLOT)
        m.ident = Buf(nc.alloc_sbuf_tensor("ident", [128, 128], F32))
        m.identb = Buf(nc.alloc_sbuf_tensor("identb", [128, 128], BF16))
        m.onesb = Buf(nc.alloc_sbuf_tensor("onesb", [128, 128], BF16))
        m.onesf = Buf(nc.alloc_sbuf_tensor("onesf", [128, 128], F32))
        m.gains = Buf(nc.alloc_sbuf_tensor("gains", [128, 14, KC], F32))
        m.epsb = Buf(nc.alloc_sbuf_tensor("epsb", [128, 2], F32))
        m.memK = Buf(nc.alloc_sbuf_tensor("memK", [128, 2, 4, c.MEM], BF16), 2)
        m.memV = Buf(nc.alloc_sbuf_tensor("memV", [128, 2, c.MEM // 128, 512], BF16), 2)
        NB = c.CONV // 128
        m.cw = Buf(nc.alloc_sbuf_tensor("cw", [128, NB, 4], F32))
        m.cb = Buf(nc.alloc_sbuf_tensor("cb", [128, NB], F32))
        m.convst = Buf(nc.alloc_sbuf_tensor("convst", [128, NB, 3], F32))
        m.hb3 = Buf(nc.alloc_sbuf_tensor("hb3", [64, 3, c.SH], F32))
        m.tri = Buf(nc.alloc_sbuf_tensor("tri", [64, 64], F32))
        m.negm = Buf(nc.alloc_sbuf_tensor("negm", [64, 64], F32))
        m.ps = Buf(nc.alloc_psum_tensor("ps", [128, 8, 512], F32), 8)
        m.ps_next = 0
        m.ps_held = set()
        AW = (nc.sbuf_bytes_remaining - 2048) // 1024 * 1024
        m.ARENA_BYTES = AW
        m.arena = Buf(nc.alloc_sbuf_tensor("arena", [128, AW // 4], F32), AW // 1024)
        m.stream = Stream(fw, m.ring, m.NSLOT)
        m.W = {}

    def V(m, off, shape, dtype, chunk_bytes=None):
        assert off + 0 <= m.ARENA_BYTES
        v = View(m.arena, off, shape, dtype, chunk_bytes)
        assert off + v.nbytes <= m.ARENA_BYTES, (off, v.nbytes, m.ARENA_BYTES)
        return v

    def psum(m, hold=False):
        while m.ps_next % 8 in m.ps_held:
            m.ps_next += 1
        b = m.ps_next % 8
        m.ps_next += 1
        if hold:
            m.ps_held.add(b)
        return b

    def release(m, b):
        m.ps_held.discard(b)

    def wprep(m, name, src, K, M, pair=None, mw=None, segs=None):
        c, fw, nc = m.c, m.fw, m.nc
        if name in m.W:
            w = m.W[name]
        else:
            w = WDesc()
            w.K, w.M = K, M
            w.KC = K // 128
            assert K % 128 == 0 and (M % 128 == 0 or M <= 128), (name, K, M)
            w.mw = max(128, (m.RING_N // w.KC) // 128 * 128)
            w.mw = min(w.mw, 512, M)
            if mw is not None:
                w.mw = mw
            w.chunks = []
            c0 = 0
            while c0 < M:
                n = min(w.mw, M - c0)
                w.chunks.append((c0, n))
                c0 += n
            w.buf = Buf(nc.dram_tensor("wb_" + name, [len(w.chunks), 128, w.KC * w.mw], BF16).ap(), 2 * len(w.chunks))
            m.W[name] = w
        for ci, (c0, n) in enumerate(w.chunks):
            dst = w.buf[ci, :, 0:w.KC * n].rearrange("p (k n) -> p k n", k=w.KC)
            if segs is not None:
                d0 = 0
                first = True
                for (s0, sn) in segs:
                    lo, hi = max(d0, c0), min(d0 + sn, c0 + n)
                    if lo < hi:
                        fw.dma("pool", dst[:, :, lo - c0:hi - c0], src[:, s0 + lo - d0:s0 + hi - d0].rearrange("(k p) n -> p k n", p=128),
                               writes=w.buf.R(2 * ci))
                    d0 += sn
            elif pair is None:
                fw.dma("pool", dst, src[:, c0:c0 + n].rearrange("(k p) n -> p k n", p=128), writes=w.buf.R(2 * ci))
            else:
                assert n == 256
                for half in range(2):
                    sc = (half * pair + ci) * 128
                    fw.dma("pool", dst[:, :, half * 128:(half + 1) * 128],
                           src[:, sc:sc + 128].rearrange("(k p) n -> p k n", p=128), writes=w.buf.R(2 * ci + half))
        return w

    def wchunk(m, w, ci):
        c0, n = w.chunks[ci]
        ap, res = m.stream.get(w.buf[ci, :, 0:w.KC * n], 128, w.KC * n, reads=w.buf.R([2 * ci, 2 * ci + 1]))
        return ap.rearrange("p (k n) -> p k n", k=w.KC), res, c0, n

    def linear_fm(m, w, rhs_fn, N, evac, rhs_res, mlo=0, mhi=None):
        fw = m.fw
        for ci in range(len(w.chunks)):
            wv, wres, c0, n = m.wchunk(w, ci)
            for j in range(n // 128):
                mi = (c0 // 128) + j
                b = m.psum()

                def grp(t, wv=wv, j=j, b=b):
                    for k in range(w.KC):
                        ins = t.matmul(m.ps[:, b, 0:N], wv[:, k, j * 128:(j + 1) * 128], rhs_fn(k),
                                       start=(k == 0), stop=(k == w.KC - 1))
                    return ins
                fw.op("pe", grp, reads=wres + rhs_res, writes=m.ps.R(b))
                evac(mi, b)

    def linear_tm(m, w, lhs_fn, nblk, evac, lhs_res):
        fw = m.fw
        for ci in range(len(w.chunks)):
            wv, wres, c0, n = m.wchunk(w, ci)
            for blk in range(nblk):
                b = m.psum()

                def grp(t, wv=wv, blk=blk, b=b, n=n):
                    for k in range(w.KC):
                        ins = t.matmul(m.ps[:, b, 0:n], lhs_fn(k, blk), wv[:, k, 0:n],
                                       start=(k == 0), stop=(k == w.KC - 1))
                    return ins
                fw.op("pe", grp, reads=wres + lhs_res, writes=m.ps.R(b))
                evac(blk, c0, n, b)

    def copy(m, eng, out, in_, reads, writes):
        if eng == "act":
            return m.fw.op("act", lambda a: a.activation(out, in_, AF.Copy), reads=reads, writes=writes)
        return m.fw.op(eng, lambda v: v.tensor_copy(out, in_), reads=reads, writes=writes)

    def load_vecs(m, items, tmp):
        fw = m.fw
        state = {"batch": [], "rows": 0}

        def flush():
            batch, rows = state["batch"], state["rows"]
            if not batch:
                return
            r = 0
            for (vec, dst, dres, n) in batch:
                fw.dma("sp", tmp[r:r + n, :], vec.rearrange("(k p) -> k p", p=128), writes=tmp.R())
                r += n
            b = m.psum()
            fw.op("pe", lambda t: t.transpose(m.ps[:, b, 0:rows], tmp[0:rows, :], m.ident[0:rows, 0:rows]),
                  reads=tmp.R() + m.ident.R(), writes=m.ps.R(b))
            r = 0
            for (vec, dst, dres, n) in batch:
                m.copy("dve", dst, m.ps[:, b, r:r + n], m.ps.R(b), dres)
                r += n
            state["batch"], state["rows"] = [], 0
        for (vec, dst, dres) in items:
            n = dst.shape[-1]
            if state["rows"] + n > 128:
                flush()
            state["batch"].append((vec, dst, dres, n))
            state["rows"] += n
        flush()

    def setup(m):
        c, fw, nc = m.c, m.fw, m.nc
        KC = c.KC
        fw.op("pool", lambda g: g.memset(m.ident[:], 1.0), writes=m.ident.R())
        fw.op("pool", lambda g: g.affine_select(m.ident[:], m.ident[:], pattern=[[-1, 128]], compare_op=ALU.is_equal,
                                                fill=0.0, base=0, channel_multiplier=1), reads=m.ident.R(), writes=m.ident.R())
        fw.op("dve", lambda v: v.tensor_copy(m.identb[:], m.ident[:]), reads=m.ident.R(), writes=m.identb.R())
        fw.op("dve", lambda v: v.memset(m.onesb[:], 1.0), writes=m.onesb.R())
        fw.op("dve", lambda v: v.memset(m.onesf[:], 1.0), writes=m.onesf.R())
        fw.op("dve", lambda v: v.memset(m.epsb[:], EPS), writes=m.epsb.R())
        tmp = m.V(0, [128, 128], F32)
        items = []
        for l in range(2):
            for i in range(4):
                items.append((m.norms[l, i, :], m.gains[:, l * 4 + i, :], m.gains.R()))
            items.append((m.mem_norm[l, :], m.gains[:, 8 + l, :], m.gains.R()))
        items.append((m.final_norm[:], m.gains[:, 10, :], m.gains.R()))
        items.append((m.ab_ssd_norm[:], m.gains[:, 11, :], m.gains.R()))
        items.append((m.c_q_norm[:], m.gains[:, 12, 0:c.QL // 128], m.gains.R()))
        items.append((m.c_kv_norm[:], m.gains[:, 13, 0:c.KVL // 128], m.gains.R()))
        m.load_vecs(items, tmp)
        QL, KVL, MH = c.QL, c.KVL, c.MH
        m.wprep("wcq", m.c_w_in[:, :], c.D, QL, segs=[(0, QL)])
        m.wprep("wckv", m.c_w_in[:, :], c.D, KVL, segs=[(QL, KVL)])
        o = QL + KVL
        m.wprep("wkpe", m.c_w_in[:, :], c.D, 128, segs=[(o, 64), (o, 64)])
        m.wprep("wkpes", m.c_w_in[:, :], c.D, 128, segs=[(o + 32, 32), (o, 32), (o + 32, 32), (o, 32)])
        m.wprep("wuqn", m.c_w_uq[:, :], QL, MH * 128, segs=[(h * 192, 128) for h in range(MH)])
        m.wprep("wuqr", m.c_w_uq[:, :], QL, MH * 64, segs=[(h * 192 + 128, 64) for h in range(MH)])
        sw = []
        for h in range(MH):
            sw += [(h * 192 + 128 + 32, 32), (h * 192 + 128, 32)]
        m.wprep("wuqs", m.c_w_uq[:, :], QL, MH * 64, segs=sw)
        m.wprep("wuk", m.c_w_uk[:, :], KVL, MH * 128)
        m.wprep("wuv", m.c_w_uv[:, :], KVL, MH * 128)
        m.wprep("wcout", m.c_w_out[:, :], c.D, c.D)
        D, G, HPG, SH = c.D, c.SG, c.HPG, c.SH
        GW = HPG * 64
        oz, oxbc, odt = 0, D, D + c.CONV
        oq = odt + SH
        for g in range(G):
            m.wprep(f"wssd{g}", m.ab_w_in[:, :], D, 2 * GW + 256,
                    segs=[(oz + g * GW, GW), (oxbc + g * GW, GW), (oxbc + D + g * 128, 128), (oxbc + D + G * 128 + g * 128, 128)])
        m.wprep("wdt", m.ab_w_in[:, :], D, SH, segs=[(odt, SH)])
        for hp in range(c.RH // 2):
            m.wprep(f"wret{hp}", m.ab_w_in[:, :], D, 2048, segs=[(oq + i * D + hp * 512, 512) for i in range(4)])
        m.wprep("waboy", m.ab_w_out[0:D, :], D, D)
        m.wprep("waboo", m.ab_w_out[D:2 * D, :], D, D)
        NB = c.CONV // 128
        items = [(m.ab_conv_w[j, :], m.cw[:, :, j], m.cw.R()) for j in range(4)]
        items.append((m.ab_conv_b[:], m.cb[:, :], m.cb.R()))
        m.load_vecs(items, tmp)
        fw.op("dve", lambda v: v.memset(m.convst[:], 0.0), writes=m.convst.R())
        for i, vec in enumerate((m.ab_dt_bias, m.ab_a_log, m.ab_d_skip)):
            fw.dma("sp", m.hb3[:, i, :], vec[:].partition_broadcast(64), writes=m.hb3.R())
        fw.op("act", lambda a: a.activation(m.hb3[:, 1, :], m.hb3[:, 1, :], AF.Exp), reads=m.hb3.R(), writes=m.hb3.R())
        fw.op("dve", lambda v: v.tensor_scalar(m.hb3[:, 1, :], m.hb3[:, 1, :], -1.0, None, op0=ALU.mult), reads=m.hb3.R(), writes=m.hb3.R())
        fw.op("pool", lambda g_: g_.memset(m.tri[:], 1.0), writes=m.tri.R())
        fw.op("pool", lambda g_: g_.affine_select(m.tri[:], m.tri[:], pattern=[[1, 64]], compare_op=ALU.is_ge, fill=0.0, base=0, channel_multiplier=-1),
              reads=m.tri.R(), writes=m.tri.R())
        fw.op("pool", lambda g_: g_.memset(m.negm[:], 0.0), writes=m.negm.R())
        fw.op("pool", lambda g_: g_.affine_select(m.negm[:], m.negm[:], pattern=[[1, 64]], compare_op=ALU.is_ge, fill=-1e30, base=0, channel_multiplier=-1),
              reads=m.negm.R(), writes=m.negm.R())
        if not hasattr(m, "ssd_st"):
            m.ssd_st = Buf(nc.dram_tensor("ssd_st", [c.SH, 128, 64], F32).ap())
            m.ret_st = Buf(nc.dram_tensor("ret_st", [c.RH, 256, 256], F32).ap())
        NJ = c.NT + 1 + 2 * (c.PAST // T)
        m.NJ = NJ
        if not hasattr(m, "kvs"):
            m.kvs = Buf(nc.dram_tensor("kvs", [MH, NJ, 128, 1536], BF16).ap(), NJ)
        FC = c.FC
        for l in range(2):
            for i in range(2):
                m.wprep(f"w1_{l}{i}", m.ffn_w1[l, i], c.D, 2 * c.DFF, pair=FC, mw=256)
                m.wprep(f"w2_{l}{i}", m.ffn_w2[l, i], c.DFF, c.D)
            m.wprep(f"wmq_{l}", m.w_mq[l], c.D, 512)
            m.wprep(f"wmkv_{l}", m.w_mkv[l], c.D, 1024)
            m.wprep(f"wmo_{l}", m.w_mo[l], 512, c.D)

    def norm_fm(m, src, gidx, dst, N, KC, sq_view, rstd_view, col0=0, gk0=0, dk0=0):
        fw = m.fw
        b = m.psum()
        cs = slice(col0, col0 + N)
        for k in range(KC):
            sl = k % 2
            fw.op("act", lambda a, k=k, sl=sl: a.activation(sq_view[:, sl, 0:N], src[:, k, cs], AF.Square),
                  reads=src.R(k), writes=sq_view.R(sl))
            fw.op("pe", lambda t, k=k, sl=sl: t.matmul(m.ps[:, b, 0:N], m.onesb[:], sq_view[:, sl, 0:N],
                                                        start=(k == 0), stop=(k == KC - 1)),
                  reads=sq_view.R(sl) + m.onesb.R(), writes=m.ps.R(b))
        nfeat = KC * 128
        fw.op("act", lambda a: a.activation(rstd_view[:, 0:N], m.ps[:, b, 0:N], AF.Sqrt, bias=m.epsb[:, 0:1], scale=1.0 / nfeat),
              reads=m.ps.R(b) + m.epsb.R(), writes=rstd_view.R())
        fw.op("dve", lambda v: v.reciprocal(rstd_view[:, 0:N], rstd_view[:, 0:N]), reads=rstd_view.R(), writes=rstd_view.R())
        for k in range(KC):
            fw.op("dve", lambda v, k=k: v.scalar_tensor_tensor(dst[:, dk0 + k, cs], src[:, k, cs], m.gains[:, gidx, gk0 + k:gk0 + k + 1],
                                                                 rstd_view[:, 0:N], op0=ALU.mult, op1=ALU.mult),
                  reads=src.R(k) + m.gains.R() + rstd_view.R(), writes=dst.R(dk0 + k))

    def load_tile(m, rows_ap, rows_res):
        c, fw = m.c, m.fw
        xin = m.V(0, [128, 4, c.D], F32, chunk_bytes=c.D * 4)
        for blk in range(4):
            fw.dma("sp", xin[:, blk, :], rows_ap[blk * 128:(blk + 1) * 128, :], reads=rows_res, writes=xin.R(blk))
        for k in range(c.KC):
            b = m.psum()

            def tr(t, k=k, b=b):
                for blk in range(4):
                    ins = t.transpose(m.ps[:, b, blk * 128:(blk + 1) * 128], xin[:, blk, k * 128:(k + 1) * 128], m.ident[:])
                return ins
            fw.op("pe", tr, reads=xin.R() + m.ident.R(), writes=m.ps.R(b))
            m.copy("act" if k % 2 else "dve", m.x[:, k, :], m.ps[:, b, :], m.ps.R(b), m.x.R(k))

    def store_tile(m, rows_ap, rows_res):
        c, fw = m.c, m.fw
        yout = m.V(0, [128, 4, c.D], F32, chunk_bytes=c.D * 4)
        sq = m.V(c.D * 16, [128, 2, T], BF16, chunk_bytes=T * 2)
        rstd = m.V(c.D * 16 + 2048, [128, T], F32)
        m.norm_fm(m.x, 10, m.x, T, c.KC, sq, rstd)
        for k in range(c.KC):
            b = m.psum()

            def tr(t, k=k, b=b):
                for blk in range(4):
                    ins = t.transpose(m.ps[:, b, blk * 128:(blk + 1) * 128], m.x[:, k, blk * 128:(blk + 1) * 128], m.ident[:])
                return ins
            fw.op("pe", tr, reads=m.x.R(k) + m.ident.R(), writes=m.ps.R(b))
            m.copy("act" if k % 2 else "dve", yout[:, :, k * 128:(k + 1) * 128],
                   m.ps[:, b, :].rearrange("p (b n) -> p b n", b=4), m.ps.R(b), yout.R())
        for blk in range(4):
            fw.dma("sp", rows_ap[blk * 128:(blk + 1) * 128, :], yout[:, blk, :], reads=yout.R(), writes=rows_res, is_output=True)

    def ffn(m, l, i):
        c, fw = m.c, m.fw
        FC = c.FC
        H = m.V(0, [128, FC, T], BF16, chunk_bytes=T * 2)
        base = FC * T * 2
        sq = m.V(base, [128, 2, T], BF16, chunk_bytes=T * 2)
        rstd = m.V(base + 2048, [128, T], F32)
        sa = m.V(base + 4096, [128, 2, T], F32, chunk_bytes=T * 4)
        m.norm_fm(m.x, l * 4 + (0 if i == 0 else 3), m.xn, T, c.KC, sq, rstd)

        def evac1(mi, b):
            j, sl = mi // 2, (mi // 2) % 2
            if mi % 2 == 0:
                fw.op("act", lambda a: a.activation(sa[:, sl, :], m.ps[:, b, :], AF.Silu), reads=m.ps.R(b), writes=sa.R(sl))
            else:
                fw.op("dve", lambda v: v.tensor_tensor(H[:, j, :], m.ps[:, b, :], sa[:, sl, :], op=ALU.mult),
                      reads=m.ps.R(b) + sa.R(sl), writes=H.R(j))
        m.linear_fm(m.W[f"w1_{l}{i}"], lambda k: m.xn[:, k, :], T, evac1, m.xn.R())

        def evac2(mi, b):
            fw.op("dve", lambda v: v.scalar_tensor_tensor(m.x[:, mi, :], m.ps[:, b, :], 0.5, m.x[:, mi, :], op0=ALU.mult, op1=ALU.add),
                  reads=m.ps.R(b) + m.x.R(mi), writes=m.x.R(mi))
        m.linear_fm(m.W[f"w2_{l}{i}"], lambda k: H[:, k, :], T, evac2, H.R())

    def drain_pool_dmas(m):
        fw = m.fw
        if fw.dry:
            return
        deps = []
        n = fw.dn["pool"]
        for i, sem in enumerate(fw.dsem["pool"]):
            uses = (n - i + fw.NDMA - 1) // fw.NDMA if n > i else 0
            if uses > 0:
                deps.append((id(sem), 16 * uses))
        fw._wait("sp", deps)

    def memkv_prompt(m):
        c, fw = m.c, m.fw
        D, KC, MB = c.D, c.KC, c.MEM // 128
        NM = c.MEM
        min_ = m.V(0, [128, MB, D], F32, chunk_bytes=D * 4)
        o = MB * D * 4
        memT = m.V(o, [128, KC, NM], F32, chunk_bytes=NM * 4); o += KC * NM * 4
        memn = m.V(o, [128, KC, NM], BF16, chunk_bytes=NM * 2); o += KC * NM * 2
        sq = m.V(o, [128, 2, T], BF16, chunk_bytes=T * 2); o += 2048
        rstd = m.V(o, [128, T], F32); o += 2048
        kvt = m.V(o, [128, MB, 1024], F32, chunk_bytes=4096); o += MB * 4096
        kb = m.V(o, [128, MB, 512], BF16, chunk_bytes=1024); o += MB * 1024
        for blk in range(MB):
            fw.dma("sp", min_[:, blk, :], m.mem_p[blk * 128:(blk + 1) * 128, :], writes=min_.R(blk))
        for k in range(KC):
            b = m.psum()

            def tr(t, k=k, b=b):
                for blk in range(MB):
                    ins = t.transpose(m.ps[:, b, blk * 128:(blk + 1) * 128], min_[:, blk, k * 128:(k + 1) * 128], m.ident[:])
                return ins
            fw.op("pe", tr, reads=min_.R() + m.ident.R(), writes=m.ps.R(b))
            m.copy("act" if k % 2 else "dve", memT[:, k, :], m.ps[:, b, 0:NM], m.ps.R(b), memT.R(k))
        for l in range(2):
            m.norm_fm(memT, 8 + l, memn, NM, KC, sq, rstd)

            def evac(blk, c0, n, b, l=l):
                m.copy("act", kvt[:, blk, c0:c0 + n], m.ps[:, b, 0:n], m.ps.R(b), kvt.R(blk))
            m.linear_tm(m.W[f"wmkv_{l}"], lambda k, blk: memn[:, k, blk * 128:(blk + 1) * 128], MB, evac, memn.R())
            for blk in range(MB):
                fw.dma("sp", m.memk_p[l, blk * 128:(blk + 1) * 128, :], kvt[:, blk, 0:512], reads=kvt.R(blk), writes=m.memk_p.R(), is_output=True)
                fw.dma("sp", m.memv_p[l, blk * 128:(blk + 1) * 128, :], kvt[:, blk, 512:1024], reads=kvt.R(blk), writes=m.memv_p.R(), is_output=True)
                m.copy("dve", m.memV[:, l, blk, :], kvt[:, blk, 512:1024], kvt.R(blk), m.memV.R(l))
                m.copy("dve", kb[:, blk, :], kvt[:, blk, 0:512], kvt.R(blk), kb.R(blk))
            m.kT_from_tok(kb, MB, m.memK[:, l], m.memK.R(l))

    def kT_from_tok(m, kb, MB, dstK, dres):
        fw = m.fw
        b = m.psum()
        psb = m.ps[:, b, :].bitcast(BF16)

        def tr(t):
            for h in range(4):
                for blk in range(MB):
                    col = (h * MB + blk) * 128
                    ins = t.transpose(psb[:, col:col + 128], kb[:, blk, h * 128:(h + 1) * 128], m.identb[:])
            return ins
        fw.op("pe", tr, reads=kb.R() + m.identb.R(), writes=m.ps.R(b))
        m.copy("dve", dstK, psb[:, 0:4 * MB * 128].rearrange("p (h n) -> p h n", h=4), m.ps.R(b), dres)

    def memattn(m, l, sample):
        c, fw = m.c, m.fw
        MB = c.MEM // 128
        o = 0
        sq = m.V(o, [128, 2, T], BF16, chunk_bytes=T * 2); o += 2048
        rstd = m.V(o, [128, T], F32); o += 2048
        qT = m.V(o, [128, 4, T], BF16, chunk_bytes=1024); o += 4096
        oT = m.V(o, [128, 4, T], BF16, chunk_bytes=1024); o += 4096
        pt = m.V(o, [128, 2, T], BF16, chunk_bytes=1024); o += 2048
        rs = m.V(o, [128, T], F32); o += 2048
        kin = m.V(o, [128, 2, MB, 512], F32, chunk_bytes=MB * 2048); o += 2 * MB * 2048
        kb = m.V(o, [128, MB, 512], BF16, chunk_bytes=1024); o += MB * 1024
        vb = m.V(o, [128, MB, 512], BF16, chunk_bytes=1024); o += MB * 1024
        kTs = m.V(o, [128, 4, c.MEM], BF16); o += 4 * c.MEM * 2
        m.norm_fm(m.x, l * 4 + 2, m.xn, T, c.KC, sq, rstd)

        def evq(mi, b):
            m.copy("act", qT[:, mi, :], m.ps[:, b, :], m.ps.R(b), qT.R(mi))
        m.linear_fm(m.W[f"wmq_{l}"], lambda k: m.xn[:, k, :], T, evq, m.xn.R())
        scale = 128.0 ** -0.5
        state = {"n": 0}

        def attend(col0, ncol, K, Kres, Vfn, Vres):
            cs = slice(col0, col0 + ncol)
            for h in range(4):
                bo = m.psum(hold=True)
                bs = m.psum(hold=True)
                for mb in range(MB):
                    b = m.psum()
                    sl = state["n"] % 2
                    state["n"] += 1
                    fw.op("pe", lambda t: t.matmul(m.ps[:, b, 0:ncol], K[:, h, mb * 128:(mb + 1) * 128], qT[:, h, cs], start=True, stop=True),
                          reads=Kres + qT.R(h), writes=m.ps.R(b))
                    fw.op("act", lambda a: a.activation(pt[:, sl, 0:ncol], m.ps[:, b, 0:ncol], AF.Exp, scale=scale),
                          reads=m.ps.R(b), writes=pt.R(sl))
                    fw.op("pe", lambda t: t.matmul(m.ps[:, bo, 0:ncol], Vfn(mb, h), pt[:, sl, 0:ncol], start=(mb == 0), stop=(mb == MB - 1)),
                          reads=Vres + pt.R(sl), writes=m.ps.R(bo))
                    fw.op("pe", lambda t: t.matmul(m.ps[:, bs, 0:ncol], m.onesb[:], pt[:, sl, 0:ncol], start=(mb == 0), stop=(mb == MB - 1)),
                          reads=m.onesb.R() + pt.R(sl), writes=m.ps.R(bs))
                fw.op("dve", lambda v: v.reciprocal(rs[:, 0:ncol], m.ps[:, bs, 0:ncol]), reads=m.ps.R(bs), writes=rs.R())
                fw.op("dve", lambda v: v.tensor_tensor(oT[:, h, cs], m.ps[:, bo, 0:ncol], rs[:, 0:ncol], op=ALU.mult),
                      reads=m.ps.R(bo) + rs.R(), writes=oT.R(h))
                m.release(bo)
                m.release(bs)
        if not sample:
            attend(0, T, m.memK[:, l], m.memK.R(l), lambda mb, h: m.memV[:, l, mb, h * 128:(h + 1) * 128], m.memV.R(l))
        else:
            for s in range(c.NSAMP):
                fw.dma("sp", kin[:, 0], m.c_mk[l, s].rearrange("(b p) n -> p b n", p=128), writes=kin.R(0))
                fw.dma("sp", kin[:, 1], m.c_mv[l, s].rearrange("(b p) n -> p b n", p=128), writes=kin.R(1))
                m.copy("dve", kb[:], kin[:, 0], kin.R(0), kb.R())
                m.copy("act", vb[:], kin[:, 1], kin.R(1), vb.R())
                m.kT_from_tok(kb, MB, kTs[:], kTs.R())
                attend(s * c.TS, c.TS, kTs, kTs.R(), lambda mb, h: vb[:, mb, h * 128:(h + 1) * 128], vb.R())

        def evo(mi, b):
            fw.op("dve", lambda v: v.tensor_tensor(m.x[:, mi, :], m.ps[:, b, :], m.x[:, mi, :], op=ALU.add),
                  reads=m.ps.R(b) + m.x.R(mi), writes=m.x.R(mi))
        m.linear_fm(m.W[f"wmo_{l}"], lambda k: oT[:, k, :], T, evo, oT.R())

    def emit(m):
        c, fw = m.c, m.fw
        m.setup()
        m.drain_pool_dmas()
        m.stream.barrier()
        m.memkv_prompt()
        ntiles = c.NT + 1
        for ti in range(ntiles):
            sample = ti == c.NT
            if sample:
                rows_in, rin_res, rows_out, rout_res = m.x_s[:, :], m.x_s.R(), m.y_s[:, :], m.y_s.R()
            else:
                rows_in, rin_res = m.x_p[ti * T:(ti + 1) * T, :], m.x_p.R()
                rows_out, rout_res = m.y_p[ti * T:(ti + 1) * T, :], m.y_p.R()
            if ti % 2 == 0:
                fw.new_epoch()
            m.load_tile(rows_in, rin_res)
            for l in range(2):
                if "ffn" in m.stages:
                    m.ffn(l, 0)
                m.mixer(l, ti, sample)
                if "mem" in m.stages:
                    m.memattn(l, sample)
                if "ffn" in m.stages:
                    m.ffn(l, 1)
            m.store_tile(rows_out, rout_res)
        fw.finish()

    def build(m):
        fw = m.fw
        fw.dry = True
        m.ps_next = 0
        m.emit()
        fw.dry = False
        m.ps_next = 0
        m.ps_held = set()
        m.stream.reset()
        m.emit()
        return m.nc


ROPE_BASE = 10000.0


def _tables(c):
    pos = np.concatenate([np.arange(c.SEQ), c.PAST + (np.arange(T) % c.TS)]).astype(np.float32)
    def tab(half):
        inv = (ROPE_BASE ** (-np.arange(half, dtype=np.float32) / np.float32(half))).astype(np.float32)
        ang = (inv[:, None] * pos[None, :]).astype(np.float32)
        return np.cos(ang).astype(np.float32), np.sin(ang).astype(np.float32)
    cr, sr = tab(128)
    rope_ret = np.stack([cr, sr]).astype(np.float32)
    cm, sm = tab(32)
    rope_mla = np.stack([np.concatenate([cm, cm], 0), np.concatenate([-sm, sm], 0)]).astype(np.float32)
    RH = c.RH
    lg = np.log1p(-np.exp2(-5.0 - np.arange(RH, dtype=np.float32))).astype(np.float32)
    out = np.zeros((2, 64, RH * 64 + RH * 3), np.float32)
    for li, L in enumerate((64, c.TS)):
        s_ = np.arange(64)[:, None, None]
        l_ = np.arange(64)[None, None, :]
        dec = np.where((s_ <= l_) & (l_ < L) & (s_ < L), np.exp(lg[None, :, None] * (l_ - s_)), 0.0)
        out[li, :, :RH * 64] = dec.reshape(64, RH * 64)
        sv = np.arange(64)[:, None]
        out[li, :, RH * 64 + 0 * RH:RH * 64 + 1 * RH] = np.exp(lg[None, :] * (sv + 1))
        out[li, :, RH * 64 + 1 * RH:RH * 64 + 2 * RH] = np.where(sv < L, np.exp(lg[None, :] * np.maximum(L - 1 - sv, 0)), 0.0)
        out[li, :, RH * 64 + 2 * RH:RH * 64 + 3 * RH] = np.exp(lg[None, :] * L)
    return rope_ret, rope_mla, out.astype(np.float32)


def make_in_maps(inp, c, ncores=8):
    f = lambda a: np.ascontiguousarray(np.asarray(a, dtype=np.float32))
    rope_ret, rope_mla, ret_dec = _tables(c)
    shared = {
        "x_s": f(inp["x_sample"]).reshape(T, c.D),
        "st_conv": f(inp["state_conv"][0]).reshape(c.NSAMP * 3, c.CONV),
        "st_ssd": f(inp["state_ssd"][0]), "st_ret": f(inp["state_ret"][0]),
        "c_ckv": f(inp["cache_ckv"][0]), "c_kpe": f(inp["cache_kpe"][0]),
        "c_mk": f(inp["cache_mem_k"]).reshape(2, c.NSAMP, c.MEM, 512), "c_mv": f(inp["cache_mem_v"]).reshape(2, c.NSAMP, c.MEM, 512),
        "norms": f(inp["norms"]), "mem_norm": f(inp["mem_norm"]), "final_norm": f(inp["final_norm"]),
        "ffn_w1": f(inp["ffn_w1"]), "ffn_w2": f(inp["ffn_w2"]),
        "w_mq": f(inp["w_mq"]), "w_mkv": f(inp["w_mkv"]), "w_mo": f(inp["w_mo"]),
        "ab_w_in": f(inp["ab_w_in"][0]), "ab_conv_w": f(inp["ab_conv_w"][0]), "ab_conv_b": f(inp["ab_conv_b"][0]),
        "ab_dt_bias": f(inp["ab_dt_bias"][0]), "ab_a_log": f(inp["ab_a_log"][0]), "ab_d_skip": f(inp["ab_d_skip"][0]),
        "ab_ssd_norm": f(inp["ab_ssd_norm"][0]), "ab_w_out": f(inp["ab_w_out"][0]),
        "c_w_in": f(inp["c_w_in"][0]), "c_q_norm": f(inp["c_q_norm"][0]), "c_kv_norm": f(inp["c_kv_norm"][0]),
        "c_w_uq": f(inp["c_w_uq"][0]), "c_w_uk": f(inp["c_w_uk"][0]).reshape(c.KVL, c.MH * 128),
        "c_w_uv": f(inp["c_w_uv"][0]).reshape(c.KVL, c.MH * 128), "c_w_out": f(inp["c_w_out"][0]),
        "rope_ret": rope_ret, "rope_mla": rope_mla, "ret_dec": ret_dec,
    }
    xp, mp = f(inp["x_prompt"]), f(inp["mem_prompt"])
    nb = xp.shape[0]
    maps = []
    for i in range(ncores):
        d = dict(shared)
        d["x_p"] = xp[i % nb]
        d["mem_p"] = mp[i % nb]
        maps.append(d)
    return maps


def gather(res, c, nb=4):
    r = res
    st = lambda k: np.stack([r[b][k] for b in range(nb)])
    y_p = st("y_p")
    y_s = r[0]["y_s"].reshape(c.NSAMP, c.TS, c.D)
    conv_p = st("conv_p")[None]
    ssd_p = st("ssd_p")[None]
    ret_p = st("ret_p")[None]
    ckv_p = st("ckv_p")[None]
    kpe_p = st("kpe_p")[None]
    memk = np.stack([r[b]["memk_p"] for b in range(nb)], axis=1).reshape(2, nb, c.MEM, 4, 128)
    memv = np.stack([r[b]["memv_p"] for b in range(nb)], axis=1).reshape(2, nb, c.MEM, 4, 128)
    conv_s = r[0]["conv_s"].reshape(1, c.NSAMP, 3, c.CONV)
    ssd_s = r[0]["ssd_s"][None]
    ret_s = r[0]["ret_s"][None]
    ckv_s = r[0]["ckv_s"].reshape(1, c.NSAMP, c.TS, c.KVL)
    kpe_s = r[0]["kpe_s"].reshape(1, c.NSAMP, c.TS, 64)
    return (y_p, y_s, conv_p, ssd_p, ret_p, ckv_p, kpe_p, memk, memv, conv_s, ssd_s, ret_s, ckv_s, kpe_s)


def kernel(**inputs):
    c = Cfg()
    m = Model(c, stages=("ffn", "mix", "mem"))
    nc = m.build()
    maps = make_in_maps(inputs, c)
    res = run_bass_kernel_spmd(nc, maps, core_ids=list(range(8)))
    outs = gather(res.results, c)
    return tuple(np.ascontiguousarray(o, dtype=np.float32) for o in outs)


def _mla(m, ti, sample):
    c, fw = m.c, m.fw
    QLC, KVLC, MH = c.QL // 128, c.KVL // 128, c.MH
    PJ = c.PAST // T
    o = 0
    def alloc(shape, dt, cb=None):
        nonlocal o
        v = m.V(o, shape, dt, cb)
        o += (v.nbytes + 1023) // 1024 * 1024
        return v
    sq = alloc([128, 2, T], BF16, T * 2)
    rstd = alloc([128, T], F32)
    qnT = alloc([128, MH, T], BF16, T * 2)
    qrT = alloc([128, MH // 2, T], BF16, T * 2)
    pt = alloc([128, 2, T], BF16, T * 2)
    rs = alloc([128, T], F32)
    snk = alloc([128, MH, c.TS], BF16)
    snv = alloc([c.TS, MH, 128], BF16)
    snr = alloc([128, c.TS], BF16)
    PD0 = o
    tab = alloc([128, 2, T], F32, T * 4)
    ckvb = alloc([128, KVLC, T], BF16, T * 2)
    cqn = alloc([128, QLC, T], BF16, T * 2)
    krf = alloc([128, T], F32)
    krb = alloc([128, T], BF16)
    kpe = alloc([128, 2, T], F32, T * 4)
    kpo = alloc([128, 4, 64], F32)
    A0 = o
    cqT = alloc([128, QLC, T], F32, T * 4)
    ckvT = alloc([128, KVLC, T], F32, T * 4)
    qraw = alloc([128, MH // 2, T], F32, T * 4)
    cko = alloc([128, 4, c.KVL], F32, c.KVL * 4)
    o = A0
    kst = alloc([128, MH, T], BF16, T * 2)
    vst = alloc([128, 4, MH * 128], BF16, MH * 256)
    col0 = c.SEQ if sample else ti * T
    jnew = c.NT if sample else ti

    m.norm_fm(m.x, 4 + 1, m.xn, T, c.KC, sq, rstd)
    for half in range(2):
        fw.dma("sp", tab[half * 64:(half + 1) * 64, :, :], m.rope_mla[:, :, col0:col0 + T].rearrange("a p n -> p a n"), writes=tab.R())
    xr = m.xn.R()
    def ev_to(view):
        def ev(mi, b):
            m.copy("act" if mi % 2 else "dve", view[:, mi, :], m.ps[:, b, :], m.ps.R(b), view.R(mi))
        return ev
    m.linear_fm(m.W["wcq"], lambda k: m.xn[:, k, :], T, ev_to(cqT), xr)
    m.linear_fm(m.W["wckv"], lambda k: m.xn[:, k, :], T, ev_to(ckvT), xr)
    m.linear_fm(m.W["wkpe"], lambda k: m.xn[:, k, :], T, lambda mi, b: m.copy("act", kpe[:, 0, :], m.ps[:, b, :], m.ps.R(b), kpe.R(0)), xr)
    m.linear_fm(m.W["wkpes"], lambda k: m.xn[:, k, :], T, lambda mi, b: m.copy("act", kpe[:, 1, :], m.ps[:, b, :], m.ps.R(b), kpe.R(1)), xr)
    fw.op("dve", lambda v: v.tensor_tensor(krf[:], kpe[:, 0, :], tab[:, 0, :], op=ALU.mult), reads=kpe.R(0) + tab.R(), writes=krf.R())
    fw.op("dve", lambda v: v.tensor_tensor(kpe[:, 1, :], kpe[:, 1, :], tab[:, 1, :], op=ALU.mult), reads=kpe.R(1) + tab.R(), writes=kpe.R(1))
    fw.op("dve", lambda v: v.tensor_tensor(krf[:], krf[:], kpe[:, 1, :], op=ALU.add), reads=kpe.R(1) + krf.R(), writes=krf.R())
    m.copy("act", krb[:], krf[:], krf.R(), krb.R())
    m.norm_fm(cqT, 12, cqn, T, QLC, sq, rstd)
    m.norm_fm(ckvT, 13, ckvT, T, KVLC, sq, rstd)
    for k in range(KVLC):
        m.copy("act", ckvb[:, k, :], ckvT[:, k, :], ckvT.R(k), ckvb.R(k))
    ck_out, kp_out = (m.ckv_s, m.kpe_s) if sample else (m.ckv_p, m.kpe_p)
    r0 = 0 if sample else ti * T
    for blk in range(4):
        for c4 in range(0, KVLC, 4):
            nn = min(4, KVLC - c4)
            b = m.psum()
            def tr(t, blk=blk, c4=c4, nn=nn, b=b):
                for cc in range(nn):
                    ins = t.transpose(m.ps[:, b, cc * 128:(cc + 1) * 128], ckvT[:, c4 + cc, blk * 128:(blk + 1) * 128], m.ident[:])
                return ins
            fw.op("pe", tr, reads=ckvT.R() + m.ident.R(), writes=m.ps.R(b))
            m.copy("dve", cko[:, blk, c4 * 128:(c4 + nn) * 128], m.ps[:, b, 0:nn * 128], m.ps.R(b), cko.R(blk))
        fw.dma("sp", ck_out[r0 + blk * 128:r0 + (blk + 1) * 128, :], cko[:, blk, :], reads=cko.R(blk), writes=ck_out.R(), is_output=True)
    b = m.psum()
    def trk(t):
        for blk in range(4):
            ins = t.transpose(m.ps[:, b, blk * 64:(blk + 1) * 64], krf[0:64, blk * 128:(blk + 1) * 128], m.ident[0:64, 0:64])
        return ins
    fw.op("pe", trk, reads=krf.R() + m.ident.R(), writes=m.ps.R(b))
    m.copy("dve", kpo[:], m.ps[:, b, 0:256].rearrange("p (b n) -> p b n", b=4), m.ps.R(b), kpo.R())
    fw.dma("sp", kp_out[r0:r0 + T, :].rearrange("(b p) n -> p b n", p=128), kpo[:], reads=kpo.R(), writes=kp_out.R(), is_output=True)
    m.linear_fm(m.W["wuqn"], lambda k: cqn[:, k, :], T, ev_to(qnT), cqn.R())
    m.linear_fm(m.W["wuqr"], lambda k: cqn[:, k, :], T, ev_to(qraw), cqn.R())
    def ev_sw(mi, b):
        fw.op("dve", lambda v: v.tensor_tensor(qraw[:, mi, :], qraw[:, mi, :], tab[:, 0, :], op=ALU.mult), reads=qraw.R(mi) + tab.R(), writes=qraw.R(mi))
        fw.op("dve", lambda v: v.tensor_tensor(rs[:], m.ps[:, b, :], tab[:, 1, :], op=ALU.mult), reads=m.ps.R(b) + tab.R(), writes=rs.R())
        fw.op("dve", lambda v: v.tensor_tensor(qrT[:, mi, :], qraw[:, mi, :], rs[:], op=ALU.add), reads=qraw.R(mi) + rs.R(), writes=qrT.R(mi))
    m.linear_fm(m.W["wuqs"], lambda k: cqn[:, k, :], T, ev_sw, cqn.R())
    m.linear_fm(m.W["wuk"], lambda k: ckvb[:, k, :], T, ev_to(kst), ckvb.R())
    def ev_v(blk, c0, n, b):
        m.copy("act" if blk % 2 else "dve", vst[:, blk, c0:c0 + n], m.ps[:, b, 0:n], m.ps.R(b), vst.R(blk))
    m.linear_tm(m.W["wuv"], lambda k, blk: ckvb[:, k, blk * 128:(blk + 1) * 128], 4, ev_v, ckvb.R())
    kres = m.kvs.R(jnew)
    fw.dma("sp", m.kvs[:, jnew, :, 0:512].rearrange("h p n -> p h n"), kst[:], reads=kst.R(), writes=kres)
    for blk in range(4):
        fw.dma("sp", m.kvs[:, jnew, :, 512 + blk * 128:512 + (blk + 1) * 128].rearrange("h p n -> p h n"),
               vst[:, blk, :].rearrange("p (h n) -> p h n", h=MH), reads=vst.R(blk), writes=kres)
    for h in range(MH):
        fw.dma("sp", m.kvs[h, jnew, :, 1024:1536], krb[:], reads=krb.R(), writes=kres)
    m.stream.barrier()

    scale = 192.0 ** -0.5
    st = {"n": 0}
    def block(h, bo, bs, c0, N, nk, kn_l, kr_l, v_l, kres_, first, last, mask64=False):
        rp = slice(0, 64) if h % 2 == 0 else slice(64, 128)
        b = m.psum()
        sl = st["n"] % 2
        st["n"] += 1
        def qk(t):
            t.matmul(m.ps[0:nk, b, 0:N], kn_l, qnT[:, h, c0:c0 + N], start=True, stop=False)
            return t.matmul(m.ps[0:nk, b, 0:N], kr_l(rp), qrT[rp, h // 2, c0:c0 + N], start=False, stop=True)
        fw.op("pe", qk, reads=kres_ + qnT.R(h) + qrT.R(h // 2), writes=m.ps.R(b))
        fw.op("act", lambda a: a.activation(pt[0:nk, sl, 0:N], m.ps[0:nk, b, 0:N], AF.Exp, scale=scale), reads=m.ps.R(b), writes=pt.R(sl))
        if mask64:
            fw.op("pool", lambda g: g.memset(pt[64:128, sl, 0:64], 0.0), reads=pt.R(sl), writes=pt.R(sl))
        def pv(t):
            t.matmul(m.ps[:, bo, c0:c0 + N], v_l, pt[0:nk, sl, 0:N], start=first, stop=last)
            return t.matmul(m.ps[:, bs, c0:c0 + N], m.onesb[0:nk, :], pt[0:nk, sl, 0:N], start=first, stop=last)
        fw.op("pe", pv, reads=kres_ + pt.R(sl) + m.onesb.R(), writes=m.ps.R(bo) + m.ps.R(bs))

    def finish_head(h, bo, bs, c0, N):
        fw.op("dve", lambda v: v.reciprocal(rs[:, 0:N], m.ps[:, bs, c0:c0 + N]), reads=m.ps.R(bs), writes=rs.R())
        fw.op("dve", lambda v: v.tensor_tensor(qnT[:, h, c0:c0 + N], m.ps[:, bo, c0:c0 + N], rs[:, 0:N], op=ALU.mult),
              reads=m.ps.R(bo) + rs.R(), writes=qnT.R(h))
        m.release(bo)
        m.release(bs)

    def chunk(h, j):
        ap, res = m.stream.get(m.kvs[h, j], 128, 1536, reads=m.kvs.R(j))
        return ap, res

    if not sample:
        for h in range(MH):
            bo, bs = m.psum(hold=True), m.psum(hold=True)
            for j in range(ti + 1):
                ap, res = chunk(h, j)
                v3 = ap[:, 512:1024].rearrange("p (b n) -> p b n", b=4)
                for blk in range(4):
                    ks = slice(blk * 128, (blk + 1) * 128)
                    diag = (j == ti)
                    c0 = blk * 128 if diag else 0
                    block(h, bo, bs, c0, T - c0, 128, ap[:, ks], lambda rp, ks=ks: ap[rp, 1024 + ks.start:1024 + ks.stop], v3[:, blk, :], res,
                          first=(j == 0 and blk == 0), last=(j == ti and blk == 3), mask64=diag)
            finish_head(h, bo, bs, 0, T)
    else:
        for s in range(c.NSAMP):
            sc = slice(s * c.TS, (s + 1) * c.TS)
            _mla_cache_kv(m, s, PD0, A0, kst, vst)
            jb = c.NT + 1 + (s % 2) * PJ
            nres = m.kvs.R(c.NT)
            fw.dma("sp", snk[:], m.kvs[:, c.NT, :, s * c.TS:(s + 1) * c.TS].rearrange("h p n -> p h n"), reads=nres, writes=snk.R())
            p0, vb = (s * c.TS) % 128, (s * c.TS) // 128
            fw.dma("sp", snv[:], m.kvs[:, c.NT, p0:p0 + c.TS, 512 + vb * 128:512 + (vb + 1) * 128].rearrange("h p n -> p h n"), reads=nres, writes=snv.R())
            fw.dma("sp", snr[:], m.kvs[0, c.NT, :, 1024 + s * c.TS:1024 + (s + 1) * c.TS], reads=nres, writes=snr.R())
            for h in range(MH):
                bo, bs = m.psum(hold=True), m.psum(hold=True)
                for jj in range(PJ):
                    ap, res = chunk(h, jb + jj)
                    v3 = ap[:, 512:1024].rearrange("p (b n) -> p b n", b=4)
                    for blk in range(4):
                        ks = slice(blk * 128, (blk + 1) * 128)
                        block(h, bo, bs, s * c.TS, c.TS, 128, ap[:, ks], lambda rp, ks=ks: ap[rp, 1024 + ks.start:1024 + ks.stop], v3[:, blk, :], res,
                              first=(jj == 0 and blk == 0), last=False)
                block(h, bo, bs, s * c.TS, c.TS, c.TS, snk[:, h, :], lambda rp: snr[rp, :], snv[:, h, :], snk.R() + snv.R() + snr.R(),
                      first=False, last=True)
                finish_head(h, bo, bs, s * c.TS, c.TS)

    def evo(mi, b):
        fw.op("dve", lambda v: v.tensor_tensor(m.x[:, mi, :], m.ps[:, b, :], m.x[:, mi, :], op=ALU.add),
              reads=m.ps.R(b) + m.x.R(mi), writes=m.x.R(mi))
    m.linear_fm(m.W["wcout"], lambda k: qnT[:, k, :], T, evo, qnT.R())


def _mla_cache_kv(m, s, PD0, A0, kst, vst):
    c, fw = m.c, m.fw
    KVLC, MH = c.KVL // 128, c.MH
    PJ = c.PAST // T
    o = PD0
    def alloc(shape, dt, cb=None):
        nonlocal o
        v = m.V(o, shape, dt, cb)
        o += (v.nbytes + 1023) // 1024 * 1024
        return v
    cin = alloc([128, 4, c.KVL], F32)
    cinb = alloc([128, 4, c.KVL], BF16)
    cT = alloc([128, KVLC, T], BF16, T * 2)
    pin = alloc([128, 4, 64], F32)
    pinb = alloc([128, 4, 128], BF16)
    prT = alloc([128, T], BF16)
    assert o <= A0, (o, A0)
    for jj in range(PJ):
        j = c.NT + 1 + (s % 2) * PJ + jj
        fw.dma("sp", cin[:], m.c_ckv[s, jj * T:(jj + 1) * T, :].rearrange("(b p) n -> p b n", p=128), writes=cin.R())
        fw.dma("sp", pin[:], m.c_kpe[s, jj * T:(jj + 1) * T, :].rearrange("(b p) n -> p b n", p=128), writes=pin.R())
        m.copy("dve", cinb[:], cin[:], cin.R(), cinb.R())
        m.copy("act", pinb[:, :, 0:64], pin[:], pin.R(), pinb.R())
        m.copy("act", pinb[:, :, 64:128], pin[:], pin.R(), pinb.R())
        for cc in range(KVLC):
            b = m.psum()
            psb = m.ps[:, b, :].bitcast(BF16)
            def tr(t, cc=cc, psb=psb):
                for blk in range(4):
                    ins = t.transpose(psb[:, blk * 128:(blk + 1) * 128], cinb[:, blk, cc * 128:(cc + 1) * 128], m.identb[:])
                return ins
            fw.op("pe", tr, reads=cinb.R() + m.identb.R(), writes=m.ps.R(b))
            m.copy("act" if cc % 2 else "dve", cT[:, cc, :], psb[:, 0:T], m.ps.R(b), cT.R(cc))
        b = m.psum()
        psb = m.ps[:, b, :].bitcast(BF16)
        def trp(t, psb=psb):
            for blk in range(4):
                ins = t.transpose(psb[:, blk * 128:(blk + 1) * 128], pinb[:, blk, :], m.identb[:])
            return ins
        fw.op("pe", trp, reads=pinb.R() + m.identb.R(), writes=m.ps.R(b))
        m.copy("dve", prT[:], psb[:, 0:T], m.ps.R(b), prT.R())
        def ev_k(mi, b):
            m.copy("act" if mi % 2 else "dve", kst[:, mi, :], m.ps[:, b, :], m.ps.R(b), kst.R(mi))
        m.linear_fm(m.W["wuk"], lambda k: cT[:, k, :], T, ev_k, cT.R())
        def ev_v(blk, c0, n, b):
            m.copy("act" if blk % 2 else "dve", vst[:, blk, c0:c0 + n], m.ps[:, b, 0:n], m.ps.R(b), vst.R(blk))
        m.linear_tm(m.W["wuv"], lambda k, blk: cT[:, k, blk * 128:(blk + 1) * 128], 4, ev_v, cT.R())
        kres = m.kvs.R(j)
        fw.dma("sp", m.kvs[:, j, :, 0:512].rearrange("h p n -> p h n"), kst[:], reads=kst.R(), writes=kres)
        for blk in range(4):
            fw.dma("sp", m.kvs[:, j, :, 512 + blk * 128:512 + (blk + 1) * 128].rearrange("h p n -> p h n"),
                   vst[:, blk, :].rearrange("p (h n) -> p h n", h=MH), reads=vst.R(blk), writes=kres)
        for h in range(MH):
            fw.dma("sp", m.kvs[h, j, :, 1024:1536], prT[:], reads=prT.R(), writes=kres)
    m.stream.barrier()


def _ssdret(m, ti, sample):
    c, fw = m.c, m.fw
    D, KC, G, HPG, SH, RH = c.D, c.KC, c.SG, c.HPG, c.SH, c.RH
    GW = HPG * 64
    GB = GW // 128
    last_tile = (ti == c.NT - 1)
    if sample:
        L, NCH, NSEG, SL = c.TS, c.NSAMP, c.NSAMP, c.TS
    else:
        L, NCH, NSEG, SL = 64, T // 64, 1, T
    SEGW = SL + 3
    chunks = [(ch * L, L) for ch in range(NCH)]
    li = 1 if sample else 0
    o = 0
    def alloc(shape, dt, cb=None):
        nonlocal o
        v = m.V(o, shape, dt, cb)
        o += (v.nbytes + 1023) // 1024 * 1024
        return v
    sq = alloc([128, 2, T], BF16, T * 2)
    rstd = alloc([128, T], F32)
    YO = alloc([128, KC, T], BF16, T * 2)
    wdt = alloc([128, KC, SH], BF16)
    dt = alloc([64, NCH, SH], F32)
    cum = alloc([64, NCH, SH], F32)
    ecum = alloc([64, NCH, SH], F32)
    rdec = alloc([64, RH * 64 + 3 * RH], F32)
    P0 = o
    zs = alloc([128, GB, T], BF16, T * 2)
    raw = alloc([128, GB + 2, NSEG * SEGW], F32, NSEG * SEGW * 4)
    xbc = alloc([128, GB + 2, T], BF16, T * 2)
    yz = alloc([128, GB, T], F32, T * 4)
    Xd = alloc([64, HPG, 64], F32)
    seg = alloc([64, HPG, 64], F32)
    MT = alloc([64, HPG, 64], BF16)
    tok = alloc([64, GW + 128], BF16)
    xdt = alloc([64, HPG, 64], BF16)
    xdtw = alloc([64, HPG, 64], BF16)
    tt = alloc([64, HPG, 64], F32)
    uu = alloc([64, HPG, 64], F32)
    ytok = alloc([64, GW], BF16)
    wv_ = alloc([64, 2, HPG], F32)
    hst = alloc([128, HPG, 64], F32)
    hbf = alloc([128, HPG, 64], BF16)
    halo = alloc([128, 128], F32)
    cso = alloc([128, 128], F32)
    assert o <= m.ARENA_BYTES

    m.norm_fm(m.x, 1, m.xn, T, KC, sq, rstd)
    fw.dma("sp", rdec[:], m.ret_dec[li], writes=rdec.R())
    wd, wres, _, _ = m.wchunk(m.W["wdt"], 0)
    m.copy("act", wdt[:], wd, wres, wdt.R())
    CPB = 512 // SH
    for c0 in range(0, NCH, CPB):
        nch = min(CPB, NCH - c0)
        b = m.psum()
        def mm(t, c0=c0, nch=nch, b=b):
            for ch in range(nch):
                col, _ = chunks[c0 + ch]
                for k in range(KC):
                    ins = t.matmul(m.ps[0:L, b, ch * SH:(ch + 1) * SH], m.xn[:, k, col:col + L], wdt[:, k, :], start=(k == 0), stop=(k == KC - 1))
            return ins
        fw.op("pe", mm, reads=m.xn.R() + wdt.R(), writes=m.ps.R(b))
        dsl = dt[0:L, c0:c0 + nch, :]
        csl = cum[0:L, c0:c0 + nch, :]
        esl = ecum[0:L, c0:c0 + nch, :]
        bia = m.hb3[0:L, 0:1, :].to_broadcast([L, nch, SH])
        fw.op("dve", lambda v: v.tensor_tensor(dsl, m.ps[0:L, b, 0:nch * SH].rearrange("p (c h) -> p c h", h=SH), bia, op=ALU.add),
              reads=m.ps.R(b) + m.hb3.R(), writes=dt.R())
        fw.op("act", lambda a: a.activation(csl, dsl, AF.Abs), reads=dt.R(), writes=cum.R())
        fw.op("act", lambda a: a.activation(csl, csl, AF.Exp, scale=-1.0), reads=cum.R(), writes=cum.R())
        fw.op("act", lambda a: a.activation(csl, csl, AF.Ln, bias=1.0), reads=cum.R(), writes=cum.R())
        fw.op("dve", lambda v: v.scalar_tensor_tensor(dsl, dsl, 0.0, csl, op0=ALU.max, op1=ALU.add), reads=dt.R() + cum.R(), writes=dt.R())
        aa = m.hb3[0:L, 1:2, :].to_broadcast([L, nch, SH])
        fw.op("dve", lambda v: v.tensor_tensor(esl, dsl, aa, op=ALU.mult), reads=dt.R() + m.hb3.R(), writes=ecum.R())
        b2 = m.psum()
        fw.op("pe", lambda t: t.matmul(m.ps[0:L, b2, 0:nch * SH], m.tri[0:L, 0:L], ecum[0:L, c0:c0 + nch, :].rearrange("p c h -> p (c h)"), start=True, stop=True),
              reads=ecum.R() + m.tri.R(), writes=m.ps.R(b2))
        m.copy("dve", csl, m.ps[0:L, b2, 0:nch * SH].rearrange("p (c h) -> p c h", h=SH), m.ps.R(b2), cum.R())
        fw.op("act", lambda a: a.activation(esl, csl, AF.Exp), reads=cum.R(), writes=ecum.R())

    raw4 = raw.ap.rearrange("p k (s w) -> p k s w", w=SEGW)
    for g in range(G):
        blks = [g * GB + i for i in range(GB)] + [KC + g, KC + G + g]
        for bi, gb_ in enumerate(blks):
            if sample:
                fw.dma("sp", halo[0:NSEG * 3, :], m.st_conv[:, gb_ * 128:(gb_ + 1) * 128], writes=halo.R())
                b = m.psum()
                fw.op("pe", lambda t: t.transpose(m.ps[:, b, 0:NSEG * 3], halo[0:NSEG * 3, :], m.ident[0:NSEG * 3, 0:NSEG * 3]),
                      reads=halo.R() + m.ident.R(), writes=m.ps.R(b))
                m.copy("dve", raw4[:, bi, :, 0:3], m.ps[:, b, 0:NSEG * 3].rearrange("p (s w) -> p s w", w=3), m.ps.R(b), raw.R(bi))
            else:
                m.copy("dve", raw4[:, bi, 0, 0:3], m.convst[:, gb_, :], m.convst.R(), raw.R(bi))
        def ev(mi, b):
            if mi < GB:
                fw.op("act", lambda a: a.activation(zs[:, mi, :], m.ps[:, b, :], AF.Silu), reads=m.ps.R(b), writes=zs.R(mi))
            else:
                bi = mi - GB
                m.copy("dve", raw4[:, bi, :, 3:3 + SL], m.ps[:, b, :].rearrange("p (s w) -> p s w", w=SL), m.ps.R(b), raw.R(bi))
        m.linear_fm(m.W[f"wssd{g}"], lambda k: m.xn[:, k, :], T, ev, m.xn.R())
        for bi, gb_ in enumerate(blks):
            if sample:
                fw.op("act", lambda a: a.activation(cso[:, 0:NSEG * 3].rearrange("p (s w) -> p s w", w=3), raw4[:, bi, :, SL:SL + 3], AF.Copy),
                      reads=raw.R(bi), writes=cso.R())
                b = m.psum()
                fw.op("pe", lambda t: t.transpose(m.ps[0:NSEG * 3, b, 0:128], cso[:, 0:NSEG * 3], m.ident[:]),
                      reads=cso.R() + m.ident.R(), writes=m.ps.R(b))
                m.copy("dve", halo[0:NSEG * 3, :], m.ps[0:NSEG * 3, b, 0:128], m.ps.R(b), halo.R())
                fw.dma("sp", m.conv_s[:, gb_ * 128:(gb_ + 1) * 128], halo[0:NSEG * 3, :], reads=halo.R(), writes=m.conv_s.R(), is_output=True)
            else:
                m.copy("act", m.convst[:, gb_, :], raw4[:, bi, 0, SL:SL + 3], raw.R(bi), m.convst.R())
            acc = yz[:, 0, :].rearrange("p (s w) -> p s w", w=SL)
            fw.op("dve", lambda v: v.tensor_scalar(acc, raw4[:, bi, :, 0:SL], m.cw[:, gb_, 0:1], m.cb[:, gb_:gb_ + 1], op0=ALU.mult, op1=ALU.add),
                  reads=raw.R(bi) + m.cw.R() + m.cb.R(), writes=yz.R(0))
            for j in range(1, 4):
                fw.op("dve", lambda v, j=j: v.scalar_tensor_tensor(acc, raw4[:, bi, :, j:j + SL], m.cw[:, gb_, j:j + 1], acc, op0=ALU.mult, op1=ALU.add),
                      reads=raw.R(bi) + m.cw.R() + yz.R(0), writes=yz.R(0))
            fw.op("act", lambda a: a.activation(xbc[:, bi, :], yz[:, 0, :], AF.Silu), reads=yz.R(0), writes=xbc.R(bi))
        hsrc = m.st_ssd if sample else m.ssd_st
        def load_state(sidx):
            if sample:
                fw.dma("sp", hst[:], m.st_ssd[sidx, g * HPG:(g + 1) * HPG].rearrange("h n p -> n h p"), writes=hst.R())
            elif ti == 0:
                fw.op("pool", lambda g_: g_.memset(hst[:], 0.0), writes=hst.R())
            else:
                fw.dma("sp", hst[:], m.ssd_st[g * HPG:(g + 1) * HPG].rearrange("h n p -> n h p"), reads=m.ssd_st.R(), writes=hst.R())
            m.copy("act", hbf[:], hst[:], hst.R(), hbf.R())
        def store_state(sidx):
            if sample:
                fw.dma("sp", m.ssd_s[sidx, g * HPG:(g + 1) * HPG].rearrange("h n p -> n h p"), hst[:], reads=hst.R(), writes=m.ssd_s.R(), is_output=True)
            else:
                fw.dma("sp", m.ssd_st[g * HPG:(g + 1) * HPG].rearrange("h n p -> n h p"), hst[:], reads=hst.R(), writes=m.ssd_st.R())
                if last_tile:
                    fw.dma("sp", m.ssd_p[g * HPG:(g + 1) * HPG].rearrange("h n p -> n h p"), hst[:], reads=hst.R(), writes=m.ssd_p.R(), is_output=True)
        if not sample:
            load_state(0)
        hs = slice(g * HPG, (g + 1) * HPG)
        for ch, (col, _) in enumerate(chunks):
            cs = slice(col, col + L)
            if sample:
                load_state(ch)
            b = m.psum()
            psb = m.ps[:, b, :].bitcast(BF16)
            def tr(t, psb=psb, cs=cs):
                for i in range(GB + 1):
                    ins = t.transpose(psb[0:L, i * 128:(i + 1) * 128], xbc[:, i, cs], m.identb[:])
                return ins
            fw.op("pe", tr, reads=xbc.R() + m.identb.R(), writes=m.ps.R(b))
            m.copy("act", tok[0:L, :], psb[0:L, 0:GW + 128], m.ps.R(b), tok.R())
            xs3 = tok[0:L, 0:GW].rearrange("p (h d) -> p h d", d=64)
            fw.op("dve", lambda v: v.tensor_tensor(Xd[0:L, :, 0:L], cum[0:L, ch, hs].unsqueeze(2).to_broadcast([L, HPG, L]),
                                                   m.ident[0:L, 0:L].unsqueeze(1).to_broadcast([L, HPG, L]), op=ALU.mult),
                  reads=cum.R() + m.ident.R(), writes=Xd.R())
            bB = m.psum()
            def mmB(t):
                for h in range(HPG):
                    ins = t.matmul(m.ps[0:L, bB, h * 64:h * 64 + L], m.onesf[0:L, 0:L], Xd[0:L, h, 0:L], start=True, stop=True)
                return ins
            fw.op("pe", mmB, reads=Xd.R() + m.onesf.R(), writes=m.ps.R(bB))
            cumB = m.ps[0:L, bB, :].rearrange("p (h l) -> p h l", l=64)[:, 0:HPG, 0:L]
            fw.op("dve", lambda v: v.tensor_tensor(seg[0:L, :, 0:L], cumB, cum[0:L, ch, hs].unsqueeze(2).to_broadcast([L, HPG, L]), op=ALU.subtract),
                  reads=m.ps.R(bB) + cum.R(), writes=seg.R())
            fw.op("dve", lambda v: v.tensor_tensor(wv_[0:L, 0, :], cumB[:, :, L - 1], cum[0:L, ch, hs], op=ALU.subtract),
                  reads=m.ps.R(bB) + cum.R(), writes=wv_.R())
            fw.op("act", lambda a: a.activation(wv_[0:L, 0, :], wv_[0:L, 0, :], AF.Exp), reads=wv_.R(), writes=wv_.R())
            fw.op("pool", lambda g_: g_.tensor_tensor(seg[0:L, :, 0:L], seg[0:L, :, 0:L], m.negm[0:L, 0:L].unsqueeze(1).to_broadcast([L, HPG, L]), op=ALU.add),
                  reads=seg.R() + m.negm.R(), writes=seg.R())
            fw.op("act", lambda a: a.activation(seg[0:L, :, 0:L], seg[0:L, :, 0:L], AF.Exp), reads=seg.R(), writes=seg.R())
            bq = m.psum()
            fw.op("pe", lambda t: t.matmul(m.ps[0:L, bq, 0:L], xbc[:, GB, cs], xbc[:, GB + 1, cs], start=True, stop=True), reads=xbc.R(GB, GB + 1), writes=m.ps.R(bq))
            fw.op("dve", lambda v: v.tensor_tensor(MT[0:L, :, 0:L], seg[0:L, :, 0:L], m.ps[0:L, bq, 0:L].unsqueeze(1).to_broadcast([L, HPG, L]), op=ALU.mult),
                  reads=seg.R() + m.ps.R(bq), writes=MT.R())
            fw.op("dve", lambda v: v.tensor_tensor(xdt[0:L], xs3, dt[0:L, ch, hs].unsqueeze(2).to_broadcast([L, HPG, 64]), op=ALU.mult),
                  reads=tok.R() + dt.R(), writes=xdt.R())
            fw.op("pool", lambda g_: g_.tensor_tensor(xdtw[0:L], xdt[0:L], wv_[0:L, 0, :].unsqueeze(2).to_broadcast([L, HPG, 64]), op=ALU.mult),
                  reads=xdt.R() + wv_.R(), writes=xdtw.R())
            bi_, be_ = m.psum(), m.psum()
            def mmy(t):
                for h in range(HPG):
                    t.matmul(m.ps[0:L, bi_, h * 64:(h + 1) * 64], MT[0:L, h, 0:L], xdt[0:L, h, :], start=True, stop=True)
                return t.matmul(m.ps[0:L, be_, 0:GW], xbc[:, GB + 1, cs], hbf[:].rearrange("p h d -> p (h d)"), start=True, stop=True)
            fw.op("pe", mmy, reads=MT.R() + xdt.R() + xbc.R(GB + 1) + hbf.R(), writes=m.ps.R(bi_) + m.ps.R(be_))
            fw.op("dve", lambda v: v.tensor_tensor(tt[0:L], m.ps[0:L, be_, 0:GW].rearrange("p (h d) -> p h d", d=64),
                                                   ecum[0:L, ch, hs].unsqueeze(2).to_broadcast([L, HPG, 64]), op=ALU.mult),
                  reads=m.ps.R(be_) + ecum.R(), writes=tt.R())
            fw.op("dve", lambda v: v.tensor_tensor(tt[0:L], tt[0:L], m.ps[0:L, bi_, 0:GW].rearrange("p (h d) -> p h d", d=64), op=ALU.add),
                  reads=m.ps.R(bi_) + tt.R(), writes=tt.R())
            fw.op("pool", lambda g_: g_.tensor_tensor(uu[0:L], xs3, m.hb3[0:L, 2, hs].unsqueeze(2).to_broadcast([L, HPG, 64]), op=ALU.mult),
                  reads=tok.R() + m.hb3.R(), writes=uu.R())
            fw.op("dve", lambda v: v.tensor_tensor(ytok[0:L, :].rearrange("p (h d) -> p h d", d=64), tt[0:L], uu[0:L], op=ALU.add),
                  reads=tt.R() + uu.R(), writes=ytok.R())
            bu = m.psum()
            fw.op("pe", lambda t: t.matmul(m.ps[:, bu, 0:GW], tok[0:L, GW:GW + 128], xdtw[0:L].rearrange("p h d -> p (h d)"), start=True, stop=True),
                  reads=tok.R() + xdtw.R(), writes=m.ps.R(bu))
            bl = m.psum()
            fw.op("pe", lambda t: t.matmul(m.ps[:, bl, 0:HPG], m.onesf[0:L, :], Xd[0:L, :, L - 1], start=True, stop=True),
                  reads=Xd.R() + m.onesf.R(), writes=m.ps.R(bl))
            fw.op("act", lambda a: a.activation(wv_[:, 1, :] if False else rstd[:, 0:HPG], m.ps[:, bl, 0:HPG], AF.Exp), reads=m.ps.R(bl), writes=rstd.R())
            fw.op("dve", lambda v: v.tensor_tensor(hst[:], hst[:], rstd[:, 0:HPG].unsqueeze(2).to_broadcast([128, HPG, 64]), op=ALU.mult),
                  reads=hst.R() + rstd.R(), writes=hst.R())
            fw.op("dve", lambda v: v.tensor_tensor(hst[:], hst[:], m.ps[:, bu, 0:GW].rearrange("p (h d) -> p h d", d=64), op=ALU.add),
                  reads=hst.R() + m.ps.R(bu), writes=hst.R())
            m.copy("act", hbf[:], hst[:], hst.R(), hbf.R())
            if sample:
                store_state(ch)
            b = m.psum()
            psb = m.ps[:, b, :].bitcast(BF16)
            def tr2(t, psb=psb):
                for i in range(GB):
                    ins = t.transpose(psb[:, i * 64:i * 64 + L], ytok[0:L, i * 128:(i + 1) * 128], m.identb[0:L, 0:L])
                return ins
            fw.op("pe", tr2, reads=ytok.R() + m.identb.R(), writes=m.ps.R(b))
            fw.op("dve", lambda v: v.tensor_tensor(yz[:, :, cs], psb[:, 0:GB * 64].rearrange("p (i l) -> p i l", l=64)[:, :, 0:L], zs[:, :, cs], op=ALU.mult),
                  reads=m.ps.R(b) + zs.R(), writes=yz.R())
        if not sample:
            store_state(0)
        m.norm_fm(yz, 11, YO, T, GB, sq, rstd, gk0=g * GB, dk0=g * GB)

    def evo(mi, b):
        fw.op("dve", lambda v: v.tensor_tensor(m.x[:, mi, :], m.ps[:, b, :], m.x[:, mi, :], op=ALU.add),
              reads=m.ps.R(b) + m.x.R(mi), writes=m.x.R(mi))
    m.linear_fm(m.W["waboy"], lambda k: YO[:, k, :], T, evo, YO.R())

    o = P0
    qkraw = alloc([128, 8, T], F32, T * 4)
    qT = alloc([128, 4, T], BF16, T * 2)
    kT = alloc([128, 4, T], BF16, T * 2)
    vT = alloc([128, 4, T], BF16, T * 2)
    gs = alloc([128, 4, T], BF16, T * 2)
    rt = alloc([128, 2, T], F32, T * 4)
    t1 = alloc([128, T], F32)
    kvt = alloc([64, 512], BF16)
    MTr = alloc([64, 64], BF16)
    isb = alloc([64, 256], F32)
    osb = alloc([64, 256], F32)
    onb = alloc([64, 256], BF16)
    vw = alloc([64, 256], BF16)
    sm = alloc([64, 4], F32)
    hr = alloc([128, 2, 256], F32)
    hrb = alloc([128, 2, 256], BF16)
    assert o <= m.ARENA_BYTES
    col0 = c.SEQ if sample else ti * T
    fw.dma("sp", rt[:], m.rope_ret[:, :, col0:col0 + T].rearrange("a p n -> p a n"), writes=rt.R())
    lg = [math.log1p(-2.0 ** (-5.0 - h)) for h in range(RH)]
    for hp in range(RH // 2):
        def ev(mi, b):
            if mi < 8:
                if mi < 4:
                    m.copy("dve", qkraw[:, mi, :], m.ps[:, b, :], m.ps.R(b), qkraw.R(mi))
                else:
                    fw.op("act", lambda a: a.activation(qkraw[:, mi, :], m.ps[:, b, :], AF.Copy, scale=256.0 ** -0.5), reads=m.ps.R(b), writes=qkraw.R(mi))
            elif mi < 12:
                m.copy("act", vT[:, mi - 8, :], m.ps[:, b, :], m.ps.R(b), vT.R(mi - 8))
            else:
                fw.op("act", lambda a: a.activation(gs[:, mi - 12, :], m.ps[:, b, :], AF.Silu), reads=m.ps.R(b), writes=gs.R(mi - 12))
        m.linear_fm(m.W[f"wret{hp}"], lambda k: m.xn[:, k, :], T, ev, m.xn.R())
        for qi, dstT in ((0, qT), (4, kT)):
            for hh in range(2):
                x1, x2 = qkraw[:, qi + 2 * hh, :], qkraw[:, qi + 2 * hh + 1, :]
                r12 = qkraw.R(qi + 2 * hh, qi + 2 * hh + 1)
                fw.op("dve", lambda v: v.tensor_tensor(t1[:], x2, rt[:, 1, :], op=ALU.mult), reads=r12 + rt.R(), writes=t1.R())
                fw.op("pool", lambda g_: g_.tensor_tensor(rstd[:], x1, rt[:, 0, :], op=ALU.mult), reads=r12 + rt.R(), writes=rstd.R())
                fw.op("dve", lambda v: v.tensor_tensor(dstT[:, 2 * hh, :], rstd[:], t1[:], op=ALU.subtract), reads=rstd.R() + t1.R(), writes=dstT.R(2 * hh))
                fw.op("dve", lambda v: v.tensor_tensor(t1[:], x1, rt[:, 1, :], op=ALU.mult), reads=r12 + rt.R(), writes=t1.R())
                fw.op("pool", lambda g_: g_.tensor_tensor(rstd[:], x2, rt[:, 0, :], op=ALU.mult), reads=r12 + rt.R(), writes=rstd.R())
                fw.op("dve", lambda v: v.tensor_tensor(dstT[:, 2 * hh + 1, :], rstd[:], t1[:], op=ALU.add), reads=rstd.R() + t1.R(), writes=dstT.R(2 * hh + 1))
        for hh in range(2):
            h = hp * 2 + hh
            gL = math.exp(lg[h] * L)
            def load_state(sidx):
                if sample:
                    for e in range(2):
                        fw.dma("sp", hr[:, e, :], m.st_ret[sidx, h, e * 128:(e + 1) * 128, :], writes=hr.R())
                elif ti == 0:
                    fw.op("pool", lambda g_: g_.memset(hr[:], 0.0), writes=hr.R())
                else:
                    for e in range(2):
                        fw.dma("sp", hr[:, e, :], m.ret_st[h, e * 128:(e + 1) * 128, :], reads=m.ret_st.R(), writes=hr.R())
                m.copy("act", hrb[:], hr[:], hr.R(), hrb.R())
            def store_state(sidx):
                for e in range(2):
                    if sample:
                        fw.dma("sp", m.ret_s[sidx, h, e * 128:(e + 1) * 128, :], hr[:, e, :], reads=hr.R(), writes=m.ret_s.R(), is_output=True)
                    else:
                        fw.dma("sp", m.ret_st[h, e * 128:(e + 1) * 128, :], hr[:, e, :], reads=hr.R(), writes=m.ret_st.R())
                        if last_tile:
                            fw.dma("sp", m.ret_p[h, e * 128:(e + 1) * 128, :], hr[:, e, :], reads=hr.R(), writes=m.ret_p.R(), is_output=True)
            if not sample:
                load_state(0)
            Dm = rdec[0:L, h * 64:h * 64 + L]
            dfs = rdec[0:L, RH * 64 + h:RH * 64 + h + 1]
            dte = rdec[0:L, RH * 64 + RH + h:RH * 64 + RH + h + 1]
            for ch, (col, _) in enumerate(chunks):
                cs = slice(col, col + L)
                if sample:
                    load_state(ch)
                b = m.psum()
                psb = m.ps[:, b, :].bitcast(BF16)
                def tr(t, psb=psb, cs=cs):
                    for e in range(2):
                        t.transpose(psb[0:L, e * 128:(e + 1) * 128], kT[:, 2 * hh + e, cs], m.identb[:])
                        ins = t.transpose(psb[0:L, 256 + e * 128:256 + (e + 1) * 128], vT[:, 2 * hh + e, cs], m.identb[:])
                    return ins
                fw.op("pe", tr, reads=kT.R(2 * hh, 2 * hh + 1) + vT.R(2 * hh, 2 * hh + 1) + m.identb.R(), writes=m.ps.R(b))
                m.copy("act", kvt[0:L, :], psb[0:L, 0:512], m.ps.R(b), kvt.R())
                bq = m.psum()
                def mq(t):
                    t.matmul(m.ps[0:L, bq, 0:L], kT[:, 2 * hh, cs], qT[:, 2 * hh, cs], start=True, stop=False)
                    return t.matmul(m.ps[0:L, bq, 0:L], kT[:, 2 * hh + 1, cs], qT[:, 2 * hh + 1, cs], start=False, stop=True)
                fw.op("pe", mq, reads=kT.R(2 * hh, 2 * hh + 1) + qT.R(2 * hh, 2 * hh + 1), writes=m.ps.R(bq))
                fw.op("dve", lambda v: v.tensor_tensor(MTr[0:L, 0:L], m.ps[0:L, bq, 0:L], Dm, op=ALU.mult), reads=m.ps.R(bq) + rdec.R(), writes=MTr.R())
                bi_, be_ = m.psum(), m.psum()
                def mo(t):
                    t.matmul(m.ps[0:L, bi_, 0:256], MTr[0:L, 0:L], kvt[0:L, 256:512], start=True, stop=True)
                    t.matmul(m.ps[0:L, be_, 0:256], qT[:, 2 * hh, cs], hrb[:, 0, :], start=True, stop=False)
                    return t.matmul(m.ps[0:L, be_, 0:256], qT[:, 2 * hh + 1, cs], hrb[:, 1, :], start=False, stop=True)
                fw.op("pe", mo, reads=MTr.R() + kvt.R() + qT.R(2 * hh, 2 * hh + 1) + hrb.R(), writes=m.ps.R(bi_) + m.ps.R(be_))
                m.copy("act", isb[0:L, :], m.ps[0:L, bi_, 0:256], m.ps.R(bi_), isb.R())
                fw.op("dve", lambda v: v.scalar_tensor_tensor(osb[0:L, :], m.ps[0:L, be_, 0:256], dfs, isb[0:L, :], op0=ALU.mult, op1=ALU.add),
                      reads=m.ps.R(be_) + rdec.R() + isb.R(), writes=osb.R())
                fw.op("act", lambda a: a.activation(isb[0:L, :], osb[0:L, :], AF.Square, accum_out=sm[0:L, 0:1]), reads=osb.R(), writes=isb.R() + sm.R())
                fw.op("act", lambda a: a.activation(sm[0:L, 1:2], sm[0:L, 0:1], AF.Sqrt, bias=m.epsb[0:L, 0:1], scale=1.0 / 256), reads=sm.R() + m.epsb.R(), writes=sm.R())
                fw.op("dve", lambda v: v.reciprocal(sm[0:L, 2:3], sm[0:L, 1:2]), reads=sm.R(), writes=sm.R())
                fw.op("dve", lambda v: v.tensor_scalar(onb[0:L, :], osb[0:L, :], sm[0:L, 2:3], None, op0=ALU.mult), reads=osb.R() + sm.R(), writes=onb.R())
                fw.op("pool", lambda g_: g_.tensor_scalar(vw[0:L, :], kvt[0:L, 256:512], dte, None, op0=ALU.mult), reads=kvt.R() + rdec.R(), writes=vw.R())
                bu = m.psum()
                def mu(t):
                    t.matmul(m.ps[:, bu, 0:256], kvt[0:L, 0:128], vw[0:L, :], start=True, stop=True)
                    return t.matmul(m.ps[:, bu, 256:512], kvt[0:L, 128:256], vw[0:L, :], start=True, stop=True)
                fw.op("pe", mu, reads=kvt.R() + vw.R(), writes=m.ps.R(bu))
                fw.op("dve", lambda v: v.scalar_tensor_tensor(hr[:], hr[:], gL, m.ps[:, bu, :].rearrange("p (e n) -> p e n", e=2), op0=ALU.mult, op1=ALU.add),
                      reads=hr.R() + m.ps.R(bu), writes=hr.R())
                m.copy("act", hrb[:], hr[:], hr.R(), hrb.R())
                if sample:
                    store_state(ch)
                b = m.psum()
                psb = m.ps[:, b, :].bitcast(BF16)
                def tr2(t, psb=psb):
                    for e in range(2):
                        ins = t.transpose(psb[:, e * 64:e * 64 + L], onb[0:L, e * 128:(e + 1) * 128], m.identb[0:L, 0:L])
                    return ins
                fw.op("pe", tr2, reads=onb.R() + m.identb.R(), writes=m.ps.R(b))
                fw.op("dve", lambda v: v.tensor_tensor(YO[:, h * 2:h * 2 + 2, cs], psb[:, 0:128].rearrange("p (i l) -> p i l", l=64)[:, :, 0:L], gs[:, 2 * hh:2 * hh + 2, cs], op=ALU.mult),
                      reads=m.ps.R(b) + gs.R(2 * hh, 2 * hh + 1), writes=YO.R(h * 2, h * 2 + 1))
            if not sample:
                store_state(0)
    m.linear_fm(m.W["waboo"], lambda k: YO[:, k, :], T, evo, YO.R())
    if last_tile and not sample:
        b = m.psum()
        NB = c.CONV // 128
        fw.op("pe", lambda t: t.transpose(m.ps[0:NB * 3, b, 0:128], m.convst[:].rearrange("p b w -> p (b w)"), m.ident[:]),
              reads=m.convst.R() + m.ident.R(), writes=m.ps.R(b))
        m.copy("dve", halo[0:NB * 3, :], m.ps[0:NB * 3, b, 0:128], m.ps.R(b), halo.R())
        for bb in range(NB):
            fw.dma("sp", m.conv_p[:, bb * 128:(bb + 1) * 128], halo[bb * 3:(bb + 1) * 3, :], reads=halo.R(), writes=m.conv_p.R(), is_output=True)


def _mixer(m, l, ti, sample):
    if l == 1 and ("mix" in m.stages or "mix1" in m.stages):
        _mla(m, ti, sample)
    if l == 0 and ("mix" in m.stages or "mix0" in m.stages):
        _ssdret(m, ti, sample)


Model.mixer = _mixer
```

```python
import math
import numpy as np
import concourse.bass as bass
import concourse.mybir as mybir
from concourse.bass_utils import run_bass_kernel_spmd

F32 = mybir.dt.float32
BF16 = mybir.dt.bfloat16
AF = mybir.ActivationFunctionType
ALU = mybir.AluOpType
AX = mybir.AxisListType
T = 512
EPS = 1e-6


class Cfg:
    def __init__(s, D=2048, DFF=5632, SEQ=8192, NSAMP=32, TS=16, PAST=2048, MEM=256,
                 SSD_G=4, QL=512, KVL=512, DEPTH=2, NSL=None):
        s.D, s.DFF, s.SEQ, s.NSAMP, s.TS, s.PAST, s.MEM = D, DFF, SEQ, NSAMP, TS, PAST, MEM
        s.DEPTH = DEPTH
        s.NSL = NSL if NSL is not None else NSAMP // 8
        s.KC = D // 128
        s.FC = DFF // 128
        s.NT = SEQ // T
        assert NSAMP * TS == T
        s.SH = D // 64; s.SG = SSD_G; s.HPG = s.SH // SSD_G; s.SN = 128; s.SP = 64
        s.CONV = D + 2 * SSD_G * 128
        s.RH = D // 256
        s.AB_IN = D + s.CONV + s.SH + 4 * D
        s.MH = D // 128; s.QL = QL; s.KVL = KVL
        s.C_IN = QL + KVL + 64
        s.MHEADS = 4; s.MINNER = 512


class Res:
    __slots__ = ("w", "r")

    def __init__(s):
        s.w = None
        s.r = {}


class Buf:
    def __init__(s, t, nslots=1):
        s.t = t
        s.res = [Res() for _ in range(nslots)]

    def __getitem__(s, key):
        return s.t[key]

    def R(s, i=None):
        if i is None:
            return list(s.res)
        if isinstance(i, (list, tuple, range)):
            return [s.res[j] for j in i]
        return [s.res[i]]


class FW:
    ENG = ("pe", "act", "dve", "pool", "sp")
    NDMA = 12
    NOSELF = ("pe",)

    def __init__(s, nc):
        s.nc = nc
        s.h = {"pe": nc.tensor, "act": nc.scalar, "dve": nc.vector, "pool": nc.gpsimd, "sp": nc.sync}
        s.sem = {e: nc.alloc_semaphore("S_" + e) for e in s.ENG}
        s.cnt = {e: 0 for e in s.ENG}
        s.known = {e: {} for e in s.ENG}
        s.dsem = {q: [nc.alloc_semaphore(f"D_{q}{i}") for i in range(s.NDMA)] for q in ("sp", "pool")}
        s.dn = {"sp": 0, "pool": 0}
        s.semobj = {}
        for e in s.ENG:
            s.semobj[id(s.sem[e])] = s.sem[e]
        for q in s.dsem:
            for x in s.dsem[q]:
                s.semobj[id(x)] = x
        s.dry = False
        s.ninst = 0
        s.out_tokens = []

    def new_epoch(s):
        if s.dry:
            return
        for e in s.ENG:
            sem = s.nc.alloc_semaphore(f"S_{e}_{len(s.semobj)}")
            s.sem[e] = sem
            s.cnt[e] = 0
            s.semobj[id(sem)] = sem

    def _wait(s, e, deps):
        kn = s.known[e]
        own = id(s.sem[e])
        best = {}
        for tok in deps:
            if tok is None:
                continue
            sid, val = tok
            if sid == own and e in s.NOSELF:
                continue
            if kn.get(sid, 0) >= val:
                continue
            if best.get(sid, 0) < val:
                best[sid] = val
        for sid, val in best.items():
            s.h[e].wait_ge(s.semobj[sid], val)
            s.ninst += 1
            kn[sid] = val

    def _deps(s, reads, writes):
        deps = []
        for r in reads:
            deps.append(r.w)
        for w in writes:
            deps.append(w.w)
            for sid, val in w.r.items():
                deps.append((sid, val))
        return deps

    def _commit(s, tok, reads, writes):
        sid, val = tok
        for r in reads:
            if r.r.get(sid, 0) < val:
                r.r[sid] = val
        for w in writes:
            w.w = tok
            w.r = {}

    def op(s, e, fn, reads=(), writes=()):
        if s.dry:
            return
        reads = [r for r in reads if r is not None]
        writes = [w for w in writes if w is not None]
        s._wait(e, s._deps(reads, writes))
        ins = fn(s.h[e])
        s.cnt[e] += 1
        ins.then_inc(s.sem[e], 1)
        s.ninst += 1
        tok = (id(s.sem[e]), s.cnt[e])
        s._commit(tok, reads, writes)
        return tok

    def dma(s, q, out, in_, reads=(), writes=(), is_output=False, **kw):
        if s.dry:
            return
        reads = [r for r in reads if r is not None]
        writes = [w for w in writes if w is not None]
        n = s.dn[q]
        slot = n % s.NDMA
        use = n // s.NDMA
        sem = s.dsem[q][slot]
        deps = s._deps(reads, writes)
        if use > 0:
            deps.append((id(sem), 16 * use))
        s._wait(q, deps)
        s.h[q].dma_start(out=out, in_=in_, **kw).then_inc(sem, 16)
        s.ninst += 1
        s.dn[q] = n + 1
        tok = (id(sem), 16 * (use + 1))
        s._commit(tok, reads, writes)
        if is_output:
            s.out_tokens.append(tok)
        return tok

    def finish(s):
        if s.dry:
            return
        for q in ("sp", "pool"):
            deps = []
            for i, sem in enumerate(s.dsem[q]):
                n = s.dn[q]
                uses = (n - i + s.NDMA - 1) // s.NDMA if n > i else 0
                if uses > 0:
                    deps.append((id(sem), 16 * uses))
            s._wait("sp", deps)


class Stream:
    def __init__(s, fw, ring, nslots):
        s.fw, s.ring, s.nslots = fw, ring, nslots
        s.plan = []
        s.i = 0
        s.issued = 0

    def reset(s):
        s.i = 0
        s.issued = 0

    def barrier(s):
        if s.fw.dry:
            s.plan.append(None)
            return
        assert s.plan[s.i] is None and s.issued <= s.i, (s.i, s.issued)
        s.i += 1
        s.issued = s.i

    def get(s, src, parts, n, reads=()):
        if s.fw.dry:
            s.plan.append((src, parts, n, list(reads)))
            return s.ring[0:parts, 0, 0:n], s.ring.R(0)
        assert s.plan[s.i] is not None and s.plan[s.i][2] == n
        j = s.issued
        while j < len(s.plan) and j <= s.i + s.nslots - 1 and s.plan[j] is not None:
            psrc, pparts, pn, preads = s.plan[j]
            slot = j % s.nslots
            s.fw.dma("sp", s.ring[0:pparts, slot, 0:pn], psrc, reads=preads, writes=s.ring.R(slot))
            j += 1
        s.issued = max(s.issued, j)
        assert s.issued > s.i
        slot = s.i % s.nslots
        s.i += 1
        return s.ring[0:parts, slot, 0:n], s.ring.R(slot)


class View:
    def __init__(s, arena, off, shape, dtype, chunk_bytes=None):
        esz = 2 if dtype == BF16 else 4
        n = 1
        for d in shape[1:]:
            n *= d
        s.nbytes = n * esz
        assert off % 4 == 0 and s.nbytes % 4 == 0
        s.arena, s.off = arena, off
        ap = arena[0:shape[0], off // 4:(off + s.nbytes) // 4]
        if dtype == BF16:
            ap = ap.bitcast(BF16)
        if len(shape) == 3:
            ap = ap.rearrange("p (k n) -> p k n", k=shape[1])
        elif len(shape) == 4:
            ap = ap.rearrange("p (a b n) -> p a b n", a=shape[1], b=shape[2])
        s.ap = ap
        s.chunk_bytes = chunk_bytes if chunk_bytes else s.nbytes

    def __getitem__(s, key):
        return s.ap[key]

    def R(s, k=None, k2=None):
        if k is None:
            lo, hi = s.off, s.off + s.nbytes
        else:
            lo = s.off + k * s.chunk_bytes
            hi = s.off + ((k2 if k2 is not None else k) + 1) * s.chunk_bytes
        return s.arena.res[lo // 1024:(hi + 1023) // 1024]


class WDesc:
    pass


class Model:
    def __init__(m, cfg, stages=("ffn", "mix", "mem")):
        m.c = c = cfg
        m.stages = stages
        m.nc = nc = bass.Bass("TRN2", target_bir_lowering=False)
        m.fw = fw = FW(nc)
        m.ins = {}
        m.outs = {}
        D, KC = c.D, c.KC
        def din(name, shape):
            b = Buf(nc.dram_tensor(name, list(shape), F32, kind="ExternalInput").ap())
            m.ins[name] = b
            return b

        def dout(name, shape):
            b = Buf(nc.dram_tensor(name, list(shape), F32, kind="ExternalOutput").ap())
            m.outs[name] = b
            return b

        m.x_p = din("x_p", [c.SEQ, D]); m.x_s = din("x_s", [T, D]); m.mem_p = din("mem_p", [c.MEM, D])
        m.st_conv = din("st_conv", [c.NSAMP * 3, c.CONV])
        m.st_ssd = din("st_ssd", [c.NSL, c.SH, 128, 64])
        m.st_ret = din("st_ret", [c.NSL, c.RH, 256, 256])
        m.c_ckv = din("c_ckv", [c.NSL, c.PAST, c.KVL]); m.c_kpe = din("c_kpe", [c.NSL, c.PAST, 64])
        m.c_mk = din("c_mk", [2, c.NSL, c.MEM, 512]); m.c_mv = din("c_mv", [2, c.NSL, c.MEM, 512])
        m.norms = din("norms", [2, 4, D]); m.mem_norm = din("mem_norm", [2, D]); m.final_norm = din("final_norm", [D])
        m.ffn_w1 = din("ffn_w1", [2, 2, D, 2 * c.DFF]); m.ffn_w2 = din("ffn_w2", [2, 2, c.DFF, D])
        m.w_mq = din("w_mq", [2, D, 512]); m.w_mkv = din("w_mkv", [2, D, 1024]); m.w_mo = din("w_mo", [2, 512, D])
        m.ab_w_in = din("ab_w_in", [D, c.AB_IN]); m.ab_conv_w = din("ab_conv_w", [4, c.CONV]); m.ab_conv_b = din("ab_conv_b", [c.CONV])
        m.ab_dt_bias = din("ab_dt_bias", [c.SH]); m.ab_a_log = din("ab_a_log", [c.SH]); m.ab_d_skip = din("ab_d_skip", [c.SH])
        m.ab_ssd_norm = din("ab_ssd_norm", [D]); m.ab_w_out = din("ab_w_out", [2 * D, D])
        m.c_w_in = din("c_w_in", [D, c.C_IN]); m.c_q_norm = din("c_q_norm", [c.QL]); m.c_kv_norm = din("c_kv_norm", [c.KVL])
        m.c_w_uq = din("c_w_uq", [c.QL, c.MH * 192]); m.c_w_uk = din("c_w_uk", [c.KVL, c.MH * 128])
        m.c_w_uv = din("c_w_uv", [c.KVL, c.MH * 128]); m.c_w_out = din("c_w_out", [D, D])
        m.rope_ret = din("rope_ret", [2, 128, c.SEQ + T])
        m.rope_mla = din("rope_mla", [2, 64, c.SEQ + T])
        m.ret_dec = din("ret_dec", [2, 64, c.RH * 64 + c.RH * 3])

        m.y_p = dout("y_p", [c.SEQ, D]); m.y_s = dout("y_s", [T, D])
        m.conv_p = dout("conv_p", [3, c.CONV]); m.ssd_p = dout("ssd_p", [c.SH, 128, 64]); m.ret_p = dout("ret_p", [c.RH, 256, 256])
        m.ckv_p = dout("ckv_p", [c.SEQ, c.KVL]); m.kpe_p = dout("kpe_p", [c.SEQ, 64])
        m.memk_p = dout("memk_p", [2, c.MEM, 512]); m.memv_p = dout("memv_p", [2, c.MEM, 512])
        m.conv_s = dout("conv_s", [c.NSAMP * 3, c.CONV]); m.ssd_s = dout("ssd_s", [c.NSL, c.SH, 128, 64])
        m.ret_s = dout("ret_s", [c.NSL, c.RH, 256, 256])
        m.ckv_s = dout("ckv_s", [T, c.KVL]); m.kpe_s = dout("kpe_s", [T, 64])

        m.RING_N = 6144
        m.NSLOT = 3
        m.x = Buf(nc.alloc_sbuf_tensor("x", [128, KC, T], F32), KC)
        m.xn = Buf(nc.alloc_sbuf_tensor("xn", [128, KC, T], BF16), KC)
        m.ring = Buf(nc.alloc_sbuf_tensor("ring", [128, m.NSLOT, m.RING_N], BF16), m.NSLOT)
        m.ident = Buf(nc.alloc_sbuf_tensor("ident", [128, 128], F32))
        m.identb = Buf(nc.alloc_sbuf_tensor("identb", [128, 128], BF16))
        m.onesb = Buf(nc.alloc_sbuf_tensor("onesb", [128, 128], BF16))
        m.onesf = Buf(nc.alloc_sbuf_tensor("onesf", [128, 128], F32))
        m.gains = Buf(nc.alloc_sbuf_tensor("gains", [128, 14, KC], F32))
        m.epsb = Buf(nc.alloc_sbuf_tensor("epsb", [128, 2], F32))
        m.memK = Buf(nc.alloc_sbuf_tensor("memK", [128, 2, 4, c.MEM], BF16), 2)
        m.memV = Buf(nc.alloc_sbuf_tensor("memV", [128, 2, c.MEM // 128, 512], BF16), 2)
        NB = c.CONV // 128
        m.cw = Buf(nc.alloc_sbuf_tensor("cw", [128, NB, 4], F32))
        m.cb = Buf(nc.alloc_sbuf_tensor("cb", [128, NB], F32))
        m.convst = Buf(nc.alloc_sbuf_tensor("convst", [128, NB, 3], F32))
        m.hb3 = Buf(nc.alloc_sbuf_tensor("hb3", [64, 3, c.SH], F32))
        m.tri = Buf(nc.alloc_sbuf_tensor("tri", [64, 64], F32))
        m.negm = Buf(nc.alloc_sbuf_tensor("negm", [64, 64], F32))
        m.ps = Buf(nc.alloc_psum_tensor("ps", [128, 8, 512], F32), 8)
        m.ps_next = 0
        m.ps_held = set()
        AW = (nc.sbuf_bytes_remaining - 2048) // 1024 * 1024
        m.ARENA_BYTES = AW
        m.arena = Buf(nc.alloc_sbuf_tensor("arena", [128, AW // 4], F32), AW // 1024)
        m.stream = Stream(fw, m.ring, m.NSLOT)
        m.W = {}

    def V(m, off, shape, dtype, chunk_bytes=None):
        assert off + 0 <= m.ARENA_BYTES
        v = View(m.arena, off, shape, dtype, chunk_bytes)
        assert off + v.nbytes <= m.ARENA_BYTES, (off, v.nbytes, m.ARENA_BYTES)
        return v

    def psum(m, hold=False):
        while m.ps_next % 8 in m.ps_held:
            m.ps_next += 1
        b = m.ps_next % 8
        m.ps_next += 1
        if hold:
            m.ps_held.add(b)
        return b

    def release(m, b):
        m.ps_held.discard(b)

    def wprep(m, name, src, K, M, pair=None, mw=None, segs=None):
        c, fw, nc = m.c, m.fw, m.nc
        if name in m.W:
            w = m.W[name]
        else:
            w = WDesc()
            w.K, w.M = K, M
            w.KC = K // 128
            assert K % 128 == 0 and (M % 128 == 0 or M <= 128), (name, K, M)
            w.mw = max(128, (m.RING_N // w.KC) // 128 * 128)
            w.mw = min(w.mw, 512, M)
            if mw is not None:
                w.mw = mw
            w.chunks = []
            c0 = 0
            while c0 < M:
                n = min(w.mw, M - c0)
                w.chunks.append((c0, n))
                c0 += n
            w.buf = Buf(nc.dram_tensor("wb_" + name, [len(w.chunks), 128, w.KC * w.mw], BF16).ap(), 2 * len(w.chunks))
            m.W[name] = w
        for ci, (c0, n) in enumerate(w.chunks):
            dst = w.buf[ci, :, 0:w.KC * n].rearrange("p (k n) -> p k n", k=w.KC)
            if segs is not None:
                d0 = 0
                first = True
                for (s0, sn) in segs:
                    lo, hi = max(d0, c0), min(d0 + sn, c0 + n)
                    if lo < hi:
                        fw.dma("pool", dst[:, :, lo - c0:hi - c0], src[:, s0 + lo - d0:s0 + hi - d0].rearrange("(k p) n -> p k n", p=128),
                               writes=w.buf.R(2 * ci))
                    d0 += sn
            elif pair is None:
                fw.dma("pool", dst, src[:, c0:c0 + n].rearrange("(k p) n -> p k n", p=128), writes=w.buf.R(2 * ci))
            else:
                assert n == 256
                for half in range(2):
                    sc = (half * pair + ci) * 128
                    fw.dma("pool", dst[:, :, half * 128:(half + 1) * 128],
                           src[:, sc:sc + 128].rearrange("(k p) n -> p k n", p=128), writes=w.buf.R(2 * ci + half))
        return w

    def wchunk(m, w, ci):
        c0, n = w.chunks[ci]
        ap, res = m.stream.get(w.buf[ci, :, 0:w.KC * n], 128, w.KC * n, reads=w.buf.R([2 * ci, 2 * ci + 1]))
        return ap.rearrange("p (k n) -> p k n", k=w.KC), res, c0, n

    def linear_fm(m, w, rhs_fn, N, evac, rhs_res, mlo=0, mhi=None):
        fw = m.fw
        for ci in range(len(w.chunks)):
            wv, wres, c0, n = m.wchunk(w, ci)
            for j in range(n // 128):
                mi = (c0 // 128) + j
                b = m.psum()

                def grp(t, wv=wv, j=j, b=b):
                    for k in range(w.KC):
                        ins = t.matmul(m.ps[:, b, 0:N], wv[:, k, j * 128:(j + 1) * 128], rhs_fn(k),
                                       start=(k == 0), stop=(k == w.KC - 1))
                    return ins
                fw.op("pe", grp, reads=wres + rhs_res, writes=m.ps.R(b))
                evac(mi, b)

    def linear_tm(m, w, lhs_fn, nblk, evac, lhs_res):
        fw = m.fw
        for ci in range(len(w.chunks)):
            wv, wres, c0, n = m.wchunk(w, ci)
            for blk in range(nblk):
                b = m.psum()

                def grp(t, wv=wv, blk=blk, b=b, n=n):
                    for k in range(w.KC):
                        ins = t.matmul(m.ps[:, b, 0:n], lhs_fn(k, blk), wv[:, k, 0:n],
                                       start=(k == 0), stop=(k == w.KC - 1))
                    return ins
                fw.op("pe", grp, reads=wres + lhs_res, writes=m.ps.R(b))
                evac(blk, c0, n, b)

    def copy(m, eng, out, in_, reads, writes):
        if eng == "act":
            return m.fw.op("act", lambda a: a.activation(out, in_, AF.Copy), reads=reads, writes=writes)
        return m.fw.op(eng, lambda v: v.tensor_copy(out, in_), reads=reads, writes=writes)

    def load_vecs(m, items, tmp):
        fw = m.fw
        state = {"batch": [], "rows": 0}

        def flush():
            batch, rows = state["batch"], state["rows"]
            if not batch:
                return
            r = 0
            for (vec, dst, dres, n) in batch:
                fw.dma("sp", tmp[r:r + n, :], vec.rearrange("(k p) -> k p", p=128), writes=tmp.R())
                r += n
            b = m.psum()
            fw.op("pe", lambda t: t.transpose(m.ps[:, b, 0:rows], tmp[0:rows, :], m.ident[0:rows, 0:rows]),
                  reads=tmp.R() + m.ident.R(), writes=m.ps.R(b))
            r = 0
            for (vec, dst, dres, n) in batch:
                m.copy("dve", dst, m.ps[:, b, r:r + n], m.ps.R(b), dres)
                r += n
            state["batch"], state["rows"] = [], 0
        for (vec, dst, dres) in items:
            n = dst.shape[-1]
            if state["rows"] + n > 128:
                flush()
            state["batch"].append((vec, dst, dres, n))
            state["rows"] += n
        flush()

    def setup(m):
        c, fw, nc = m.c, m.fw, m.nc
        KC = c.KC
        fw.op("pool", lambda g: g.memset(m.ident[:], 1.0), writes=m.ident.R())
        fw.op("pool", lambda g: g.affine_select(m.ident[:], m.ident[:], pattern=[[-1, 128]], compare_op=ALU.is_equal,
                                                fill=0.0, base=0, channel_multiplier=1), reads=m.ident.R(), writes=m.ident.R())
        fw.op("dve", lambda v: v.tensor_copy(m.identb[:], m.ident[:]), reads=m.ident.R(), writes=m.identb.R())
        fw.op("dve", lambda v: v.memset(m.onesb[:], 1.0), writes=m.onesb.R())
        fw.op("dve", lambda v: v.memset(m.onesf[:], 1.0), writes=m.onesf.R())
        fw.op("dve", lambda v: v.memset(m.epsb[:], EPS), writes=m.epsb.R())
        tmp = m.V(0, [128, 128], F32)
        items = []
        for l in range(2):
            for i in range(4):
                items.append((m.norms[l, i, :], m.gains[:, l * 4 + i, :], m.gains.R()))
            items.append((m.mem_norm[l, :], m.gains[:, 8 + l, :], m.gains.R()))
        items.append((m.final_norm[:], m.gains[:, 10, :], m.gains.R()))
        items.append((m.ab_ssd_norm[:], m.gains[:, 11, :], m.gains.R()))
        items.append((m.c_q_norm[:], m.gains[:, 12, 0:c.QL // 128], m.gains.R()))
        items.append((m.c_kv_norm[:], m.gains[:, 13, 0:c.KVL // 128], m.gains.R()))
        m.load_vecs(items, tmp)
        QL, KVL, MH = c.QL, c.KVL, c.MH
        m.wprep("wcq", m.c_w_in[:, :], c.D, QL, segs=[(0, QL)])
        m.wprep("wckv", m.c_w_in[:, :], c.D, KVL, segs=[(QL, KVL)])
        o = QL + KVL
        m.wprep("wkpe", m.c_w_in[:, :], c.D, 128, segs=[(o, 64), (o, 64)])
        m.wprep("wkpes", m.c_w_in[:, :], c.D, 128, segs=[(o + 32, 32), (o, 32), (o + 32, 32), (o, 32)])
        m.wprep("wuqn", m.c_w_uq[:, :], QL, MH * 128, segs=[(h * 192, 128) for h in range(MH)])
        m.wprep("wuqr", m.c_w_uq[:, :], QL, MH * 64, segs=[(h * 192 + 128, 64) for h in range(MH)])
        sw = []
        for h in range(MH):
            sw += [(h * 192 + 128 + 32, 32), (h * 192 + 128, 32)]
        m.wprep("wuqs", m.c_w_uq[:, :], QL, MH * 64, segs=sw)
        m.wprep("wuk", m.c_w_uk[:, :], KVL, MH * 128)
        m.wprep("wuv", m.c_w_uv[:, :], KVL, MH * 128)
        m.wprep("wcout", m.c_w_out[:, :], c.D, c.D)
        D, G, HPG, SH = c.D, c.SG, c.HPG, c.SH
        GW = HPG * 64
        oz, oxbc, odt = 0, D, D + c.CONV
        oq = odt + SH
        for g in range(G):
            m.wprep(f"wssd{g}", m.ab_w_in[:, :], D, 2 * GW + 256,
                    segs=[(oz + g * GW, GW), (oxbc + g * GW, GW), (oxbc + D + g * 128, 128), (oxbc + D + G * 128 + g * 128, 128)])
        m.wprep("wdt", m.ab_w_in[:, :], D, SH, segs=[(odt, SH)])
        for hp in range(c.RH // 2):
            m.wprep(f"wret{hp}", m.ab_w_in[:, :], D, 2048, segs=[(oq + i * D + hp * 512, 512) for i in range(4)])
        m.wprep("waboy", m.ab_w_out[0:D, :], D, D)
        m.wprep("waboo", m.ab_w_out[D:2 * D, :], D, D)
        NB = c.CONV // 128
        items = [(m.ab_conv_w[j, :], m.cw[:, :, j], m.cw.R()) for j in range(4)]
        items.append((m.ab_conv_b[:], m.cb[:, :], m.cb.R()))
        m.load_vecs(items, tmp)
        fw.op("dve", lambda v: v.memset(m.convst[:], 0.0), writes=m.convst.R())
        for i, vec in enumerate((m.ab_dt_bias, m.ab_a_log, m.ab_d_skip)):
            fw.dma("sp", m.hb3[:, i, :], vec[:].partition_broadcast(64), writes=m.hb3.R())
        fw.op("act", lambda a: a.activation(m.hb3[:, 1, :], m.hb3[:, 1, :], AF.Exp), reads=m.hb3.R(), writes=m.hb3.R())
        fw.op("dve", lambda v: v.tensor_scalar(m.hb3[:, 1, :], m.hb3[:, 1, :], -1.0, None, op0=ALU.mult), reads=m.hb3.R(), writes=m.hb3.R())
        fw.op("pool", lambda g_: g_.memset(m.tri[:], 1.0), writes=m.tri.R())
        fw.op("pool", lambda g_: g_.affine_select(m.tri[:], m.tri[:], pattern=[[1, 64]], compare_op=ALU.is_ge, fill=0.0, base=0, channel_multiplier=-1),
              reads=m.tri.R(), writes=m.tri.R())
        fw.op("pool", lambda g_: g_.memset(m.negm[:], 0.0), writes=m.negm.R())
        fw.op("pool", lambda g_: g_.affine_select(m.negm[:], m.negm[:], pattern=[[1, 64]], compare_op=ALU.is_ge, fill=-1e30, base=0, channel_multiplier=-1),
              reads=m.negm.R(), writes=m.negm.R())
        if not hasattr(m, "ssd_st"):
            m.ssd_st = Buf(nc.dram_tensor("ssd_st", [c.SH, 128, 64], F32).ap())
            m.ret_st = Buf(nc.dram_tensor("ret_st", [c.RH, 256, 256], F32).ap())
        NJ = c.NT + 1 + 2 * (c.PAST // T)
        m.NJ = NJ
        if not hasattr(m, "kvs"):
            m.kvs = Buf(nc.dram_tensor("kvs", [MH, NJ, 128, 1536], BF16).ap(), NJ)
        FC = c.FC
        for l in range(2):
            for i in range(2):
                m.wprep(f"w1_{l}{i}", m.ffn_w1[l, i], c.D, 2 * c.DFF, pair=FC, mw=256)
                m.wprep(f"w2_{l}{i}", m.ffn_w2[l, i], c.DFF, c.D)
            m.wprep(f"wmq_{l}", m.w_mq[l], c.D, 512)
            m.wprep(f"wmkv_{l}", m.w_mkv[l], c.D, 1024)
            m.wprep(f"wmo_{l}", m.w_mo[l], 512, c.D)

    def norm_fm(m, src, gidx, dst, N, KC, sq_view, rstd_view, col0=0, gk0=0, dk0=0):
        fw = m.fw
        b = m.psum()
        cs = slice(col0, col0 + N)
        for k in range(KC):
            sl = k % 2
            fw.op("act", lambda a, k=k, sl=sl: a.activation(sq_view[:, sl, 0:N], src[:, k, cs], AF.Square),
                  reads=src.R(k), writes=sq_view.R(sl))
            fw.op("pe", lambda t, k=k, sl=sl: t.matmul(m.ps[:, b, 0:N], m.onesb[:], sq_view[:, sl, 0:N],
                                                        start=(k == 0), stop=(k == KC - 1)),
                  reads=sq_view.R(sl) + m.onesb.R(), writes=m.ps.R(b))
        nfeat = KC * 128
        fw.op("act", lambda a: a.activation(rstd_view[:, 0:N], m.ps[:, b, 0:N], AF.Sqrt, bias=m.epsb[:, 0:1], scale=1.0 / nfeat),
              reads=m.ps.R(b) + m.epsb.R(), writes=rstd_view.R())
        fw.op("dve", lambda v: v.reciprocal(rstd_view[:, 0:N], rstd_view[:, 0:N]), reads=rstd_view.R(), writes=rstd_view.R())
        for k in range(KC):
            fw.op("dve", lambda v, k=k: v.scalar_tensor_tensor(dst[:, dk0 + k, cs], src[:, k, cs], m.gains[:, gidx, gk0 + k:gk0 + k + 1],
                                                                 rstd_view[:, 0:N], op0=ALU.mult, op1=ALU.mult),
                  reads=src.R(k) + m.gains.R() + rstd_view.R(), writes=dst.R(dk0 + k))

    def load_tile(m, rows_ap, rows_res):
        c, fw = m.c, m.fw
        xin = m.V(0, [128, 4, c.D], F32, chunk_bytes=c.D * 4)
        for blk in range(4):
            fw.dma("sp", xin[:, blk, :], rows_ap[blk * 128:(blk + 1) * 128, :], reads=rows_res, writes=xin.R(blk))
        for k in range(c.KC):
            b = m.psum()

            def tr(t, k=k, b=b):
                for blk in range(4):
                    ins = t.transpose(m.ps[:, b, blk * 128:(blk + 1) * 128], xin[:, blk, k * 128:(k + 1) * 128], m.ident[:])
                return ins
            fw.op("pe", tr, reads=xin.R() + m.ident.R(), writes=m.ps.R(b))
            m.copy("act" if k % 2 else "dve", m.x[:, k, :], m.ps[:, b, :], m.ps.R(b), m.x.R(k))

    def store_tile(m, rows_ap, rows_res):
        c, fw = m.c, m.fw
        yout = m.V(0, [128, 4, c.D], F32, chunk_bytes=c.D * 4)
        sq = m.V(c.D * 16, [128, 2, T], BF16, chunk_bytes=T * 2)
        rstd = m.V(c.D * 16 + 2048, [128, T], F32)
        m.norm_fm(m.x, 10, m.x, T, c.KC, sq, rstd)
        for k in range(c.KC):
            b = m.psum()

            def tr(t, k=k, b=b):
                for blk in range(4):
                    ins = t.transpose(m.ps[:, b, blk * 128:(blk + 1) * 128], m.x[:, k, blk * 128:(blk + 1) * 128], m.ident[:])
                return ins
            fw.op("pe", tr, reads=m.x.R(k) + m.ident.R(), writes=m.ps.R(b))
            m.copy("act" if k % 2 else "dve", yout[:, :, k * 128:(k + 1) * 128],
                   m.ps[:, b, :].rearrange("p (b n) -> p b n", b=4), m.ps.R(b), yout.R())
        for blk in range(4):
            fw.dma("sp", rows_ap[blk * 128:(blk + 1) * 128, :], yout[:, blk, :], reads=yout.R(), writes=rows_res, is_output=True)

    def ffn(m, l, i):
        c, fw = m.c, m.fw
        FC = c.FC
        H = m.V(0, [128, FC, T], BF16, chunk_bytes=T * 2)
        base = FC * T * 2
        sq = m.V(base, [128, 2, T], BF16, chunk_bytes=T * 2)
        rstd = m.V(base + 2048, [128, T], F32)
        sa = m.V(base + 4096, [128, 2, T], F32, chunk_bytes=T * 4)
        m.norm_fm(m.x, l * 4 + (0 if i == 0 else 3), m.xn, T, c.KC, sq, rstd)

        def evac1(mi, b):
            j, sl = mi // 2, (mi // 2) % 2
            if mi % 2 == 0:
                fw.op("act", lambda a: a.activation(sa[:, sl, :], m.ps[:, b, :], AF.Silu), reads=m.ps.R(b), writes=sa.R(sl))
            else:
                fw.op("dve", lambda v: v.tensor_tensor(H[:, j, :], m.ps[:, b, :], sa[:, sl, :], op=ALU.mult),
                      reads=m.ps.R(b) + sa.R(sl), writes=H.R(j))
        m.linear_fm(m.W[f"w1_{l}{i}"], lambda k: m.xn[:, k, :], T, evac1, m.xn.R())

        def evac2(mi, b):
            fw.op("dve", lambda v: v.scalar_tensor_tensor(m.x[:, mi, :], m.ps[:, b, :], 0.5, m.x[:, mi, :], op0=ALU.mult, op1=ALU.add),
                  reads=m.ps.R(b) + m.x.R(mi), writes=m.x.R(mi))
        m.linear_fm(m.W[f"w2_{l}{i}"], lambda k: H[:, k, :], T, evac2, H.R())

    def drain_pool_dmas(m):
        fw = m.fw
        if fw.dry:
            return
        deps = []
        n = fw.dn["pool"]
        for i, sem in enumerate(fw.dsem["pool"]):
            uses = (n - i + fw.NDMA - 1) // fw.NDMA if n > i else 0
            if uses > 0:
                deps.append((id(sem), 16 * uses))
        fw._wait("sp", deps)

    def memkv_prompt(m):
        c, fw = m.c, m.fw
        D, KC, MB = c.D, c.KC, c.MEM // 128
        NM = c.MEM
        min_ = m.V(0, [128, MB, D], F32, chunk_bytes=D * 4)
        o = MB * D * 4
        memT = m.V(o, [128, KC, NM], F32, chunk_bytes=NM * 4); o += KC * NM * 4
        memn = m.V(o, [128, KC, NM], BF16, chunk_bytes=NM * 2); o += KC * NM * 2
        sq = m.V(o, [128, 2, T], BF16, chunk_bytes=T * 2); o += 2048
        rstd = m.V(o, [128, T], F32); o += 2048
        kvt = m.V(o, [128, MB, 1024], F32, chunk_bytes=4096); o += MB * 4096
        kb = m.V(o, [128, MB, 512], BF16, chunk_bytes=1024); o += MB * 1024
        for blk in range(MB):
            fw.dma("sp", min_[:, blk, :], m.mem_p[blk * 128:(blk + 1) * 128, :], writes=min_.R(blk))
        for k in range(KC):
            b = m.psum()

            def tr(t, k=k, b=b):
                for blk in range(MB):
                    ins = t.transpose(m.ps[:, b, blk * 128:(blk + 1) * 128], min_[:, blk, k * 128:(k + 1) * 128], m.ident[:])
                return ins
            fw.op("pe", tr, reads=min_.R() + m.ident.R(), writes=m.ps.R(b))
            m.copy("act" if k % 2 else "dve", memT[:, k, :], m.ps[:, b, 0:NM], m.ps.R(b), memT.R(k))
        for l in range(2):
            m.norm_fm(memT, 8 + l, memn, NM, KC, sq, rstd)

            def evac(blk, c0, n, b, l=l):
                m.copy("act", kvt[:, blk, c0:c0 + n], m.ps[:, b, 0:n], m.ps.R(b), kvt.R(blk))
            m.linear_tm(m.W[f"wmkv_{l}"], lambda k, blk: memn[:, k, blk * 128:(blk + 1) * 128], MB, evac, memn.R())
            for blk in range(MB):
                fw.dma("sp", m.memk_p[l, blk * 128:(blk + 1) * 128, :], kvt[:, blk, 0:512], reads=kvt.R(blk), writes=m.memk_p.R(), is_output=True)
                fw.dma("sp", m.memv_p[l, blk * 128:(blk + 1) * 128, :], kvt[:, blk, 512:1024], reads=kvt.R(blk), writes=m.memv_p.R(), is_output=True)
                m.copy("dve", m.memV[:, l, blk, :], kvt[:, blk, 512:1024], kvt.R(blk), m.memV.R(l))
                m.copy("dve", kb[:, blk, :], kvt[:, blk, 0:512], kvt.R(blk), kb.R(blk))
            m.kT_from_tok(kb, MB, m.memK[:, l], m.memK.R(l))

    def kT_from_tok(m, kb, MB, dstK, dres):
        fw = m.fw
        b = m.psum()
        psb = m.ps[:, b, :].bitcast(BF16)

        def tr(t):
            for h in range(4):
                for blk in range(MB):
                    col = (h * MB + blk) * 128
                    ins = t.transpose(psb[:, col:col + 128], kb[:, blk, h * 128:(h + 1) * 128], m.identb[:])
            return ins
        fw.op("pe", tr, reads=kb.R() + m.identb.R(), writes=m.ps.R(b))
        m.copy("dve", dstK, psb[:, 0:4 * MB * 128].rearrange("p (h n) -> p h n", h=4), m.ps.R(b), dres)

    def memattn(m, l, sample):
        c, fw = m.c, m.fw
        MB = c.MEM // 128
        o = 0
        sq = m.V(o, [128, 2, T], BF16, chunk_bytes=T * 2); o += 2048
        rstd = m.V(o, [128, T], F32); o += 2048
        qT = m.V(o, [128, 4, T], BF16, chunk_bytes=1024); o += 4096
        oT = m.V(o, [128, 4, T], BF16, chunk_bytes=1024); o += 4096
        pt = m.V(o, [128, 2, T], BF16, chunk_bytes=1024); o += 2048
        rs = m.V(o, [128, T], F32); o += 2048
        kin = m.V(o, [128, 2, MB, 512], F32, chunk_bytes=MB * 2048); o += 2 * MB * 2048
        kb = m.V(o, [128, MB, 512], BF16, chunk_bytes=1024); o += MB * 1024
        vb = m.V(o, [128, MB, 512], BF16, chunk_bytes=1024); o += MB * 1024
        kTs = m.V(o, [128, 4, c.MEM], BF16); o += 4 * c.MEM * 2
        m.norm_fm(m.x, l * 4 + 2, m.xn, T, c.KC, sq, rstd)

        def evq(mi, b):
            m.copy("act", qT[:, mi, :], m.ps[:, b, :], m.ps.R(b), qT.R(mi))
        m.linear_fm(m.W[f"wmq_{l}"], lambda k: m.xn[:, k, :], T, evq, m.xn.R())
        scale = 128.0 ** -0.5
        state = {"n": 0}

        def attend(col0, ncol, K, Kres, Vfn, Vres):
            cs = slice(col0, col0 + ncol)
            for h in range(4):
                bo = m.psum(hold=True)
                bs = m.psum(hold=True)
                for mb in range(MB):
                    b = m.psum()
                    sl = state["n"] % 2
                    state["n"] += 1
                    fw.op("pe", lambda t: t.matmul(m.ps[:, b, 0:ncol], K[:, h, mb * 128:(mb + 1) * 128], qT[:, h, cs], start=True, stop=True),
                          reads=Kres + qT.R(h), writes=m.ps.R(b))
                    fw.op("act", lambda a: a.activation(pt[:, sl, 0:ncol], m.ps[:, b, 0:ncol], AF.Exp, scale=scale),
                          reads=m.ps.R(b), writes=pt.R(sl))
                    fw.op("pe", lambda t: t.matmul(m.ps[:, bo, 0:ncol], Vfn(mb, h), pt[:, sl, 0:ncol], start=(mb == 0), stop=(mb == MB - 1)),
                          reads=Vres + pt.R(sl), writes=m.ps.R(bo))
                    fw.op("pe", lambda t: t.matmul(m.ps[:, bs, 0:ncol], m.onesb[:], pt[:, sl, 0:ncol], start=(mb == 0), stop=(mb == MB - 1)),
                          reads=m.onesb.R() + pt.R(sl), writes=m.ps.R(bs))
                fw.op("dve", lambda v: v.reciprocal(rs[:, 0:ncol], m.ps[:, bs, 0:ncol]), reads=m.ps.R(bs), writes=rs.R())
                fw.op("dve", lambda v: v.tensor_tensor(oT[:, h, cs], m.ps[:, bo, 0:ncol], rs[:, 0:ncol], op=ALU.mult),
                      reads=m.ps.R(bo) + rs.R(), writes=oT.R(h))
                m.release(bo)
                m.release(bs)
        if not sample:
            attend(0, T, m.memK[:, l], m.memK.R(l), lambda mb, h: m.memV[:, l, mb, h * 128:(h + 1) * 128], m.memV.R(l))
        else:
            for s in range(c.NSL):
                fw.dma("sp", kin[:, 0], m.c_mk[l, s].rearrange("(b p) n -> p b n", p=128), writes=kin.R(0))
                fw.dma("sp", kin[:, 1], m.c_mv[l, s].rearrange("(b p) n -> p b n", p=128), writes=kin.R(1))
                m.copy("dve", kb[:], kin[:, 0], kin.R(0), kb.R())
                m.copy("act", vb[:], kin[:, 1], kin.R(1), vb.R())
                m.kT_from_tok(kb, MB, kTs[:], kTs.R())
                attend(s * c.TS, c.TS, kTs, kTs.R(), lambda mb, h: vb[:, mb, h * 128:(h + 1) * 128], vb.R())

        def evo(mi, b):
            fw.op("dve", lambda v: v.tensor_tensor(m.x[:, mi, :], m.ps[:, b, :], m.x[:, mi, :], op=ALU.add),
                  reads=m.ps.R(b) + m.x.R(mi), writes=m.x.R(mi))
        m.linear_fm(m.W[f"wmo_{l}"], lambda k: oT[:, k, :], T, evo, oT.R())

    def emit(m):
        c, fw = m.c, m.fw
        m.setup()
        m.drain_pool_dmas()
        m.stream.barrier()
        m.memkv_prompt()
        ntiles = c.NT + 1
        for ti in range(ntiles):
            sample = ti == c.NT
            if sample:
                rows_in, rin_res, rows_out, rout_res = m.x_s[:, :], m.x_s.R(), m.y_s[:, :], m.y_s.R()
            else:
                rows_in, rin_res = m.x_p[ti * T:(ti + 1) * T, :], m.x_p.R()
                rows_out, rout_res = m.y_p[ti * T:(ti + 1) * T, :], m.y_p.R()
            if ti % 2 == 0:
                fw.new_epoch()
            m.load_tile(rows_in, rin_res)
            for l in range(2):
                if "ffn" in m.stages:
                    m.ffn(l, 0)
                m.mixer(l, ti, sample)
                if "mem" in m.stages:
                    m.memattn(l, sample)
                if "ffn" in m.stages:
                    m.ffn(l, 1)
            m.store_tile(rows_out, rout_res)
        fw.finish()

    def build(m):
        fw = m.fw
        fw.dry = True
        m.ps_next = 0
        m.emit()
        fw.dry = False
        m.ps_next = 0
        m.ps_held = set()
        m.stream.reset()
        m.emit()
        return m.nc


ROPE_BASE = 10000.0


def _tables(c):
    pos = np.concatenate([np.arange(c.SEQ), c.PAST + (np.arange(T) % c.TS)]).astype(np.float32)
    def tab(half):
        inv = (ROPE_BASE ** (-np.arange(half, dtype=np.float32) / np.float32(half))).astype(np.float32)
        ang = (inv[:, None] * pos[None, :]).astype(np.float32)
        return np.cos(ang).astype(np.float32), np.sin(ang).astype(np.float32)
    cr, sr = tab(128)
    rope_ret = np.stack([cr, sr]).astype(np.float32)
    cm, sm = tab(32)
    rope_mla = np.stack([np.concatenate([cm, cm], 0), np.concatenate([-sm, sm], 0)]).astype(np.float32)
    RH = c.RH
    lg = np.log1p(-np.exp2(-5.0 - np.arange(RH, dtype=np.float32))).astype(np.float32)
    out = np.zeros((2, 64, RH * 64 + RH * 3), np.float32)
    for li, L in enumerate((64, c.TS)):
        s_ = np.arange(64)[:, None, None]
        l_ = np.arange(64)[None, None, :]
        dec = np.where((s_ <= l_) & (l_ < L) & (s_ < L), np.exp(lg[None, :, None] * (l_ - s_)), 0.0)
        out[li, :, :RH * 64] = dec.reshape(64, RH * 64)
        sv = np.arange(64)[:, None]
        out[li, :, RH * 64 + 0 * RH:RH * 64 + 1 * RH] = np.exp(lg[None, :] * (sv + 1))
        out[li, :, RH * 64 + 1 * RH:RH * 64 + 2 * RH] = np.where(sv < L, np.exp(lg[None, :] * np.maximum(L - 1 - sv, 0)), 0.0)
        out[li, :, RH * 64 + 2 * RH:RH * 64 + 3 * RH] = np.exp(lg[None, :] * L)
    return rope_ret, rope_mla, out.astype(np.float32)


def make_in_maps(inp, c, ncores=8):
    f = lambda a: np.ascontiguousarray(np.asarray(a, dtype=np.float32))
    rope_ret, rope_mla, ret_dec = _tables(c)
    shared = {
        "norms": f(inp["norms"]), "mem_norm": f(inp["mem_norm"]), "final_norm": f(inp["final_norm"]),
        "ffn_w1": f(inp["ffn_w1"]), "ffn_w2": f(inp["ffn_w2"]),
        "w_mq": f(inp["w_mq"]), "w_mkv": f(inp["w_mkv"]), "w_mo": f(inp["w_mo"]),
        "ab_w_in": f(inp["ab_w_in"][0]), "ab_conv_w": f(inp["ab_conv_w"][0]), "ab_conv_b": f(inp["ab_conv_b"][0]),
        "ab_dt_bias": f(inp["ab_dt_bias"][0]), "ab_a_log": f(inp["ab_a_log"][0]), "ab_d_skip": f(inp["ab_d_skip"][0]),
        "ab_ssd_norm": f(inp["ab_ssd_norm"][0]), "ab_w_out": f(inp["ab_w_out"][0]),
        "c_w_in": f(inp["c_w_in"][0]), "c_q_norm": f(inp["c_q_norm"][0]), "c_kv_norm": f(inp["c_kv_norm"][0]),
        "c_w_uq": f(inp["c_w_uq"][0]), "c_w_uk": f(inp["c_w_uk"][0]).reshape(c.KVL, c.MH * 128),
        "c_w_uv": f(inp["c_w_uv"][0]).reshape(c.KVL, c.MH * 128), "c_w_out": f(inp["c_w_out"][0]),
        "rope_ret": rope_ret, "rope_mla": rope_mla, "ret_dec": ret_dec,
    }
    xp, mp = f(inp["x_prompt"]), f(inp["mem_prompt"])
    xs = f(inp["x_sample"]); sconv = f(inp["state_conv"][0]); sssd = f(inp["state_ssd"][0]); sret = f(inp["state_ret"][0])
    cckv = f(inp["cache_ckv"][0]); ckpe = f(inp["cache_kpe"][0])
    cmk = f(inp["cache_mem_k"]).reshape(2, c.NSAMP, c.MEM, 512); cmv = f(inp["cache_mem_v"]).reshape(2, c.NSAMP, c.MEM, 512)
    nb = xp.shape[0]
    NSL = c.NSL
    maps = []
    for i in range(ncores):
        d = dict(shared)
        d["x_p"] = xp[i % nb]
        d["mem_p"] = mp[i % nb]
        sl = slice(i * NSL, (i + 1) * NSL)
        x_s = np.zeros((T, c.D), np.float32)
        x_s[:NSL * c.TS] = xs[sl].reshape(NSL * c.TS, c.D)
        d["x_s"] = x_s
        sc = np.zeros((c.NSAMP * 3, c.CONV), np.float32)
        sc[:NSL * 3] = sconv[sl].reshape(NSL * 3, c.CONV)
        d["st_conv"] = sc
        d["st_ssd"] = np.ascontiguousarray(sssd[sl]); d["st_ret"] = np.ascontiguousarray(sret[sl])
        d["c_ckv"] = np.ascontiguousarray(cckv[sl]); d["c_kpe"] = np.ascontiguousarray(ckpe[sl])
        d["c_mk"] = np.ascontiguousarray(cmk[:, sl]); d["c_mv"] = np.ascontiguousarray(cmv[:, sl])
        maps.append(d)
    return maps


def gather(res, c, nb=4):
    r = res
    ncs = c.NSAMP // c.NSL
    NSL = c.NSL
    st = lambda k: np.stack([r[b][k] for b in range(nb)])
    cat = lambda k, n: np.concatenate([r[i][k][:n] for i in range(ncs)], axis=0)
    y_p = st("y_p")
    y_s = cat("y_s", NSL * c.TS).reshape(c.NSAMP, c.TS, c.D)
    conv_p = st("conv_p")[None]
    ssd_p = st("ssd_p")[None]
    ret_p = st("ret_p")[None]
    ckv_p = st("ckv_p")[None]
    kpe_p = st("kpe_p")[None]
    memk = np.stack([r[b]["memk_p"] for b in range(nb)], axis=1).reshape(2, nb, c.MEM, 4, 128)
    memv = np.stack([r[b]["memv_p"] for b in range(nb)], axis=1).reshape(2, nb, c.MEM, 4, 128)
    conv_s = cat("conv_s", NSL * 3).reshape(1, c.NSAMP, 3, c.CONV)
    ssd_s = cat("ssd_s", NSL)[None]
    ret_s = cat("ret_s", NSL)[None]
    ckv_s = cat("ckv_s", NSL * c.TS).reshape(1, c.NSAMP, c.TS, c.KVL)
    kpe_s = cat("kpe_s", NSL * c.TS).reshape(1, c.NSAMP, c.TS, 64)
    return (y_p, y_s, conv_p, ssd_p, ret_p, ckv_p, kpe_p, memk, memv, conv_s, ssd_s, ret_s, ckv_s, kpe_s)


def kernel(**inputs):
    c = Cfg()
    m = Model(c, stages=("ffn", "mix", "mem"))
    nc = m.build()
    maps = make_in_maps(inputs, c)
    res = run_bass_kernel_spmd(nc, maps, core_ids=list(range(8)))
    outs = gather(res.results, c)
    return tuple(np.ascontiguousarray(o, dtype=np.float32) for o in outs)


def _mla(m, ti, sample):
    c, fw = m.c, m.fw
    QLC, KVLC, MH = c.QL // 128, c.KVL // 128, c.MH
    PJ = c.PAST // T
    o = 0
    def alloc(shape, dt, cb=None):
        nonlocal o
        v = m.V(o, shape, dt, cb)
        o += (v.nbytes + 1023) // 1024 * 1024
        return v
    sq = alloc([128, 2, T], BF16, T * 2)
    rstd = alloc([128, T], F32)
    qnT = alloc([128, MH, T], BF16, T * 2)
    qrT = alloc([128, MH // 2, T], BF16, T * 2)
    pt = alloc([128, 2, T], BF16, T * 2)
    rs = alloc([128, T], F32)
    snk = alloc([128, MH, c.TS], BF16)
    snv = alloc([c.TS, MH, 128], BF16)
    snr = alloc([128, c.TS], BF16)
    PD0 = o
    tab = alloc([128, 2, T], F32, T * 4)
    ckvb = alloc([128, KVLC, T], BF16, T * 2)
    cqn = alloc([128, QLC, T], BF16, T * 2)
    krf = alloc([128, T], F32)
    krb = alloc([128, T], BF16)
    kpe = alloc([128, 2, T], F32, T * 4)
    kpo = alloc([128, 4, 64], F32)
    A0 = o
    cqT = alloc([128, QLC, T], F32, T * 4)
    ckvT = alloc([128, KVLC, T], F32, T * 4)
    qraw = alloc([128, MH // 2, T], F32, T * 4)
    cko = alloc([128, 4, c.KVL], F32, c.KVL * 4)
    o = A0
    kst = alloc([128, MH, T], BF16, T * 2)
    vst = alloc([128, 4, MH * 128], BF16, MH * 256)
    col0 = c.SEQ if sample else ti * T
    jnew = c.NT if sample else ti

    m.norm_fm(m.x, 4 + 1, m.xn, T, c.KC, sq, rstd)
    for half in range(2):
        fw.dma("sp", tab[half * 64:(half + 1) * 64, :, :], m.rope_mla[:, :, col0:col0 + T].rearrange("a p n -> p a n"), writes=tab.R())
    xr = m.xn.R()
    def ev_to(view):
        def ev(mi, b):
            m.copy("act" if mi % 2 else "dve", view[:, mi, :], m.ps[:, b, :], m.ps.R(b), view.R(mi))
        return ev
    m.linear_fm(m.W["wcq"], lambda k: m.xn[:, k, :], T, ev_to(cqT), xr)
    m.linear_fm(m.W["wckv"], lambda k: m.xn[:, k, :], T, ev_to(ckvT), xr)
    m.linear_fm(m.W["wkpe"], lambda k: m.xn[:, k, :], T, lambda mi, b: m.copy("act", kpe[:, 0, :], m.ps[:, b, :], m.ps.R(b), kpe.R(0)), xr)
    m.linear_fm(m.W["wkpes"], lambda k: m.xn[:, k, :], T, lambda mi, b: m.copy("act", kpe[:, 1, :], m.ps[:, b, :], m.ps.R(b), kpe.R(1)), xr)
    fw.op("dve", lambda v: v.tensor_tensor(krf[:], kpe[:, 0, :], tab[:, 0, :], op=ALU.mult), reads=kpe.R(0) + tab.R(), writes=krf.R())
    fw.op("dve", lambda v: v.tensor_tensor(kpe[:, 1, :], kpe[:, 1, :], tab[:, 1, :], op=ALU.mult), reads=kpe.R(1) + tab.R(), writes=kpe.R(1))
    fw.op("dve", lambda v: v.tensor_tensor(krf[:], krf[:], kpe[:, 1, :], op=ALU.add), reads=kpe.R(1) + krf.R(), writes=krf.R())
    m.copy("act", krb[:], krf[:], krf.R(), krb.R())
    m.norm_fm(cqT, 12, cqn, T, QLC, sq, rstd)
    m.norm_fm(ckvT, 13, ckvT, T, KVLC, sq, rstd)
    for k in range(KVLC):
        m.copy("act", ckvb[:, k, :], ckvT[:, k, :], ckvT.R(k), ckvb.R(k))
    ck_out, kp_out = (m.ckv_s, m.kpe_s) if sample else (m.ckv_p, m.kpe_p)
    r0 = 0 if sample else ti * T
    for blk in range(4):
        for c4 in range(0, KVLC, 4):
            nn = min(4, KVLC - c4)
            b = m.psum()
            def tr(t, blk=blk, c4=c4, nn=nn, b=b):
                for cc in range(nn):
                    ins = t.transpose(m.ps[:, b, cc * 128:(cc + 1) * 128], ckvT[:, c4 + cc, blk * 128:(blk + 1) * 128], m.ident[:])
                return ins
            fw.op("pe", tr, reads=ckvT.R() + m.ident.R(), writes=m.ps.R(b))
            m.copy("dve", cko[:, blk, c4 * 128:(c4 + nn) * 128], m.ps[:, b, 0:nn * 128], m.ps.R(b), cko.R(blk))
        fw.dma("sp", ck_out[r0 + blk * 128:r0 + (blk + 1) * 128, :], cko[:, blk, :], reads=cko.R(blk), writes=ck_out.R(), is_output=True)
    b = m.psum()
    def trk(t):
        for blk in range(4):
            ins = t.transpose(m.ps[:, b, blk * 64:(blk + 1) * 64], krf[0:64, blk * 128:(blk + 1) * 128], m.ident[0:64, 0:64])
        return ins
    fw.op("pe", trk, reads=krf.R() + m.ident.R(), writes=m.ps.R(b))
    m.copy("dve", kpo[:], m.ps[:, b, 0:256].rearrange("p (b n) -> p b n", b=4), m.ps.R(b), kpo.R())
    fw.dma("sp", kp_out[r0:r0 + T, :].rearrange("(b p) n -> p b n", p=128), kpo[:], reads=kpo.R(), writes=kp_out.R(), is_output=True)
    m.linear_fm(m.W["wuqn"], lambda k: cqn[:, k, :], T, ev_to(qnT), cqn.R())
    m.linear_fm(m.W["wuqr"], lambda k: cqn[:, k, :], T, ev_to(qraw), cqn.R())
    def ev_sw(mi, b):
        fw.op("dve", lambda v: v.tensor_tensor(qraw[:, mi, :], qraw[:, mi, :], tab[:, 0, :], op=ALU.mult), reads=qraw.R(mi) + tab.R(), writes=qraw.R(mi))
        fw.op("dve", lambda v: v.tensor_tensor(rs[:], m.ps[:, b, :], tab[:, 1, :], op=ALU.mult), reads=m.ps.R(b) + tab.R(), writes=rs.R())
        fw.op("dve", lambda v: v.tensor_tensor(qrT[:, mi, :], qraw[:, mi, :], rs[:], op=ALU.add), reads=qraw.R(mi) + rs.R(), writes=qrT.R(mi))
    m.linear_fm(m.W["wuqs"], lambda k: cqn[:, k, :], T, ev_sw, cqn.R())
    m.linear_fm(m.W["wuk"], lambda k: ckvb[:, k, :], T, ev_to(kst), ckvb.R())
    def ev_v(blk, c0, n, b):
        m.copy("act" if blk % 2 else "dve", vst[:, blk, c0:c0 + n], m.ps[:, b, 0:n], m.ps.R(b), vst.R(blk))
    m.linear_tm(m.W["wuv"], lambda k, blk: ckvb[:, k, blk * 128:(blk + 1) * 128], 4, ev_v, ckvb.R())
    kres = m.kvs.R(jnew)
    fw.dma("sp", m.kvs[:, jnew, :, 0:512].rearrange("h p n -> p h n"), kst[:], reads=kst.R(), writes=kres)
    for blk in range(4):
        fw.dma("sp", m.kvs[:, jnew, :, 512 + blk * 128:512 + (blk + 1) * 128].rearrange("h p n -> p h n"),
               vst[:, blk, :].rearrange("p (h n) -> p h n", h=MH), reads=vst.R(blk), writes=kres)
    for h in range(MH):
        fw.dma("sp", m.kvs[h, jnew, :, 1024:1536], krb[:], reads=krb.R(), writes=kres)
    m.stream.barrier()

    scale = 192.0 ** -0.5
    st = {"n": 0}
    def block(h, bo, bs, c0, N, nk, kn_l, kr_l, v_l, kres_, first, last, mask64=False):
        rp = slice(0, 64) if h % 2 == 0 else slice(64, 128)
        b = m.psum()
        sl = st["n"] % 2
        st["n"] += 1
        def qk(t):
            t.matmul(m.ps[0:nk, b, 0:N], kn_l, qnT[:, h, c0:c0 + N], start=True, stop=False)
            return t.matmul(m.ps[0:nk, b, 0:N], kr_l(rp), qrT[rp, h // 2, c0:c0 + N], start=False, stop=True)
        fw.op("pe", qk, reads=kres_ + qnT.R(h) + qrT.R(h // 2), writes=m.ps.R(b))
        fw.op("act", lambda a: a.activation(pt[0:nk, sl, 0:N], m.ps[0:nk, b, 0:N], AF.Exp, scale=scale), reads=m.ps.R(b), writes=pt.R(sl))
        if mask64:
            fw.op("pool", lambda g: g.memset(pt[64:128, sl, 0:64], 0.0), reads=pt.R(sl), writes=pt.R(sl))
        def pv(t):
            t.matmul(m.ps[:, bo, c0:c0 + N], v_l, pt[0:nk, sl, 0:N], start=first, stop=last)
            return t.matmul(m.ps[:, bs, c0:c0 + N], m.onesb[0:nk, :], pt[0:nk, sl, 0:N], start=first, stop=last)
        fw.op("pe", pv, reads=kres_ + pt.R(sl) + m.onesb.R(), writes=m.ps.R(bo) + m.ps.R(bs))

    def finish_head(h, bo, bs, c0, N):
        fw.op("dve", lambda v: v.reciprocal(rs[:, 0:N], m.ps[:, bs, c0:c0 + N]), reads=m.ps.R(bs), writes=rs.R())
        fw.op("dve", lambda v: v.tensor_tensor(qnT[:, h, c0:c0 + N], m.ps[:, bo, c0:c0 + N], rs[:, 0:N], op=ALU.mult),
              reads=m.ps.R(bo) + rs.R(), writes=qnT.R(h))
        m.release(bo)
        m.release(bs)

    def chunk(h, j):
        ap, res = m.stream.get(m.kvs[h, j], 128, 1536, reads=m.kvs.R(j))
        return ap, res

    if not sample:
        for h in range(MH):
            bo, bs = m.psum(hold=True), m.psum(hold=True)
            for j in range(ti + 1):
                ap, res = chunk(h, j)
                v3 = ap[:, 512:1024].rearrange("p (b n) -> p b n", b=4)
                for blk in range(4):
                    ks = slice(blk * 128, (blk + 1) * 128)
                    diag = (j == ti)
                    c0 = blk * 128 if diag else 0
                    block(h, bo, bs, c0, T - c0, 128, ap[:, ks], lambda rp, ks=ks: ap[rp, 1024 + ks.start:1024 + ks.stop], v3[:, blk, :], res,
                          first=(j == 0 and blk == 0), last=(j == ti and blk == 3), mask64=diag)
            finish_head(h, bo, bs, 0, T)
    else:
        for s in range(c.NSL):
            sc = slice(s * c.TS, (s + 1) * c.TS)
            _mla_cache_kv(m, s, PD0, A0, kst, vst)
            jb = c.NT + 1 + (s % 2) * PJ
            nres = m.kvs.R(c.NT)
            fw.dma("sp", snk[:], m.kvs[:, c.NT, :, s * c.TS:(s + 1) * c.TS].rearrange("h p n -> p h n"), reads=nres, writes=snk.R())
            p0, vb = (s * c.TS) % 128, (s * c.TS) // 128
            fw.dma("sp", snv[:], m.kvs[:, c.NT, p0:p0 + c.TS, 512 + vb * 128:512 + (vb + 1) * 128].rearrange("h p n -> p h n"), reads=nres, writes=snv.R())
            fw.dma("sp", snr[:], m.kvs[0, c.NT, :, 1024 + s * c.TS:1024 + (s + 1) * c.TS], reads=nres, writes=snr.R())
            for h in range(MH):
                bo, bs = m.psum(hold=True), m.psum(hold=True)
                for jj in range(PJ):
                    ap, res = chunk(h, jb + jj)
                    v3 = ap[:, 512:1024].rearrange("p (b n) -> p b n", b=4)
                    for blk in range(4):
                        ks = slice(blk * 128, (blk + 1) * 128)
                        block(h, bo, bs, s * c.TS, c.TS, 128, ap[:, ks], lambda rp, ks=ks: ap[rp, 1024 + ks.start:1024 + ks.stop], v3[:, blk, :], res,
                              first=(jj == 0 and blk == 0), last=False)
                block(h, bo, bs, s * c.TS, c.TS, c.TS, snk[:, h, :], lambda rp: snr[rp, :], snv[:, h, :], snk.R() + snv.R() + snr.R(),
                      first=False, last=True)
                finish_head(h, bo, bs, s * c.TS, c.TS)

    def evo(mi, b):
        fw.op("dve", lambda v: v.tensor_tensor(m.x[:, mi, :], m.ps[:, b, :], m.x[:, mi, :], op=ALU.add),
              reads=m.ps.R(b) + m.x.R(mi), writes=m.x.R(mi))
    m.linear_fm(m.W["wcout"], lambda k: qnT[:, k, :], T, evo, qnT.R())


def _mla_cache_kv(m, s, PD0, A0, kst, vst):
    c, fw = m.c, m.fw
    KVLC, MH = c.KVL // 128, c.MH
    PJ = c.PAST // T
    o = PD0
    def alloc(shape, dt, cb=None):
        nonlocal o
        v = m.V(o, shape, dt, cb)
        o += (v.nbytes + 1023) // 1024 * 1024
        return v
    cin = alloc([128, 4, c.KVL], F32)
    cinb = alloc([128, 4, c.KVL], BF16)
    cT = alloc([128, KVLC, T], BF16, T * 2)
    pin = alloc([128, 4, 64], F32)
    pinb = alloc([128, 4, 128], BF16)
    prT = alloc([128, T], BF16)
    assert o <= A0, (o, A0)
    for jj in range(PJ):
        j = c.NT + 1 + (s % 2) * PJ + jj
        fw.dma("sp", cin[:], m.c_ckv[s, jj * T:(jj + 1) * T, :].rearrange("(b p) n -> p b n", p=128), writes=cin.R())
        fw.dma("sp", pin[:], m.c_kpe[s, jj * T:(jj + 1) * T, :].rearrange("(b p) n -> p b n", p=128), writes=pin.R())
        m.copy("dve", cinb[:], cin[:], cin.R(), cinb.R())
        m.copy("act", pinb[:, :, 0:64], pin[:], pin.R(), pinb.R())
        m.copy("act", pinb[:, :, 64:128], pin[:], pin.R(), pinb.R())
        for cc in range(KVLC):
            b = m.psum()
            psb = m.ps[:, b, :].bitcast(BF16)
            def tr(t, cc=cc, psb=psb):
                for blk in range(4):
                    ins = t.transpose(psb[:, blk * 128:(blk + 1) * 128], cinb[:, blk, cc * 128:(cc + 1) * 128], m.identb[:])
                return ins
            fw.op("pe", tr, reads=cinb.R() + m.identb.R(), writes=m.ps.R(b))
            m.copy("act" if cc % 2 else "dve", cT[:, cc, :], psb[:, 0:T], m.ps.R(b), cT.R(cc))
        b = m.psum()
        psb = m.ps[:, b, :].bitcast(BF16)
        def trp(t, psb=psb):
            for blk in range(4):
                ins = t.transpose(psb[:, blk * 128:(blk + 1) * 128], pinb[:, blk, :], m.identb[:])
            return ins
        fw.op("pe", trp, reads=pinb.R() + m.identb.R(), writes=m.ps.R(b))
        m.copy("dve", prT[:], psb[:, 0:T], m.ps.R(b), prT.R())
        def ev_k(mi, b):
            m.copy("act" if mi % 2 else "dve", kst[:, mi, :], m.ps[:, b, :], m.ps.R(b), kst.R(mi))
        m.linear_fm(m.W["wuk"], lambda k: cT[:, k, :], T, ev_k, cT.R())
        def ev_v(blk, c0, n, b):
            m.copy("act" if blk % 2 else "dve", vst[:, blk, c0:c0 + n], m.ps[:, b, 0:n], m.ps.R(b), vst.R(blk))
        m.linear_tm(m.W["wuv"], lambda k, blk: cT[:, k, blk * 128:(blk + 1) * 128], 4, ev_v, cT.R())
        kres = m.kvs.R(j)
        fw.dma("sp", m.kvs[:, j, :, 0:512].rearrange("h p n -> p h n"), kst[:], reads=kst.R(), writes=kres)
        for blk in range(4):
            fw.dma("sp", m.kvs[:, j, :, 512 + blk * 128:512 + (blk + 1) * 128].rearrange("h p n -> p h n"),
                   vst[:, blk, :].rearrange("p (h n) -> p h n", h=MH), reads=vst.R(blk), writes=kres)
        for h in range(MH):
            fw.dma("sp", m.kvs[h, j, :, 1024:1536], prT[:], reads=prT.R(), writes=kres)
    m.stream.barrier()


def _ssdret(m, ti, sample):
    c, fw = m.c, m.fw
    D, KC, G, HPG, SH, RH = c.D, c.KC, c.SG, c.HPG, c.SH, c.RH
    GW = HPG * 64
    GB = GW // 128
    last_tile = (ti == c.NT - 1)
    if sample:
        L, NCH, NSEG, SL = c.TS, c.NSL, c.NSAMP, c.TS
    else:
        L, NCH, NSEG, SL = 64, T // 64, 1, T
    SEGW = SL + 3
    chunks = [(ch * L, L) for ch in range(NCH)]
    li = 1 if sample else 0
    o = 0
    def alloc(shape, dt, cb=None):
        nonlocal o
        v = m.V(o, shape, dt, cb)
        o += (v.nbytes + 1023) // 1024 * 1024
        return v
    sq = alloc([128, 2, T], BF16, T * 2)
    rstd = alloc([128, T], F32)
    YO = alloc([128, KC, T], BF16, T * 2)
    wdt = alloc([128, KC, SH], BF16)
    dt = alloc([64, NCH, SH], F32)
    cum = alloc([64, NCH, SH], F32)
    ecum = alloc([64, NCH, SH], F32)
    rdec = alloc([64, RH * 64 + 3 * RH], F32)
    P0 = o
    zs = alloc([128, GB, T], BF16, T * 2)
    raw = alloc([128, GB + 2, NSEG * SEGW], F32, NSEG * SEGW * 4)
    xbc = alloc([128, GB + 2, T], BF16, T * 2)
    yz = alloc([128, GB, T], F32, T * 4)
    Xd = alloc([64, HPG, 64], F32)
    seg = alloc([64, HPG, 64], F32)
    MT = alloc([64, HPG, 64], BF16)
    tok = alloc([64, GW + 128], BF16)
    xdt = alloc([64, HPG, 64], BF16)
    xdtw = alloc([64, HPG, 64], BF16)
    tt = alloc([64, HPG, 64], F32)
    uu = alloc([64, HPG, 64], F32)
    ytok = alloc([64, GW], BF16)
    wv_ = alloc([64, 2, HPG], F32)
    hst = alloc([128, HPG, 64], F32)
    hbf = alloc([128, HPG, 64], BF16)
    halo = alloc([128, 128], F32)
    cso = alloc([128, 128], F32)
    assert o <= m.ARENA_BYTES

    m.norm_fm(m.x, 1, m.xn, T, KC, sq, rstd)
    fw.dma("sp", rdec[:], m.ret_dec[li], writes=rdec.R())
    wd, wres, _, _ = m.wchunk(m.W["wdt"], 0)
    m.copy("act", wdt[:], wd, wres, wdt.R())
    CPB = 512 // SH
    for c0 in range(0, NCH, CPB):
        nch = min(CPB, NCH - c0)
        b = m.psum()
        def mm(t, c0=c0, nch=nch, b=b):
            for ch in range(nch):
                col, _ = chunks[c0 + ch]
                for k in range(KC):
                    ins = t.matmul(m.ps[0:L, b, ch * SH:(ch + 1) * SH], m.xn[:, k, col:col + L], wdt[:, k, :], start=(k == 0), stop=(k == KC - 1))
            return ins
        fw.op("pe", mm, reads=m.xn.R() + wdt.R(), writes=m.ps.R(b))
        dsl = dt[0:L, c0:c0 + nch, :]
        csl = cum[0:L, c0:c0 + nch, :]
        esl = ecum[0:L, c0:c0 + nch, :]
        bia = m.hb3[0:L, 0:1, :].to_broadcast([L, nch, SH])
        fw.op("dve", lambda v: v.tensor_tensor(dsl, m.ps[0:L, b, 0:nch * SH].rearrange("p (c h) -> p c h", h=SH), bia, op=ALU.add),
              reads=m.ps.R(b) + m.hb3.R(), writes=dt.R())
        fw.op("act", lambda a: a.activation(csl, dsl, AF.Abs), reads=dt.R(), writes=cum.R())
        fw.op("act", lambda a: a.activation(csl, csl, AF.Exp, scale=-1.0), reads=cum.R(), writes=cum.R())
        fw.op("act", lambda a: a.activation(csl, csl, AF.Ln, bias=1.0), reads=cum.R(), writes=cum.R())
        fw.op("dve", lambda v: v.scalar_tensor_tensor(dsl, dsl, 0.0, csl, op0=ALU.max, op1=ALU.add), reads=dt.R() + cum.R(), writes=dt.R())
        aa = m.hb3[0:L, 1:2, :].to_broadcast([L, nch, SH])
        fw.op("dve", lambda v: v.tensor_tensor(esl, dsl, aa, op=ALU.mult), reads=dt.R() + m.hb3.R(), writes=ecum.R())
        b2 = m.psum()
        fw.op("pe", lambda t: t.matmul(m.ps[0:L, b2, 0:nch * SH], m.tri[0:L, 0:L], ecum[0:L, c0:c0 + nch, :].rearrange("p c h -> p (c h)"), start=True, stop=True),
              reads=ecum.R() + m.tri.R(), writes=m.ps.R(b2))
        m.copy("dve", csl, m.ps[0:L, b2, 0:nch * SH].rearrange("p (c h) -> p c h", h=SH), m.ps.R(b2), cum.R())
        fw.op("act", lambda a: a.activation(esl, csl, AF.Exp), reads=cum.R(), writes=ecum.R())

    raw4 = raw.ap.rearrange("p k (s w) -> p k s w", w=SEGW)
    for g in range(G):
        blks = [g * GB + i for i in range(GB)] + [KC + g, KC + G + g]
        for bi, gb_ in enumerate(blks):
            if sample:
                fw.dma("sp", halo[0:NSEG * 3, :], m.st_conv[:, gb_ * 128:(gb_ + 1) * 128], writes=halo.R())
                b = m.psum()
                fw.op("pe", lambda t: t.transpose(m.ps[:, b, 0:NSEG * 3], halo[0:NSEG * 3, :], m.ident[0:NSEG * 3, 0:NSEG * 3]),
                      reads=halo.R() + m.ident.R(), writes=m.ps.R(b))
                m.copy("dve", raw4[:, bi, :, 0:3], m.ps[:, b, 0:NSEG * 3].rearrange("p (s w) -> p s w", w=3), m.ps.R(b), raw.R(bi))
            else:
                m.copy("dve", raw4[:, bi, 0, 0:3], m.convst[:, gb_, :], m.convst.R(), raw.R(bi))
        def ev(mi, b):
            if mi < GB:
                fw.op("act", lambda a: a.activation(zs[:, mi, :], m.ps[:, b, :], AF.Silu), reads=m.ps.R(b), writes=zs.R(mi))
            else:
                bi = mi - GB
                m.copy("dve", raw4[:, bi, :, 3:3 + SL], m.ps[:, b, :].rearrange("p (s w) -> p s w", w=SL), m.ps.R(b), raw.R(bi))
        m.linear_fm(m.W[f"wssd{g}"], lambda k: m.xn[:, k, :], T, ev, m.xn.R())
        for bi, gb_ in enumerate(blks):
            if sample:
                fw.op("act", lambda a: a.activation(cso[:, 0:NSEG * 3].rearrange("p (s w) -> p s w", w=3), raw4[:, bi, :, SL:SL + 3], AF.Copy),
                      reads=raw.R(bi), writes=cso.R())
                b = m.psum()
                fw.op("pe", lambda t: t.transpose(m.ps[0:NSEG * 3, b, 0:128], cso[:, 0:NSEG * 3], m.ident[:]),
                      reads=cso.R() + m.ident.R(), writes=m.ps.R(b))
                m.copy("dve", halo[0:NSEG * 3, :], m.ps[0:NSEG * 3, b, 0:128], m.ps.R(b), halo.R())
                fw.dma("sp", m.conv_s[:, gb_ * 128:(gb_ + 1) * 128], halo[0:NSEG * 3, :], reads=halo.R(), writes=m.conv_s.R(), is_output=True)
            else:
                m.copy("act", m.convst[:, gb_, :], raw4[:, bi, 0, SL:SL + 3], raw.R(bi), m.convst.R())
            acc = yz[:, 0, :].rearrange("p (s w) -> p s w", w=SL)
            fw.op("dve", lambda v: v.tensor_scalar(acc, raw4[:, bi, :, 0:SL], m.cw[:, gb_, 0:1], m.cb[:, gb_:gb_ + 1], op0=ALU.mult, op1=ALU.add),
                  reads=raw.R(bi) + m.cw.R() + m.cb.R(), writes=yz.R(0))
            for j in range(1, 4):
                fw.op("dve", lambda v, j=j: v.scalar_tensor_tensor(acc, raw4[:, bi, :, j:j + SL], m.cw[:, gb_, j:j + 1], acc, op0=ALU.mult, op1=ALU.add),
                      reads=raw.R(bi) + m.cw.R() + yz.R(0), writes=yz.R(0))
            fw.op("act", lambda a: a.activation(xbc[:, bi, :], yz[:, 0, :], AF.Silu), reads=yz.R(0), writes=xbc.R(bi))
        hsrc = m.st_ssd if sample else m.ssd_st
        def load_state(sidx):
            if sample:
                fw.dma("sp", hst[:], m.st_ssd[sidx, g * HPG:(g + 1) * HPG].rearrange("h n p -> n h p"), writes=hst.R())
            elif ti == 0:
                fw.op("pool", lambda g_: g_.memset(hst[:], 0.0), writes=hst.R())
            else:
                fw.dma("sp", hst[:], m.ssd_st[g * HPG:(g + 1) * HPG].rearrange("h n p -> n h p"), reads=m.ssd_st.R(), writes=hst.R())
            m.copy("act", hbf[:], hst[:], hst.R(), hbf.R())
        def store_state(sidx):
            if sample:
                fw.dma("sp", m.ssd_s[sidx, g * HPG:(g + 1) * HPG].rearrange("h n p -> n h p"), hst[:], reads=hst.R(), writes=m.ssd_s.R(), is_output=True)
            else:
                fw.dma("sp", m.ssd_st[g * HPG:(g + 1) * HPG].rearrange("h n p -> n h p"), hst[:], reads=hst.R(), writes=m.ssd_st.R())
                if last_tile:
                    fw.dma("sp", m.ssd_p[g * HPG:(g + 1) * HPG].rearrange("h n p -> n h p"), hst[:], reads=hst.R(), writes=m.ssd_p.R(), is_output=True)
        if not sample:
            load_state(0)
        hs = slice(g * HPG, (g + 1) * HPG)
        for ch, (col, _) in enumerate(chunks):
            cs = slice(col, col + L)
            if sample:
                load_state(ch)
            b = m.psum()
            psb = m.ps[:, b, :].bitcast(BF16)
            def tr(t, psb=psb, cs=cs):
                for i in range(GB + 1):
                    ins = t.transpose(psb[0:L, i * 128:(i + 1) * 128], xbc[:, i, cs], m.identb[:])
                return ins
            fw.op("pe", tr, reads=xbc.R() + m.identb.R(), writes=m.ps.R(b))
            m.copy("act", tok[0:L, :], psb[0:L, 0:GW + 128], m.ps.R(b), tok.R())
            xs3 = tok[0:L, 0:GW].rearrange("p (h d) -> p h d", d=64)
            fw.op("dve", lambda v: v.tensor_tensor(Xd[0:L, :, 0:L], cum[0:L, ch, hs].unsqueeze(2).to_broadcast([L, HPG, L]),
                                                   m.ident[0:L, 0:L].unsqueeze(1).to_broadcast([L, HPG, L]), op=ALU.mult),
                  reads=cum.R() + m.ident.R(), writes=Xd.R())
            bB = m.psum()
            def mmB(t):
                for h in range(HPG):
                    ins = t.matmul(m.ps[0:L, bB, h * 64:h * 64 + L], m.onesf[0:L, 0:L], Xd[0:L, h, 0:L], start=True, stop=True)
                return ins
            fw.op("pe", mmB, reads=Xd.R() + m.onesf.R(), writes=m.ps.R(bB))
            cumB = m.ps[0:L, bB, :].rearrange("p (h l) -> p h l", l=64)[:, 0:HPG, 0:L]
            fw.op("dve", lambda v: v.tensor_tensor(seg[0:L, :, 0:L], cumB, cum[0:L, ch, hs].unsqueeze(2).to_broadcast([L, HPG, L]), op=ALU.subtract),
                  reads=m.ps.R(bB) + cum.R(), writes=seg.R())
            fw.op("dve", lambda v: v.tensor_tensor(wv_[0:L, 0, :], cumB[:, :, L - 1], cum[0:L, ch, hs], op=ALU.subtract),
                  reads=m.ps.R(bB) + cum.R(), writes=wv_.R())
            fw.op("act", lambda a: a.activation(wv_[0:L, 0, :], wv_[0:L, 0, :], AF.Exp), reads=wv_.R(), writes=wv_.R())
            fw.op("pool", lambda g_: g_.tensor_tensor(seg[0:L, :, 0:L], seg[0:L, :, 0:L], m.negm[0:L, 0:L].unsqueeze(1).to_broadcast([L, HPG, L]), op=ALU.add),
                  reads=seg.R() + m.negm.R(), writes=seg.R())
            fw.op("act", lambda a: a.activation(seg[0:L, :, 0:L], seg[0:L, :, 0:L], AF.Exp), reads=seg.R(), writes=seg.R())
            bq = m.psum()
            fw.op("pe", lambda t: t.matmul(m.ps[0:L, bq, 0:L], xbc[:, GB, cs], xbc[:, GB + 1, cs], start=True, stop=True), reads=xbc.R(GB, GB + 1), writes=m.ps.R(bq))
            fw.op("dve", lambda v: v.tensor_tensor(MT[0:L, :, 0:L], seg[0:L, :, 0:L], m.ps[0:L, bq, 0:L].unsqueeze(1).to_broadcast([L, HPG, L]), op=ALU.mult),
                  reads=seg.R() + m.ps.R(bq), writes=MT.R())
            fw.op("dve", lambda v: v.tensor_tensor(xdt[0:L], xs3, dt[0:L, ch, hs].unsqueeze(2).to_broadcast([L, HPG, 64]), op=ALU.mult),
                  reads=tok.R() + dt.R(), writes=xdt.R())
            fw.op("pool", lambda g_: g_.tensor_tensor(xdtw[0:L], xdt[0:L], wv_[0:L, 0, :].unsqueeze(2).to_broadcast([L, HPG, 64]), op=ALU.mult),
                  reads=xdt.R() + wv_.R(), writes=xdtw.R())
            bi_, be_ = m.psum(), m.psum()
            def mmy(t):
                for h in range(HPG):
                    t.matmul(m.ps[0:L, bi_, h * 64:(h + 1) * 64], MT[0:L, h, 0:L], xdt[0:L, h, :], start=True, stop=True)
                return t.matmul(m.ps[0:L, be_, 0:GW], xbc[:, GB + 1, cs], hbf[:].rearrange("p h d -> p (h d)"), start=True, stop=True)
            fw.op("pe", mmy, reads=MT.R() + xdt.R() + xbc.R(GB + 1) + hbf.R(), writes=m.ps.R(bi_) + m.ps.R(be_))
            fw.op("dve", lambda v: v.tensor_tensor(tt[0:L], m.ps[0:L, be_, 0:GW].rearrange("p (h d) -> p h d", d=64),
                                                   ecum[0:L, ch, hs].unsqueeze(2).to_broadcast([L, HPG, 64]), op=ALU.mult),
                  reads=m.ps.R(be_) + ecum.R(), writes=tt.R())
            fw.op("dve", lambda v: v.tensor_tensor(tt[0:L], tt[0:L], m.ps[0:L, bi_, 0:GW].rearrange("p (h d) -> p h d", d=64), op=ALU.add),
                  reads=m.ps.R(bi_) + tt.R(), writes=tt.R())
            fw.op("pool", lambda g_: g_.tensor_tensor(uu[0:L], xs3, m.hb3[0:L, 2, hs].unsqueeze(2).to_broadcast([L, HPG, 64]), op=ALU.mult),
                  reads=tok.R() + m.hb3.R(), writes=uu.R())
            fw.op("dve", lambda v: v.tensor_tensor(ytok[0:L, :].rearrange("p (h d) -> p h d", d=64), tt[0:L], uu[0:L], op=ALU.add),
                  reads=tt.R() + uu.R(), writes=ytok.R())
            bu = m.psum()
            fw.op("pe", lambda t: t.matmul(m.ps[:, bu, 0:GW], tok[0:L, GW:GW + 128], xdtw[0:L].rearrange("p h d -> p (h d)"), start=True, stop=True),
                  reads=tok.R() + xdtw.R(), writes=m.ps.R(bu))
            bl = m.psum()
            fw.op("pe", lambda t: t.matmul(m.ps[:, bl, 0:HPG], m.onesf[0:L, :], Xd[0:L, :, L - 1], start=True, stop=True),
                  reads=Xd.R() + m.onesf.R(), writes=m.ps.R(bl))
            fw.op("act", lambda a: a.activation(wv_[:, 1, :] if False else rstd[:, 0:HPG], m.ps[:, bl, 0:HPG], AF.Exp), reads=m.ps.R(bl), writes=rstd.R())
            fw.op("dve", lambda v: v.tensor_tensor(hst[:], hst[:], rstd[:, 0:HPG].unsqueeze(2).to_broadcast([128, HPG, 64]), op=ALU.mult),
                  reads=hst.R() + rstd.R(), writes=hst.R())
            fw.op("dve", lambda v: v.tensor_tensor(hst[:], hst[:], m.ps[:, bu, 0:GW].rearrange("p (h d) -> p h d", d=64), op=ALU.add),
                  reads=hst.R() + m.ps.R(bu), writes=hst.R())
            m.copy("act", hbf[:], hst[:], hst.R(), hbf.R())
            if sample:
                store_state(ch)
            b = m.psum()
            psb = m.ps[:, b, :].bitcast(BF16)
            def tr2(t, psb=psb):
                for i in range(GB):
                    ins = t.transpose(psb[:, i * 64:i * 64 + L], ytok[0:L, i * 128:(i + 1) * 128], m.identb[0:L, 0:L])
                return ins
            fw.op("pe", tr2, reads=ytok.R() + m.identb.R(), writes=m.ps.R(b))
            fw.op("dve", lambda v: v.tensor_tensor(yz[:, :, cs], psb[:, 0:GB * 64].rearrange("p (i l) -> p i l", l=64)[:, :, 0:L], zs[:, :, cs], op=ALU.mult),
                  reads=m.ps.R(b) + zs.R(), writes=yz.R())
        if not sample:
            store_state(0)
        m.norm_fm(yz, 11, YO, T, GB, sq, rstd, gk0=g * GB, dk0=g * GB)

    def evo(mi, b):
        fw.op("dve", lambda v: v.tensor_tensor(m.x[:, mi, :], m.ps[:, b, :], m.x[:, mi, :], op=ALU.add),
              reads=m.ps.R(b) + m.x.R(mi), writes=m.x.R(mi))
    m.linear_fm(m.W["waboy"], lambda k: YO[:, k, :], T, evo, YO.R())

    o = P0
    qkraw = alloc([128, 8, T], F32, T * 4)
    qT = alloc([128, 4, T], BF16, T * 2)
    kT = alloc([128, 4, T], BF16, T * 2)
    vT = alloc([128, 4, T], BF16, T * 2)
    gs = alloc([128, 4, T], BF16, T * 2)
    rt = alloc([128, 2, T], F32, T * 4)
    t1 = alloc([128, T], F32)
    kvt = alloc([64, 512], BF16)
    MTr = alloc([64, 64], BF16)
    isb = alloc([64, 256], F32)
    osb = alloc([64, 256], F32)
    onb = alloc([64, 256], BF16)
    vw = alloc([64, 256], BF16)
    sm = alloc([64, 4], F32)
    hr = alloc([128, 2, 256], F32)
    hrb = alloc([128, 2, 256], BF16)
    assert o <= m.ARENA_BYTES
    col0 = c.SEQ if sample else ti * T
    fw.dma("sp", rt[:], m.rope_ret[:, :, col0:col0 + T].rearrange("a p n -> p a n"), writes=rt.R())
    lg = [math.log1p(-2.0 ** (-5.0 - h)) for h in range(RH)]
    for hp in range(RH // 2):
        def ev(mi, b):
            if mi < 8:
                if mi < 4:
                    m.copy("dve", qkraw[:, mi, :], m.ps[:, b, :], m.ps.R(b), qkraw.R(mi))
                else:
                    fw.op("act", lambda a: a.activation(qkraw[:, mi, :], m.ps[:, b, :], AF.Copy, scale=256.0 ** -0.5), reads=m.ps.R(b), writes=qkraw.R(mi))
            elif mi < 12:
                m.copy("act", vT[:, mi - 8, :], m.ps[:, b, :], m.ps.R(b), vT.R(mi - 8))
            else:
                fw.op("act", lambda a: a.activation(gs[:, mi - 12, :], m.ps[:, b, :], AF.Silu), reads=m.ps.R(b), writes=gs.R(mi - 12))
        m.linear_fm(m.W[f"wret{hp}"], lambda k: m.xn[:, k, :], T, ev, m.xn.R())
        for qi, dstT in ((0, qT), (4, kT)):
            for hh in range(2):
                x1, x2 = qkraw[:, qi + 2 * hh, :], qkraw[:, qi + 2 * hh + 1, :]
                r12 = qkraw.R(qi + 2 * hh, qi + 2 * hh + 1)
                fw.op("dve", lambda v: v.tensor_tensor(t1[:], x2, rt[:, 1, :], op=ALU.mult), reads=r12 + rt.R(), writes=t1.R())
                fw.op("pool", lambda g_: g_.tensor_tensor(rstd[:], x1, rt[:, 0, :], op=ALU.mult), reads=r12 + rt.R(), writes=rstd.R())
                fw.op("dve", lambda v: v.tensor_tensor(dstT[:, 2 * hh, :], rstd[:], t1[:], op=ALU.subtract), reads=rstd.R() + t1.R(), writes=dstT.R(2 * hh))
                fw.op("dve", lambda v: v.tensor_tensor(t1[:], x1, rt[:, 1, :], op=ALU.mult), reads=r12 + rt.R(), writes=t1.R())
                fw.op("pool", lambda g_: g_.tensor_tensor(rstd[:], x2, rt[:, 0, :], op=ALU.mult), reads=r12 + rt.R(), writes=rstd.R())
                fw.op("dve", lambda v: v.tensor_tensor(dstT[:, 2 * hh + 1, :], rstd[:], t1[:], op=ALU.add), reads=rstd.R() + t1.R(), writes=dstT.R(2 * hh + 1))
        for hh in range(2):
            h = hp * 2 + hh
            gL = math.exp(lg[h] * L)
            def load_state(sidx):
                if sample:
                    for e in range(2):
                        fw.dma("sp", hr[:, e, :], m.st_ret[sidx, h, e * 128:(e + 1) * 128, :], writes=hr.R())
                elif ti == 0:
                    fw.op("pool", lambda g_: g_.memset(hr[:], 0.0), writes=hr.R())
                else:
                    for e in range(2):
                        fw.dma("sp", hr[:, e, :], m.ret_st[h, e * 128:(e + 1) * 128, :], reads=m.ret_st.R(), writes=hr.R())
                m.copy("act", hrb[:], hr[:], hr.R(), hrb.R())
            def store_state(sidx):
                for e in range(2):
                    if sample:
                        fw.dma("sp", m.ret_s[sidx, h, e * 128:(e + 1) * 128, :], hr[:, e, :], reads=hr.R(), writes=m.ret_s.R(), is_output=True)
                    else:
                        fw.dma("sp", m.ret_st[h, e * 128:(e + 1) * 128, :], hr[:, e, :], reads=hr.R(), writes=m.ret_st.R())
                        if last_tile:
                            fw.dma("sp", m.ret_p[h, e * 128:(e + 1) * 128, :], hr[:, e, :], reads=hr.R(), writes=m.ret_p.R(), is_output=True)
            if not sample:
                load_state(0)
            Dm = rdec[0:L, h * 64:h * 64 + L]
            dfs = rdec[0:L, RH * 64 + h:RH * 64 + h + 1]
            dte = rdec[0:L, RH * 64 + RH + h:RH * 64 + RH + h + 1]
            for ch, (col, _) in enumerate(chunks):
                cs = slice(col, col + L)
                if sample:
                    load_state(ch)
                b = m.psum()
                psb = m.ps[:, b, :].bitcast(BF16)
                def tr(t, psb=psb, cs=cs):
                    for e in range(2):
                        t.transpose(psb[0:L, e * 128:(e + 1) * 128], kT[:, 2 * hh + e, cs], m.identb[:])
                        ins = t.transpose(psb[0:L, 256 + e * 128:256 + (e + 1) * 128], vT[:, 2 * hh + e, cs], m.identb[:])
                    return ins
                fw.op("pe", tr, reads=kT.R(2 * hh, 2 * hh + 1) + vT.R(2 * hh, 2 * hh + 1) + m.identb.R(), writes=m.ps.R(b))
                m.copy("act", kvt[0:L, :], psb[0:L, 0:512], m.ps.R(b), kvt.R())
                bq = m.psum()
                def mq(t):
                    t.matmul(m.ps[0:L, bq, 0:L], kT[:, 2 * hh, cs], qT[:, 2 * hh, cs], start=True, stop=False)
                    return t.matmul(m.ps[0:L, bq, 0:L], kT[:, 2 * hh + 1, cs], qT[:, 2 * hh + 1, cs], start=False, stop=True)
                fw.op("pe", mq, reads=kT.R(2 * hh, 2 * hh + 1) + qT.R(2 * hh, 2 * hh + 1), writes=m.ps.R(bq))
                fw.op("dve", lambda v: v.tensor_tensor(MTr[0:L, 0:L], m.ps[0:L, bq, 0:L], Dm, op=ALU.mult), reads=m.ps.R(bq) + rdec.R(), writes=MTr.R())
                bi_, be_ = m.psum(), m.psum()
                def mo(t):
                    t.matmul(m.ps[0:L, bi_, 0:256], MTr[0:L, 0:L], kvt[0:L, 256:512], start=True, stop=True)
                    t.matmul(m.ps[0:L, be_, 0:256], qT[:, 2 * hh, cs], hrb[:, 0, :], start=True, stop=False)
                    return t.matmul(m.ps[0:L, be_, 0:256], qT[:, 2 * hh + 1, cs], hrb[:, 1, :], start=False, stop=True)
                fw.op("pe", mo, reads=MTr.R() + kvt.R() + qT.R(2 * hh, 2 * hh + 1) + hrb.R(), writes=m.ps.R(bi_) + m.ps.R(be_))
                m.copy("act", isb[0:L, :], m.ps[0:L, bi_, 0:256], m.ps.R(bi_), isb.R())
                fw.op("dve", lambda v: v.scalar_tensor_tensor(osb[0:L, :], m.ps[0:L, be_, 0:256], dfs, isb[0:L, :], op0=ALU.mult, op1=ALU.add),
                      reads=m.ps.R(be_) + rdec.R() + isb.R(), writes=osb.R())
                fw.op("act", lambda a: a.activation(isb[0:L, :], osb[0:L, :], AF.Square, accum_out=sm[0:L, 0:1]), reads=osb.R(), writes=isb.R() + sm.R())
                fw.op("act", lambda a: a.activation(sm[0:L, 1:2], sm[0:L, 0:1], AF.Sqrt, bias=m.epsb[0:L, 0:1], scale=1.0 / 256), reads=sm.R() + m.epsb.R(), writes=sm.R())
                fw.op("dve", lambda v: v.reciprocal(sm[0:L, 2:3], sm[0:L, 1:2]), reads=sm.R(), writes=sm.R())
                fw.op("dve", lambda v: v.tensor_scalar(onb[0:L, :], osb[0:L, :], sm[0:L, 2:3], None, op0=ALU.mult), reads=osb.R() + sm.R(), writes=onb.R())
                fw.op("pool", lambda g_: g_.tensor_scalar(vw[0:L, :], kvt[0:L, 256:512], dte, None, op0=ALU.mult), reads=kvt.R() + rdec.R(), writes=vw.R())
                bu = m.psum()
                def mu(t):
                    t.matmul(m.ps[:, bu, 0:256], kvt[0:L, 0:128], vw[0:L, :], start=True, stop=True)
                    return t.matmul(m.ps[:, bu, 256:512], kvt[0:L, 128:256], vw[0:L, :], start=True, stop=True)
                fw.op("pe", mu, reads=kvt.R() + vw.R(), writes=m.ps.R(bu))
                fw.op("dve", lambda v: v.scalar_tensor_tensor(hr[:], hr[:], gL, m.ps[:, bu, :].rearrange("p (e n) -> p e n", e=2), op0=ALU.mult, op1=ALU.add),
                      reads=hr.R() + m.ps.R(bu), writes=hr.R())
                m.copy("act", hrb[:], hr[:], hr.R(), hrb.R())
                if sample:
                    store_state(ch)
                b = m.psum()
                psb = m.ps[:, b, :].bitcast(BF16)
                def tr2(t, psb=psb):
                    for e in range(2):
                        ins = t.transpose(psb[:, e * 64:e * 64 + L], onb[0:L, e * 128:(e + 1) * 128], m.identb[0:L, 0:L])
                    return ins
                fw.op("pe", tr2, reads=onb.R() + m.identb.R(), writes=m.ps.R(b))
                fw.op("dve", lambda v: v.tensor_tensor(YO[:, h * 2:h * 2 + 2, cs], psb[:, 0:128].rearrange("p (i l) -> p i l", l=64)[:, :, 0:L], gs[:, 2 * hh:2 * hh + 2, cs], op=ALU.mult),
                      reads=m.ps.R(b) + gs.R(2 * hh, 2 * hh + 1), writes=YO.R(h * 2, h * 2 + 1))
            if not sample:
                store_state(0)
    m.linear_fm(m.W["waboo"], lambda k: YO[:, k, :], T, evo, YO.R())
    if last_tile and not sample:
        b = m.psum()
        NB = c.CONV // 128
        fw.op("pe", lambda t: t.transpose(m.ps[0:NB * 3, b, 0:128], m.convst[:].rearrange("p b w -> p (b w)"), m.ident[:]),
              reads=m.convst.R() + m.ident.R(), writes=m.ps.R(b))
        m.copy("dve", halo[0:NB * 3, :], m.ps[0:NB * 3, b, 0:128], m.ps.R(b), halo.R())
        for bb in range(NB):
            fw.dma("sp", m.conv_p[:, bb * 128:(bb + 1) * 128], halo[bb * 3:(bb + 1) * 3, :], reads=halo.R(), writes=m.conv_p.R(), is_output=True)


def _mixer(m, l, ti, sample):
    if l == 1 and ("mix" in m.stages or "mix1" in m.stages):
        _mla(m, ti, sample)
    if l == 0 and ("mix" in m.stages or "mix0" in m.stages):
        _ssdret(m, ti, sample)


Model.mixer = _mixer
```
